# Optimizing a Trainium2 kernel written in Bass

```python
import jax, jax.numpy as jnp
from jax import lax
import numpy as np

D_MODEL = 2048
BATCH = 2
SEQ = 4096
DEPTH = 1

GRID_W = 64
CTX_LEN = 256
D_A = 2048
D_B = 2048
D_MIX = D_A + D_B
N_BLOCKS_A = 16
BLOCK_A = D_A // N_BLOCKS_A
CONV_W = 4
CONV_PAD_L = 2
CONV_PAD_R = CONV_W - 1 - CONV_PAD_L
LRU_C = 8.0
N_HEADS_B = 16
HEAD_B = D_B // N_HEADS_B
CHUNK = 64
EPS = 1e-6
SPLITS = (D_A, 2 * D_A, 2 * D_A + D_B, 2 * D_A + 2 * D_B, 2 * D_A + 3 * D_B, 2 * D_A + 4 * D_B)
IN_COLS = 2 * D_A + 5 * D_B

kernel_name = "hybrid_rglru_hgrn2_dit_layer"


def rmsnorm(x, w):
    xf = x.astype(jnp.float32)
    y = xf * lax.rsqrt(jnp.mean(xf * xf, axis=-1, keepdims=True) + EPS)
    return (y * w.astype(jnp.float32)).astype(x.dtype)


def flip(z):
    return jnp.flip(z, axis=1)


def to_colmajor(z, rows):
    b, t, ch = z.shape
    return z.reshape(b, rows, GRID_W, ch).swapaxes(1, 2).reshape(b, t, ch)


def from_colmajor(z, rows):
    b, t, ch = z.shape
    return z.reshape(b, GRID_W, rows, ch).swapaxes(1, 2).reshape(b, t, ch)


def dwconv_centred(u, w, b):
    t = u.shape[1]
    up = jnp.pad(u, ((0, 0), (CONV_PAD_L, CONV_PAD_R), (0, 0)))
    return b + sum(up[:, k:k + t] * w[k] for k in range(CONV_W))


def linear_scan(a, b, h0):
    b = b.at[:, 0].add(a[:, 0] * h0)

    def comb(left, right):
        al, bl = left
        ar, br = right
        return ar * al, ar * bl + br

    _, h = lax.associative_scan(comb, (a, b), axis=1)
    return h


def rglru_scan(u, w_r, b_r, w_i, b_i, lam, h0):
    uf = u.astype(jnp.float32)
    ub = uf.reshape(uf.shape[0], uf.shape[1], N_BLOCKS_A, BLOCK_A)
    r = jax.nn.sigmoid(jnp.einsum('btnc,ncd->btnd', ub, w_r.astype(jnp.float32)).reshape(uf.shape) + b_r.astype(jnp.float32))
    i = jax.nn.sigmoid(jnp.einsum('btnc,ncd->btnd', ub, w_i.astype(jnp.float32)).reshape(uf.shape) + b_i.astype(jnp.float32))
    log_a = -LRU_C * r * jax.nn.softplus(-lam.astype(jnp.float32))
    a = jnp.exp(log_a)
    bx = jnp.sqrt(-jnp.expm1(2.0 * log_a)) * (i * uf)
    h = linear_scan(a, bx, h0)
    return h, h[:, -1]


def hgrn2_chunked(q, k, v, log_f, s0):
    bsz, t = q.shape[0], q.shape[1]
    n = t // CHUNK

    def chunks(z):
        return z.reshape(bsz, n, CHUNK, N_HEADS_B, HEAD_B).transpose(1, 0, 3, 2, 4)

    causal = jnp.tril(jnp.ones((CHUNK, CHUNK), dtype=bool))

    def step(s, inp):
        qc, kc, vc, gc = inp
        F = jnp.cumsum(gc, axis=2)
        diff = F[:, :, :, None, :] - F[:, :, None, :, :]
        decay = jnp.exp(jnp.where(causal[:, :, None], diff, -jnp.inf))
        scores = jnp.einsum('bhtk,bhsk,bhtsk->bhts', qc, kc, decay)
        o = jnp.einsum('bhts,bhsv->bhtv', scores, vc) + jnp.einsum('bhtk,bhkv->bhtv', qc * jnp.exp(F), s)
        f_last = F[:, :, -1]
        s_new = jnp.exp(f_last)[..., None] * s + jnp.einsum('bhsk,bhsv->bhkv', kc * jnp.exp(f_last[:, :, None] - F), vc)
        return s_new, o

    s_final, o = lax.scan(step, s0, (chunks(q), chunks(k), chunks(v), chunks(log_f)))
    o = o.transpose(1, 0, 3, 2, 4).reshape(bsz, t, N_HEADS_B, HEAD_B)
    return o, s_final


def hgrn2_dir(q, f_pre, v, lb, s0):
    f = lb + (1.0 - lb) * jax.nn.sigmoid(f_pre)
    return hgrn2_chunked(q, 1.0 - f, v, jnp.log(f), s0)


def split_heads(z):
    return z.astype(jnp.float32).reshape(z.shape[0], z.shape[1], N_HEADS_B, HEAD_B)


def hybrid_layer(x, ctx, c, c_ctx, rows, ada_w, ada_b, norm_w, w_in, conv_w, conv_b,
                 lru_wr, lru_br, lru_wi, lru_bi, lru_lambda, lb_f, lb_b, hgrn_norm_w, w_out, last):
    bsz, t = x.shape[0], x.shape[1]
    shift, scale, gate = jnp.split(jax.nn.silu(c) @ ada_w + ada_b, 3, axis=-1)
    shift_c, scale_c, gate_c = jnp.split(jax.nn.silu(c_ctx) @ ada_w + ada_b, 3, axis=-1)
    h_lat = rmsnorm(x, norm_w) * (1.0 + scale[:, None]) + shift[:, None]
    h_ctx = rmsnorm(ctx, norm_w) * (1.0 + scale_c) + shift_c
    xa_l, ga_l, q_l, ff_l, fb_l, v_l, gb_l = jnp.split(h_lat @ w_in, SPLITS, axis=-1)
    xa_c, ga_c, q_c, ff_c, fb_c, v_c, gb_c = jnp.split(h_ctx @ w_in, SPLITS, axis=-1)

    ua_l = dwconv_centred(xa_l, conv_w, conv_b)
    ua_c = dwconv_centred(xa_c, conv_w, conv_b)
    p_fwd = (lru_wr[0], lru_br[0], lru_wi[0], lru_bi[0], lru_lambda[0])
    p_bwd = (lru_wr[1], lru_br[1], lru_wi[1], lru_bi[1], lru_lambda[1])
    h0 = jnp.zeros((bsz, D_A), jnp.float32)
    hc_f, sc_f = rglru_scan(ua_c, *p_fwd, h0)
    hl_f, _ = rglru_scan(ua_l, *p_fwd, sc_f)
    hc_b, sc_b = rglru_scan(flip(ua_c), *p_bwd, h0)
    hl_b, _ = rglru_scan(flip(ua_l), *p_bwd, sc_b)
    ya_l = (hl_f + flip(hl_b)).astype(x.dtype) * jax.nn.silu(ga_l)

    qL = jax.nn.silu(split_heads(to_colmajor(q_l, rows)))
    ffL, fbL, vL = [split_heads(to_colmajor(z, rows)) for z in (ff_l, fb_l, v_l)]
    qC = jax.nn.silu(split_heads(q_c))
    ffC, fbC, vC = [split_heads(z) for z in (ff_c, fb_c, v_c)]
    s0 = jnp.zeros((bsz, N_HEADS_B, HEAD_B, HEAD_B), jnp.float32)
    oc_f, Sc_f = hgrn2_dir(qC, ffC, vC, lb_f, s0)
    ol_f, _ = hgrn2_dir(qL, ffL, vL, lb_f, Sc_f)
    oc_b, Sc_b = hgrn2_dir(flip(qC), flip(fbC), flip(vC), lb_b, s0)
    ol_b, _ = hgrn2_dir(flip(qL), flip(fbL), flip(vL), lb_b, Sc_b)
    ol = from_colmajor((ol_f + flip(ol_b)).reshape(bsz, t, D_B), rows)
    ol = rmsnorm(ol.reshape(bsz, t, N_HEADS_B, HEAD_B), hgrn_norm_w).reshape(bsz, t, D_B)
    yb_l = ol.astype(x.dtype) * jax.nn.silu(gb_l)

    x = x + gate[:, None] * (jnp.concatenate([ya_l, yb_l], axis=-1) @ w_out)

    if not last:
        ya_c = (hc_f + flip(hc_b)).astype(ctx.dtype) * jax.nn.silu(ga_c)
        oc = rmsnorm(oc_f + flip(oc_b), hgrn_norm_w).reshape(bsz, ctx.shape[1], D_B)
        yb_c = oc.astype(ctx.dtype) * jax.nn.silu(gb_c)
        ctx = ctx + gate_c * (jnp.concatenate([ya_c, yb_c], axis=-1) @ w_out)
    return x, ctx


def setup_inputs(seed: int = 0) -> dict:
    key = jax.random.key(seed)
    ks = jax.random.split(key, 20)
    f32 = jnp.float32
    nrm = lambda k, shape, s: jax.random.normal(k, shape, f32) * s
    u = jax.random.uniform(ks[14], (DEPTH, 2, D_A), f32, 0.9, 0.999)
    a = u ** (1.0 / LRU_C)
    lru_lambda = jnp.log(a) - jnp.log1p(-a)
    return {
        "x": nrm(ks[0], (BATCH, SEQ, D_MODEL), 1.0),
        "c": nrm(ks[1], (BATCH, D_MODEL), 1.0),
        "ctx": nrm(ks[2], (BATCH, CTX_LEN, D_MODEL), 1.0),
        "c_ctx": nrm(ks[3], (D_MODEL,), 1.0),
        "ada_w": nrm(ks[4], (DEPTH, D_MODEL, 3 * D_MODEL), 0.5 * D_MODEL ** -0.5),
        "ada_b": nrm(ks[5], (DEPTH, 3 * D_MODEL), 0.01),
        "norm_w": 1.0 + nrm(ks[6], (DEPTH, D_MODEL), 0.02),
        "w_in": nrm(ks[7], (DEPTH, D_MODEL, IN_COLS), D_MODEL ** -0.5),
        "conv_w": nrm(ks[8], (DEPTH, CONV_W, D_A), CONV_W ** -0.5),
        "conv_b": nrm(ks[9], (DEPTH, D_A), 0.01),
        "lru_wr": nrm(ks[10], (DEPTH, 2, N_BLOCKS_A, BLOCK_A, BLOCK_A), BLOCK_A ** -0.5),
        "lru_br": nrm(ks[11], (DEPTH, 2, D_A), 0.01),
        "lru_wi": nrm(ks[12], (DEPTH, 2, N_BLOCKS_A, BLOCK_A, BLOCK_A), BLOCK_A ** -0.5),
        "lru_bi": nrm(ks[13], (DEPTH, 2, D_A), 0.01),
        "lru_lambda": lru_lambda,
        "hgrn_lb_logits": nrm(ks[15], (2, DEPTH + 1, D_B), 0.5),
        "hgrn_norm_w": 1.0 + nrm(ks[16], (DEPTH, HEAD_B), 0.02),
        "w_out": nrm(ks[17], (DEPTH, D_MIX, D_MODEL), D_MIX ** -0.5),
        "final_norm_w": 1.0 + nrm(ks[18], (D_MODEL,), 0.02),
    }


def reference(x, c, ctx, c_ctx, ada_w, ada_b, norm_w, w_in, conv_w, conv_b,
              lru_wr, lru_br, lru_wi, lru_bi, lru_lambda, hgrn_lb_logits, hgrn_norm_w,
              w_out, final_norm_w):
    rows = x.shape[1] // GRID_W
    lb_all = jnp.cumsum(jax.nn.softmax(hgrn_lb_logits.astype(jnp.float32), axis=1), axis=1)
    for l in range(DEPTH):
        x, ctx = hybrid_layer(
            x, ctx, c, c_ctx, rows, ada_w[l], ada_b[l], norm_w[l], w_in[l], conv_w[l], conv_b[l],
            lru_wr[l], lru_br[l], lru_wi[l], lru_bi[l], lru_lambda[l],
            lb_all[0, l].reshape(N_HEADS_B, HEAD_B), lb_all[1, l].reshape(N_HEADS_B, HEAD_B),
            hgrn_norm_w[l], w_out[l], l == DEPTH - 1)
    return rmsnorm(x, final_norm_w)
```

```python
import numpy as np
import concourse.bass as bass
import concourse.mybir as mybir
from concourse.bass_utils import run_bass_kernel_spmd
from contextlib import ExitStack

F32 = mybir.dt.float32
BF16 = mybir.dt.bfloat16
AF = mybir.ActivationFunctionType
ALU = mybir.AluOpType

D = 2048
SEQ = 4096
CTX = 256
T = CTX + SEQ
NCOL = 3584
EPS = 1e-6
CH = 64
NCHK = T // CH


class Sched:
    ENG = ("tensor", "vector", "scalar", "gpsimd", "sync")
    ROT = 4000

    def __init__(self, nc, stack, name, sync_same=True):
        self.nc = nc
        self.stack = stack
        self.name = name
        self.ops = []
        self.last_w = {}
        self.readers = {}
        self.sync_same = sync_same
        self.bulk = set()

    def op(self, eng, fn, r=(), w=(), dma_key=None, bulk=False, inc=16, grp=None, ext=()):
        i = len(self.ops)
        deps = set()
        for k in r:
            if k in self.last_w:
                deps.add(self.last_w[k])
        for k in w:
            if k in self.last_w:
                deps.add(self.last_w[k])
            last = {}
            for ri in self.readers.get(k, ()):
                ro = self.ops[ri]
                if ro["dma_key"] is not None:
                    deps.add(ri)
                else:
                    last[ro["eng"]] = max(last.get(ro["eng"], -1), ri)
            deps.update(last.values())
        pf = getattr(self, "pending_fence", None)
        if pf and eng in pf:
            deps.update(pf.pop(eng))
        self.ops.append(dict(eng=eng, fn=fn, deps=deps, dma_key=dma_key, idx=i, sig=None, inc=inc))
        if bulk and grp is None:
            grp = "all"
        self.ops[-1]["grp"] = grp
        self.ops[-1]["ext"] = list(ext)
        for k in r:
            self.readers.setdefault(k, []).append(i)
        for k in w:
            self.last_w[k] = i
            self.readers[k] = []
        return i

    def fence(self):
        last = {}
        dmas = set()
        for o in self.ops:
            if o["dma_key"] is not None:
                dmas.add(o["idx"])
            else:
                last[o["eng"]] = o["idx"]
        self.pending_fence = {e: set(last.values()) | set(dmas) for e in self.ENG}

    def emit(self, nofinal=()):
        nc = self.nc
        ops = self.ops
        need = [False] * len(ops)
        for o in ops:
            for d in o["deps"]:
                if ops[d]["dma_key"] is not None:
                    continue
                if ops[d]["eng"] != o["eng"] or (self.sync_same and o["eng"] != "tensor"):
                    need[d] = True
        eng_cnt = {e: 0 for e in self.ENG}
        eng_sems = {e: [] for e in self.ENG}
        dma_sems = {}
        for o in ops:
            if o["dma_key"] is not None:
                k = o["dma_key"]
                if k not in dma_sems:
                    dma_sems[k] = [self.stack.enter_context(nc.semaphore("d%s_%d" % (self.name, len(dma_sems)))), 0]
                dma_sems[k][1] += o["inc"]
                o["sig"] = [dma_sems[k][0], dma_sems[k][1], o["inc"]]
            elif need[o["idx"]]:
                e = o["eng"]
                c = eng_cnt[e]
                si = c // self.ROT
                if si >= len(eng_sems[e]):
                    eng_sems[e].append(self.stack.enter_context(nc.semaphore("e%s_%s_%d" % (self.name, e, si))))
                o["sig"] = [eng_sems[e][si], c % self.ROT + 1, 1]
                eng_cnt[e] = c + 1
        gmax = {}
        for o in ops:
            if o["dma_key"] is not None and o["grp"] is not None:
                gk = (o["dma_key"], o["grp"])
                gmax[gk] = max(gmax.get(gk, 0), o["sig"][1])
        for o in ops:
            if o["dma_key"] is not None and o["grp"] is not None:
                o["sig"][1] = gmax[(o["dma_key"], o["grp"])]
        self.dma_sems = dma_sems
        nwaits = {e: 0 for e in self.ENG}
        with nc.Block(no_gpsimd_drain=(self.name != "a")) as block:
            for eng in self.ENG:
                def body(e, eng=eng):
                    seen = {}
                    for o in ops:
                        if o["eng"] != eng:
                            continue
                        for d in sorted(o["deps"]):
                            sg = ops[d]["sig"]
                            if sg is None:
                                continue
                            if ops[d]["dma_key"] is None and ops[d]["eng"] == eng and not (self.sync_same and eng != "tensor"):
                                continue
                            sem, val, _ = sg
                            key = id(sem)
                            if seen.get(key, 0) >= val:
                                continue
                            e.wait_ge(sem, val)
                            nwaits[eng] += 1
                            seen[key] = val
                        for (xsem, xval) in o["ext"]:
                            if seen.get(id(xsem), 0) < xval:
                                e.wait_ge(xsem, xval)
                                seen[id(xsem)] = xval
                        inst = o["fn"](e)
                        if o["sig"] is not None:
                            inst.then_inc(o["sig"][0], o["sig"][2])
                    if eng == "sync":
                        for k, (sem, cnt) in dma_sems.items():
                            if k in nofinal:
                                continue
                            e.wait_ge(sem, cnt)
                getattr(block, eng)(body)
        self.nwaits = nwaits


def AP(t, off, pat):
    return bass.AP(t, off, [list(p) for p in pat])


GROUPS = [
    ("xa", None, F32, True),
    ("sga", AF.Silu, BF16, False),
    ("q", AF.Silu, BF16, False),
    ("sff", AF.Sigmoid, F32, True),
    ("sfb", AF.Sigmoid, F32, True),
    ("v", None, BF16, True),
    ("sgb", AF.Silu, BF16, False),
]


def build_program(debug=False, stop_after=99):
    nc = bass.Bass("TRN2", target_bir_lowering=False)
    I = {}

    def din(name, shape, dt=F32):
        I[name] = nc.dram_tensor(name, list(shape), dt, kind="ExternalInput")
        return I[name]

    x_d = din("x", [SEQ, D]); ctx_d = din("ctx", [CTX, D]); cvec_d = din("cvec", [2, D])
    adaw_d = din("ada_w", [D, 3 * D]); adab_d = din("ada_b", [1, 3 * D]); normw_d = din("norm_w", [1, D])
    win_d = din("w_in", [D, NCOL])
    convw_d = din("conv_w", [4, 512]); convb_d = din("conv_b", [1, 512])
    wr_d = din("lru_wr", [2, 4, 128, 128]); wi_d = din("lru_wi", [2, 4, 128, 128])
    br_d = din("lru_br", [2, 512]); bi_d = din("lru_bi", [2, 512]); lam_d = din("lru_lam", [2, 512])
    lbl_d = din("lb_logits", [2, 2, 512]); hnw_d = din("hnorm_w", [1, 128])
    wout_d = din("w_out", [2 * D, 512]); fnw_d = din("fnorm_w", [1, 512])
    xq_d = din("xq", [SEQ, 512]); adawg_d = din("ada_wg", [D, 512]); adabg_d = din("ada_bg", [1, 512])
    out_d = nc.dram_tensor("out", [SEQ, 512], F32, kind="ExternalOutput")

    skind = "ExternalOutput" if debug else "Internal"

    def dscr(name, shape, dt):
        return nc.dram_tensor(name, list(shape), dt, kind=skind)

    ada_s = dscr("ada_s", [2, 3 * D], F32)
    G_d = {}
    for (gname, _, gdt, needctx) in GROUPS:
        G_d[gname] = dscr("p_" + gname, [512, T if needctx else SEQ], gdt)
    ada_g = dscr("ada_g", [2, 512], F32)
    y_u = [dscr("y_u%d" % i, [128, SEQ], BF16) for i in range(8)]
    yg_u = [nc.dram_tensor("yg_u%d" % i, [512, SEQ], BF16) for i in range(8)]
    ss_loc = nc.dram_tensor("ss_loc", [128, 32], F32)
    ss_all = nc.dram_tensor("ss_all", [512, 32], F32)
    RG = [[0, 1, 2, 3], [4, 5, 6, 7]]

    with ExitStack() as top:
        def sbt(stack, name, shape, dt=F32):
            return stack.enter_context(nc.sbuf_tensor(name, list(shape), dt))

        def pst(stack, name, shape, dt=F32):
            return stack.enter_context(nc.psum_tensor(name, list(shape), dt))

        with ExitStack() as ph:
            S = Sched(nc, top, "a")
            w_sb = sbt(ph, "w_sb", [128, 16, NCOL], BF16)
            ident_f = sbt(ph, "ident_f", [128, 128]); ident = sbt(ph, "ident", [128, 128], BF16)
            ccol = sbt(ph, "ccol", [128, 2, 16]); scolT = sbt(ph, "scolT", [128, 16, 2])
            modc = sbt(ph, "modc", [128, 4, 16]); nwc = sbt(ph, "nwc", [128, 16])
            gsh = sbt(ph, "gsh", [128, 4, 16])
            ss = sbt(ph, "ss", [128, 40]); rstd = sbt(ph, "rstd", [128, 40])
            pacc = [pst(ph, "pacc%d" % i, [128, 512]) for i in range(4)]
            ptr = [pst(ph, "ptr%d" % i, [128, 1024], BF16) for i in range(4)]

            S.op("gpsimd", lambda e: e.memset(ident_f[:], 1.0), w=["identf"])
            S.op("gpsimd", lambda e: e.affine_select(out=ident_f[:], in_=ident_f[:], pattern=[[-1, 128]], compare_op=ALU.is_equal, fill=0.0, base=0, channel_multiplier=1), r=["identf"], w=["identf"])
            S.op("vector", lambda e: e.tensor_copy(out=ident[:], in_=ident_f[:]), r=["identf"], w=["ident"])
            for g in range(7):
                for kc in range(16):
                    S.op("gpsimd", lambda e, g=g, kc=kc: e.dma_start(out=w_sb[:, kc, g * 512:(g + 1) * 512], in_=win_d.ap()[kc * 128:(kc + 1) * 128, g * 512:(g + 1) * 512]),
                         w=[("w", g, kc)], dma_key=("w", g), bulk=True)
            for i in range(2):
                S.op("sync", lambda e, i=i: e.dma_start(out=ccol[:, i, :], in_=AP(cvec_d, i * D, [[1, 128], [128, 16]]), allow_slow_non_contiguous=True), w=[("ccol", i)], dma_key="cn", bulk=True)
            S.op("sync", lambda e: e.dma_start(out=nwc[:], in_=AP(normw_d, 0, [[1, 128], [128, 16]]), allow_slow_non_contiguous=True), w=["nwc"], dma_key="cn", bulk=True)
            for i in range(2):
                S.op("scalar", lambda e, i=i: e.activation(out=AP(scolT, i, [[32, 128], [2, 16]]), in_=ccol[:, i, :], func=AF.Silu), r=[("ccol", i)], w=[("scolT", i)])
            with ExitStack() as p0:
                aslot = [sbt(p0, "aslot%d" % i, [128, 2048]) for i in range(3)]
                rows = sbt(p0, "rows", [2, 3 * D]); adab2 = sbt(p0, "adab2", [2, 3 * D])
                S.op("sync", lambda e: e.dma_start(out=adab2[:], in_=AP(adab_d, 0, [[0, 2], [1, 3 * D]])), w=["adab2"], dma_key="adab2")
                li = 0
                for sec in range(2):
                    for kc in range(16):
                        sl = li % 3
                        S.op("sync", lambda e, sec=sec, kc=kc, sl=sl: e.dma_start(out=aslot[sl][:], in_=adaw_d.ap()[kc * 128:(kc + 1) * 128, sec * D:(sec + 1) * D]), w=[("aslot", sl)], dma_key=("aslot", sl))
                        for nb in range(4):
                            S.op("tensor", lambda e, kc=kc, sl=sl, nb=nb: e.matmul(pacc[nb][0:2, :], lhsT=scolT[:, kc, :], rhs=aslot[sl][:, nb * 512:(nb + 1) * 512], start=(kc == 0), stop=(kc == 15)),
                                 r=[("aslot", sl), ("scolT", 0), ("scolT", 1)], w=[("pacc", nb)])
                        li += 1
                    for nb in range(4):
                        c0 = sec * D + nb * 512
                        S.op("vector", lambda e, nb=nb, c0=c0: e.tensor_tensor(out=rows[0:2, c0:c0 + 512], in0=pacc[nb][0:2, :], in1=adab2[0:2, c0:c0 + 512], op=ALU.add),
                             r=[("pacc", nb), "adab2"], w=["rows"])
                for kc in range(16):
                    sl = li % 3
                    S.op("sync", lambda e, kc=kc, sl=sl: e.dma_start(out=aslot[sl][:, 0:512], in_=adawg_d.ap()[kc * 128:(kc + 1) * 128, :]), w=[("aslot", sl)], dma_key=("aslot", sl))
                    S.op("tensor", lambda e, kc=kc, sl=sl: e.matmul(pacc[0][0:2, :], lhsT=scolT[:, kc, :], rhs=aslot[sl][:, 0:512], start=(kc == 0), stop=(kc == 15)),
                         r=[("aslot", sl), ("scolT", 0), ("scolT", 1)], w=[("pacc", 0)])
                    li += 1
                rowsg = sbt(p0, "rowsg", [2, 512]); adabg2 = sbt(p0, "adabg2", [2, 512])
                S.op("sync", lambda e: e.dma_start(out=adabg2[:], in_=AP(adabg_d, 0, [[0, 2], [1, 512]])), w=["adabg2"], dma_key="adabg2")
                S.op("vector", lambda e: e.tensor_tensor(out=rowsg[0:2, :], in0=pacc[0][0:2, :], in1=adabg2[0:2, :], op=ALU.add), r=[("pacc", 0), "adabg2"], w=["rowsg"])
                S.op("sync", lambda e: e.dma_start(out=ada_g.ap(), in_=rowsg[:]), r=["rowsg"], w=["ada_g"], dma_key="ada_g")
                S.op("sync", lambda e: e.dma_start(out=ada_s.ap()[:, 0:2 * D], in_=rows[:, 0:2 * D]), r=["rows"], w=["ada_s"], dma_key="ada_s")
                for i, (row, sec) in enumerate([(0, 0), (0, 1), (1, 0), (1, 1)]):
                    S.op("sync", lambda e, i=i, row=row, sec=sec: e.dma_start(out=modc[:, i, :], in_=AP(ada_s, row * 3 * D + sec * D, [[1, 128], [128, 16]]), allow_slow_non_contiguous=True),
                         r=["ada_s"], w=[("modc", i)], dma_key="modc", bulk=True)
                for tgt, (sh_i, sc_i) in enumerate([(0, 1), (2, 3)]):
                    S.op("vector", lambda e, tgt=tgt, sc_i=sc_i: e.scalar_tensor_tensor(out=gsh[:, 2 * tgt, :], in0=modc[:, sc_i, :], scalar=1.0, in1=nwc[:], op0=ALU.add, op1=ALU.mult),
                         r=[("modc", sc_i), "nwc"], w=[("gsh", 2 * tgt)])
                    S.op("vector", lambda e, tgt=tgt, sh_i=sh_i: e.tensor_copy(out=gsh[:, 2 * tgt + 1, :], in_=modc[:, sh_i, :]), r=[("modc", sh_i)], w=[("gsh", 2 * tgt + 1)])

            S.fence()
            with ExitStack() as p1:
                xt = [sbt(p1, "xt%d" % i, [128, D]) for i in range(2)]
                xn = [sbt(p1, "xn%d" % i, [128, D], BF16) for i in range(2)]
                hT = [sbt(p1, "hT%d" % i, [128, 16, 512], BF16) for i in range(2)]
                stgF = [sbt(p1, "stgF%d" % i, [128, 4, 512]) for i in range(2)]
                stgB = [sbt(p1, "stgB%d" % i, [128, 4, 512], BF16) for i in range(2)]
                mtmp = [sbt(p1, "mtmp%d" % i, [128, 8, 128]) for i in range(2)]
                mcount = [0]
                scountB = [0]
                mstmp = sbt(p1, "mstmp", [128, 40])
                blocks = [(0, 2, True)] + [(2 + 4 * i, 4, False) for i in range(8)]
                tcount = [0]
                scount = [0]

                pstate = {}

                def prepA(bi, tl):
                    t0, nt, isctx = blocks[bi]
                    tt = t0 + tl
                    xs = xt[tcount[0] % 2]; xk = ("xt", tcount[0] % 2)
                    xb = xn[tcount[0] % 2]; xnk = ("xn", tcount[0] % 2)
                    pk = tcount[0] % 2
                    tcount[0] += 1
                    pstate[(bi, tl)] = (xb, xnk, pk)
                    src = ctx_d.ap()[tt * 128:(tt + 1) * 128, :] if isctx else x_d.ap()[(tt - 2) * 128:(tt - 1) * 128, :]
                    S.op("sync", lambda e, xs=xs, src=src: e.dma_start(out=xs[:], in_=src), w=[xk], dma_key=xk)
                    S.op("scalar", lambda e, xs=xs, xb=xb, tt=tt: e.activation(out=xb[:], in_=xs[:], func=AF.Square, accum_out=ss[:, tt:tt + 1]), r=[xk], w=[xnk, ("ss", tt)])
                    S.op("vector", lambda e, tt=tt: e.tensor_scalar(out=mstmp[:, tt:tt + 1], in0=ss[:, tt:tt + 1], scalar1=1.0 / D, scalar2=EPS, op0=ALU.mult, op1=ALU.add), r=[("ss", tt)], w=[("ms", tt)])
                    S.op("scalar", lambda e, tt=tt: e.activation(out=mstmp[:, tt:tt + 1], in_=mstmp[:, tt:tt + 1], func=AF.Ln), r=[("ms", tt)], w=[("ms", tt)])
                    S.op("scalar", lambda e, tt=tt: e.activation(out=rstd[:, tt:tt + 1], in_=mstmp[:, tt:tt + 1], func=AF.Exp, scale=-0.5), r=[("ms", tt)], w=[("rstd", tt)])
                    S.op("scalar", lambda e, xs=xs, xb=xb, tt=tt: e.activation(out=xb[:], in_=xs[:], func=AF.Identity, scale=rstd[:, tt:tt + 1]), r=[xk, ("rstd", tt)], w=[xnk])

                def prepB(bi, tl):
                    t0, nt, isctx = blocks[bi]
                    hb = hT[bi % 2]
                    gi = 2 if isctx else 0
                    xb, xnk, pk = pstate[(bi, tl)]
                    for half in range(2):
                        pt_ = ptr[pk * 2 + half]; ptk = ("ptr", pk * 2 + half)
                        for kl in range(8):
                            kc = half * 8 + kl
                            S.op("tensor", lambda e, pt_=pt_, xb=xb, kl=kl, kc=kc: e.transpose(out=pt_[:, kl * 128:(kl + 1) * 128], in_=xb[:, kc * 128:(kc + 1) * 128], identity=ident[:]),
                                 r=[xnk, "ident"], w=[ptk])
                        hk = ("hT", bi % 2)
                        dst = AP(hb, half * 8 * 512 + tl * 128, [[16 * 512, 128], [512, 8], [1, 128]])
                        srcp = AP(pt_, 0, [[1024, 128], [128, 8], [1, 128]])
                        gb_ = AP(gsh, gi * 16 + half * 8, [[64, 128], [1, 8], [0, 128]])
                        sb_ = AP(gsh, (gi + 1) * 16 + half * 8, [[64, 128], [1, 8], [0, 128]])
                        mt = mtmp[mcount[0] % 2]; mk = ("mtmp", mcount[0] % 2)
                        mcount[0] += 1
                        S.op("vector", lambda e, mt=mt, srcp=srcp, gb_=gb_: e.tensor_tensor(out=mt[:], in0=srcp, in1=gb_, op=ALU.mult), r=[ptk, ("gsh", gi)], w=[mk])
                        S.op("gpsimd", lambda e, dst=dst, mt=mt, sb_=sb_: e.tensor_tensor(out=dst, in0=mt[:], in1=sb_, op=ALU.add), r=[mk, ("gsh", gi + 1)], w=[hk])

                def proj(bi, hooks=()):
                    hooks = list(hooks)
                    t0, nt, isctx = blocks[bi]
                    ntok = nt * 128
                    hb = hT[bi % 2]; hk = ("hT", bi % 2)
                    for g, (gname, func, gdt, needctx) in enumerate(GROUPS):
                        if isctx and not needctx:
                            continue
                        if gdt == F32:
                            si = scount[0] % 2
                            scount[0] += 1
                            sk = ("stgF", si)
                            st_ = stgF[si]
                        else:
                            si = scountB[0] % 2
                            scountB[0] += 1
                            sk = ("stgB", si)
                            st_ = stgB[si]
                        for ml in range(4):
                            m = g * 4 + ml
                            pa = pacc[m % 4]; pak = ("pacc", m % 4)
                            for kc in range(16):
                                S.op("tensor", lambda e, pa=pa, kc=kc, m=m, hb=hb, ntok=ntok: e.matmul(pa[:, 0:ntok], lhsT=w_sb[:, kc, m * 128:(m + 1) * 128], rhs=hb[:, kc, 0:ntok], start=(kc == 0), stop=(kc == 15)),
                                     r=[hk, ("w", g, kc)], w=[pak])
                            dst = st_[:, ml, 0:ntok]
                            if func is None:
                                S.op("vector", lambda e, dst=dst, pa=pa, ntok=ntok: e.tensor_copy(out=dst, in_=pa[:, 0:ntok]), r=[pak], w=[sk])
                            else:
                                S.op("scalar", lambda e, dst=dst, pa=pa, ntok=ntok, func=func: e.activation(out=dst, in_=pa[:, 0:ntok], func=func), r=[pak], w=[sk])
                        Tg = T if needctx else SEQ
                        toff = t0 * 128 if needctx else (t0 - 2) * 128
                        srcs = st_[:, :, 0:ntok]
                        dstd = AP(G_d[gname], toff, [[Tg, 128], [128 * Tg, 4], [1, ntok]])
                        S.op("gpsimd", lambda e, dstd=dstd, srcs=srcs: e.dma_start(out=dstd, in_=srcs), r=[sk], w=[("G", gname, bi)], dma_key=sk)
                        if hooks:
                            for fn_ in hooks.pop(0):
                                fn_()
                    while hooks:
                        for fn_ in hooks.pop(0):
                            fn_()

                for tl in range(blocks[0][1]):
                    prepA(0, tl)
                    prepB(0, tl)
                for bi in range(len(blocks)):
                    hooks = []
                    if bi + 1 < len(blocks):
                        nt1 = blocks[bi + 1][1]
                        hooks.append([lambda b=bi + 1: prepA(b, 0)])
                        for tl in range(1, nt1):
                            hooks.append([lambda b=bi + 1, t=tl: prepB(b, t - 1), lambda b=bi + 1, t=tl: prepA(b, t)])
                        hooks.append([lambda b=bi + 1, t=nt1: prepB(b, t - 1)])
                    proj(bi, hooks)
            S.emit()
            print("phase a waits", S.nwaits, "ops", len(S.ops))
        if stop_after <= 1:
            return nc

        with ExitStack() as ph:
            S = Sched(nc, top, "b")
            NP = T + 6
            xpad = sbt(ph, "xpad", [128, NP]); uu = [sbt(ph, "u%d" % i, [128, T]) for i in range(2)]; ubfs = [sbt(ph, "ubf%d" % i, [128, T], BF16) for i in range(2)]
            Rb = [sbt(ph, "Rb%d" % i, [128, T]) for i in range(2)]; Ib = [sbt(ph, "Ib%d" % i, [128, T]) for i in range(2)]; Zb = [sbt(ph, "Zb%d" % i, [128, T]) for i in range(2)]
            sga = [sbt(ph, "sga%d" % i, [128, SEQ], BF16) for i in range(2)]; yA = sbt(ph, "yA", [128, SEQ], BF16)
            cw = sbt(ph, "cw", [128, 4, 4]); cb = sbt(ph, "cb", [128, 4]); onesA = sbt(ph, "onesA", [128, 1])
            S.op("vector", lambda e: e.memset(onesA[:], 1.0), w=["onesA"])
            brt = sbt(ph, "brt", [128, 2, 4]); bit = sbt(ph, "bit", [128, 2, 4]); lamt = sbt(ph, "lamt", [128, 2, 4]); c1 = sbt(ph, "c1", [128, 2, 4])
            wr_sb = sbt(ph, "wr_sb", [128, 8, 128], BF16); wi_sb = sbt(ph, "wi_sb", [128, 8, 128], BF16)
            pg = [pst(ph, "pg%d" % i, [128, 512]) for i in range(4)]
            for k in range(4):
                S.op("sync", lambda e, k=k: e.dma_start(out=cw[:, :, k], in_=AP(convw_d, k * 512, [[1, 128], [128, 4]]), allow_slow_non_contiguous=True), w=[("cw", k)], dma_key="par", bulk=True)
            S.op("sync", lambda e: e.dma_start(out=cb[:], in_=AP(convb_d, 0, [[1, 128], [128, 4]]), allow_slow_non_contiguous=True), w=["cb"], dma_key="par", bulk=True)
            for d in range(2):
                for nm, tl_, dd in (("brt", brt, br_d), ("bit", bit, bi_d), ("lamt", lamt, lam_d)):
                    S.op("sync", lambda e, d=d, tl_=tl_, dd=dd: e.dma_start(out=tl_[:, d, :], in_=AP(dd, d * 512, [[1, 128], [128, 4]]), allow_slow_non_contiguous=True), w=[(nm, d)], dma_key="par", bulk=True)
            S.op("gpsimd", lambda e: e.dma_start(out=wr_sb[:], in_=AP(wr_d, 0, [[128, 128], [16384, 8], [1, 128]])), w=["wr"], dma_key="wr")
            S.op("gpsimd", lambda e: e.dma_start(out=wi_sb[:], in_=AP(wi_d, 0, [[128, 128], [16384, 8], [1, 128]])), w=["wi"], dma_key="wi")
            S.op("scalar", lambda e: e.activation(out=c1[:], in_=lamt[:], func=AF.Exp, scale=-1.0), r=[("lamt", 0), ("lamt", 1)], w=["c1"])
            S.op("vector", lambda e: e.tensor_scalar(out=c1[:], in0=c1[:], scalar1=1.0, scalar2=None, op0=ALU.add), r=["c1"], w=["c1"])
            S.op("scalar", lambda e: e.activation(out=c1[:], in_=c1[:], func=AF.Ln), r=["c1"], w=["c1"])
            S.op("vector", lambda e: e.tensor_scalar(out=c1[:], in0=c1[:], scalar1=-8.0, scalar2=None, op0=ALU.mult), r=["c1"], w=["c1"])
            for (a0, a1) in ((0, 2), (258, 261), (4357, 4358)):
                S.op("gpsimd", lambda e, a0=a0, a1=a1: e.memset(xpad[:, a0:a1], 0.0), w=[("pad", a0)])
            def loadA(n):
                S.op("sync", lambda e, n=n: e.dma_start(out=xpad[:, 2:258], in_=G_d["xa"].ap()[n * 128:(n + 1) * 128, 0:256]), w=["xc"], dma_key="xc")
                S.op("sync", lambda e, n=n: e.dma_start(out=xpad[:, 261:4357], in_=G_d["xa"].ap()[n * 128:(n + 1) * 128, 256:T]), w=["xl"], dma_key="xl")

            def loadS(n):
                S.op("sync", lambda e, n=n: e.dma_start(out=sga[n % 2][:], in_=G_d["sga"].ap()[n * 128:(n + 1) * 128, :]), w=[("sga", n % 2)], dma_key=("sga", n % 2))
            loadA(0)
            loadS(0)
            loadS(1)

            def convA(n):
                u = uu[n % 2]; ubf = ubfs[n % 2]
                padk = [("pad", 0), ("pad", 258), ("pad", 4357)]
                for (uo, xo, ln, xk) in ((0, 0, 256, "xc"), (256, 259, SEQ, "xl")):
                    S.op("vector", lambda e, uo=uo, xo=xo, ln=ln, n=n, u=u: e.tensor_scalar(out=u[:, uo:uo + ln], in0=xpad[:, xo:xo + ln], scalar1=cw[:, n, 0:1], scalar2=cb[:, n:n + 1], op0=ALU.mult, op1=ALU.add),
                         r=[xk, ("cw", 0), "cb"] + padk, w=[("u", n % 2, uo)])
                    for k in range(1, 4):
                        S.op("vector", lambda e, uo=uo, xo=xo, ln=ln, n=n, k=k, u=u: e.scalar_tensor_tensor(out=u[:, uo:uo + ln], in0=xpad[:, xo + k:xo + k + ln], scalar=cw[:, n, k:k + 1], in1=u[:, uo:uo + ln], op0=ALU.mult, op1=ALU.add),
                             r=[xk, ("cw", k)] + padk, w=[("u", n % 2, uo)])
                if n + 1 < 4:
                    loadA(n + 1)
                S.op("gpsimd", lambda e, u=u, ubf=ubf: e.tensor_copy(out=ubf[:], in_=u[:]), r=[("u", n % 2, 0), ("u", n % 2, 256)], w=[("ubf", n % 2)])
            convA(0)
            for n in range(4):
                u = uu[n % 2]; ubf = ubfs[n % 2]
                ubk = ("ubf", n % 2)
                for blk in range(9):
                    b0 = blk * 512; bn = min(512, T - b0)
                    for d in range(2):
                        pr = pg[d * 2]; pi = pg[d * 2 + 1]
                        S.op("tensor", lambda e, pr=pr, d=d, n=n, b0=b0, bn=bn, ubf=ubf: e.matmul(pr[:, 0:bn], lhsT=wr_sb[:, d * 4 + n, :], rhs=ubf[:, b0:b0 + bn], start=True, stop=True), r=[ubk, "wr"], w=[("pg", d * 2)])
                        S.op("tensor", lambda e, pi=pi, d=d, n=n, b0=b0, bn=bn, ubf=ubf: e.matmul(pi[:, 0:bn], lhsT=wi_sb[:, d * 4 + n, :], rhs=ubf[:, b0:b0 + bn], start=True, stop=True), r=[ubk, "wi"], w=[("pg", d * 2 + 1)])
                        S.op("scalar", lambda e, pr=pr, d=d, n=n, b0=b0, bn=bn: e.activation(out=Rb[d][:, b0:b0 + bn], in_=pr[:, 0:bn], func=AF.Sigmoid, bias=brt[:, d, n:n + 1]), r=[("pg", d * 2), ("brt", d)], w=[("Rb", d)])
                        S.op("scalar", lambda e, pi=pi, d=d, n=n, b0=b0, bn=bn: e.activation(out=Ib[d][:, b0:b0 + bn], in_=pi[:, 0:bn], func=AF.Sigmoid, bias=bit[:, d, n:n + 1]), r=[("pg", d * 2 + 1), ("bit", d)], w=[("Ib", d)])
                if n + 1 < 4:
                    convA(n + 1)
                for d in range(2):
                    S.op("scalar", lambda e, d=d, n=n: e.activation(out=Rb[d][:], in_=Rb[d][:], func=AF.Exp, scale=c1[:, d, n:n + 1]), r=[("Rb", d), "c1"], w=[("Rb", d)])
                for d in range(2):
                    S.op("gpsimd", lambda e, d=d, u=u: e.tensor_tensor(out=Ib[d][:], in0=Ib[d][:], in1=u[:], op=ALU.mult), r=[("Ib", d), ("u", n % 2, 0), ("u", n % 2, 256)], w=[("Ib", d)])
                for d in range(2):
                    S.op("scalar", lambda e, d=d: e.activation(out=Zb[d][:], in_=Rb[d][:], func=AF.Square), r=[("Rb", d)], w=[("Zb", d)])
                for d in range(2):
                    S.op("scalar", lambda e, d=d: e.activation(out=Zb[d][:], in_=Zb[d][:], func=AF.Ln, scale=-1.0, bias=onesA[:, 0:1]), r=[("Zb", d), "onesA"], w=[("Zb", d)])
                for d in range(2):
                    S.op("scalar", lambda e, d=d: e.activation(out=Zb[d][:], in_=Zb[d][:], func=AF.Exp, scale=0.5), r=[("Zb", d)], w=[("Zb", d)])
                for d in range(2):
                    S.op("vector", lambda e, d=d: e.tensor_tensor(out=Ib[d][:], in0=Ib[d][:], in1=Zb[d][:], op=ALU.mult), r=[("Ib", d), ("Zb", d)], w=[("Ib", d)])
                S.op("vector", lambda e: e.tensor_tensor_scan(out=Zb[0][:], data0=Rb[0][:], data1=Ib[0][:], initial=0.0, op0=ALU.mult, op1=ALU.add), r=[("Rb", 0), ("Ib", 0), ("Zb", 0)], w=[("Zb", 0)])
                rv = lambda t_, a0, ln: AP(t_, a0 + ln - 1, [[T, 128], [-1, ln]])
                S.op("vector", lambda e: e.tensor_tensor_scan(out=rv(Zb[1], 0, 256), data0=rv(Rb[1], 0, 256), data1=rv(Ib[1], 0, 256), initial=0.0, op0=ALU.mult, op1=ALU.add), r=[("Rb", 1), ("Ib", 1), ("Zb", 1)], w=[("Zb", 1)])
                S.op("vector", lambda e: e.tensor_tensor_scan(out=rv(Zb[1], 256, SEQ), data0=rv(Rb[1], 256, SEQ), data1=rv(Ib[1], 256, SEQ), initial=Zb[1][:, 0:1], op0=ALU.mult, op1=ALU.add), r=[("Rb", 1), ("Ib", 1), ("Zb", 1)], w=[("Zb", 1)])
                S.op("gpsimd", lambda e: e.tensor_tensor(out=Zb[0][:, 256:T], in0=Zb[0][:, 256:T], in1=Zb[1][:, 256:T], op=ALU.add), r=[("Zb", 0), ("Zb", 1)], w=[("Zb", 0)])
                S.op("vector", lambda e, n=n: e.tensor_tensor(out=yA[:], in0=Zb[0][:, 256:T], in1=sga[n % 2][:], op=ALU.mult), r=[("Zb", 0), ("sga", n % 2)], w=["yA"])
                if n + 2 < 4:
                    loadS(n + 2)
                S.op("sync", lambda e, n=n: e.dma_start(out=y_u[n].ap(), in_=yA[:]), r=["yA"], w=[("y_u", n)], dma_key="yAst")
                if not debug:
                    S.op("gpsimd", lambda e, n=n: e.collective_compute("AllGather", ALU.bypass, replica_groups=RG, ins=[y_u[n].ap().opt()], outs=[yg_u[n].ap().opt()]), r=[("y_u", n)], w=[("yg_u", n)], dma_key=("ag", n), inc=1)
            S.emit(nofinal=[("ag", 3)])
            agsig = {}
            for n_ in range(4):
                if ("ag", n_) in S.dma_sems:
                    agsig[n_] = tuple(S.dma_sems[("ag", n_)])
            print("phase b waits", S.nwaits, "ops", len(S.ops))
        if stop_after <= 2:
            return nc

        with ExitStack() as ph:
            S = Sched(nc, top, "c")
            SF = sbt(ph, "SF", [128, T]); qb = sbt(ph, "qb", [128, SEQ], BF16); vb = sbt(ph, "vb", [128, T], BF16); vcm = sbt(ph, "vcm", [128, T], BF16)
            Vtok = sbt(ph, "Vtok", [64, NCHK, 128], BF16); Ktok = sbt(ph, "Ktok", [64, NCHK, 128], BF16)
            W = [sbt(ph, "W%d" % i, [128, T]) for i in range(3)]
            Qt = [sbt(ph, "Qt%d" % i, [128, SEQ], BF16) for i in range(2)]; Kt = [sbt(ph, "Kt%d" % i, [128, T], BF16) for i in range(2)]
            AT = sbt(ph, "AT", [64, SEQ], BF16); O = sbt(ph, "O", [128, SEQ])
            Mx = sbt(ph, "Mx", [128, T + 1], BF16)
            sgbb = sbt(ph, "sgbb", [128, SEQ], BF16)
            Tst = [sbt(ph, "Tst%d" % i, [128, 128]) for i in range(4)]
            Sbf = [sbt(ph, "Sbf%d" % i, [128, 128], BF16) for i in range(4)]
            gam = [sbt(ph, "gam%d" % i, [128, NCHK]) for i in range(2)]
            lbl = sbt(ph, "lbl", [128, 4, 4]); lb = sbt(ph, "lb", [128, 2, 4]); oml = sbt(ph, "oml", [128, 2, 4])
            hnw = sbt(ph, "hnw", [128, 1]); epsT = sbt(ph, "epsT", [128, 1])
            mF = sbt(ph, "mF", [64, 64]); mB = sbt(ph, "mB", [64, 64]); ones_bf = sbt(ph, "ones_bf", [128, 128], BF16)
            identb_f = sbt(ph, "identb_f", [128, 128]); identb = sbt(ph, "identb", [128, 128], BF16)
            pT = [pst(ph, "pT%d" % i, [128, 1024], BF16) for i in range(2)]
            pA = [pst(ph, "pA%d" % i, [128, 512]) for i in range(2)]
            pO = [pst(ph, "pO%d" % i, [128, 512]) for i in range(2)]
            pS = [pst(ph, "pS%d" % i, [128, 512]) for i in range(2)]
            S.op("gpsimd", lambda e: e.memset(identb_f[:], 1.0), w=["identf"])
            S.op("gpsimd", lambda e: e.affine_select(out=identb_f[:], in_=identb_f[:], pattern=[[-1, 128]], compare_op=ALU.is_equal, fill=0.0, base=0, channel_multiplier=1), r=["identf"], w=["identf"])
            S.op("vector", lambda e: e.tensor_copy(out=identb[:], in_=identb_f[:]), r=["identf"], w=["ident"])
            S.op("gpsimd", lambda e: e.memset(mF[:], 1.0), w=["mF"])
            S.op("gpsimd", lambda e: e.affine_select(out=mF[:], in_=mF[:], pattern=[[1, 64]], compare_op=ALU.is_ge, fill=0.0, base=0, channel_multiplier=-1), r=["mF"], w=["mF"])
            S.op("gpsimd", lambda e: e.memset(mB[:], 1.0), w=["mB"])
            S.op("gpsimd", lambda e: e.affine_select(out=mB[:], in_=mB[:], pattern=[[-1, 64]], compare_op=ALU.is_ge, fill=0.0, base=0, channel_multiplier=1), r=["mB"], w=["mB"])
            S.op("vector", lambda e: e.memset(ones_bf[:], 1.0), w=["ones"])
            S.op("vector", lambda e: e.memset(epsT[:], EPS), w=["eps"])
            S.op("vector", lambda e: e.memset(Mx[:], 1.0), w=["Mx"])
            S.op("vector", lambda e: e.memset(AP(Mx, 0, [[T + 1, 128], [64, NCHK + 1]]), 0.0), r=["Mx"], w=["Mx"])
            for d in range(2):
                for l in range(2):
                    S.op("sync", lambda e, d=d, l=l: e.dma_start(out=lbl[:, d * 2 + l, :], in_=AP(lbl_d, (d * 2 + l) * 512, [[1, 128], [128, 4]]), allow_slow_non_contiguous=True), w=[("lbl", d * 2 + l)], dma_key="par", bulk=True)
            S.op("sync", lambda e: e.dma_start(out=hnw[:], in_=AP(hnw_d, 0, [[1, 128], [1, 1]])), w=["hnw"], dma_key="par", bulk=True)
            for d in range(2):
                S.op("vector", lambda e, d=d: e.tensor_tensor(out=lb[:, d, :], in0=lbl[:, 2 * d, :], in1=lbl[:, 2 * d + 1, :], op=ALU.subtract), r=[("lbl", 2 * d), ("lbl", 2 * d + 1)], w=[("lb", d)])
            S.op("scalar", lambda e: e.activation(out=lb[:], in_=lb[:], func=AF.Sigmoid), r=[("lb", 0), ("lb", 1)], w=["lbs"])
            S.op("vector", lambda e: e.tensor_scalar(out=oml[:], in0=lb[:], scalar1=-1.0, scalar2=1.0, op0=ALU.mult, op1=ALU.add), r=["lbs"], w=["oml"])

            def perm_in(t_, tw):
                return AP(t_, tw - SEQ, [[tw, 128], [1, 64], [64, 64]])

            def nat3(t_, tw, off):
                return AP(t_, off, [[tw, 128], [64, 64], [1, 64]])

            tctr = [0]

            def transposes(src, dstT, srck, dstk):
                for g8 in range(9):
                    n8 = min(8, NCHK - g8 * 8)
                    pt_ = pT[tctr[0] % 2]; ptk = ("pT", tctr[0] % 2)
                    tctr[0] += 1
                    for i8 in range(n8):
                        ck = g8 * 8 + i8
                        S.op("tensor", lambda e, pt_=pt_, i8=i8, ck=ck, src=src: e.transpose(out=pt_[0:64, i8 * 128:(i8 + 1) * 128], in_=src[:, ck * 64:(ck + 1) * 64], identity=identb[:]), r=[srck, "ident"], w=[ptk])
                    if g8 % 2 == 0:
                        S.op("scalar", lambda e, pt_=pt_, g8=g8, n8=n8, dstT=dstT: e.activation(out=dstT[:, g8 * 8:g8 * 8 + n8, :], in_=pt_[0:64, 0:n8 * 128], func=AF.Identity), r=[ptk], w=[dstk])
                    else:
                        S.op("vector", lambda e, pt_=pt_, g8=g8, n8=n8, dstT=dstT: e.tensor_copy(out=dstT[:, g8 * 8:g8 * 8 + n8, :], in_=pt_[0:64, 0:n8 * 128]), r=[ptk], w=[dstk])

            def vcopy(hd):
                r0 = hd * 128
                S.op("gpsimd", lambda e: e.tensor_copy(out=vcm[:, 0:256], in_=vb[:, 0:256]), r=["vb"], w=["vcm"])
                S.op("gpsimd", lambda e: e.tensor_copy(out=nat3(vcm, T, 256), in_=perm_in(vb, T)), r=["vb"], w=["vcm"])
                if hd + 1 < 4:
                    S.op("sync", lambda e, r0=r0: e.dma_start(out=vb[:], in_=G_d["v"].ap()[r0 + 128:r0 + 256, :]), w=["vb"], dma_key="vb")

            def vstage(hd):
                r0 = hd * 128
                S.op("sync", lambda e, r0=r0: e.dma_start(out=sgbb[:], in_=G_d["sgb"].ap()[r0:r0 + 128, :]), w=["sgbb"], dma_key="sgbb")
                if hd == 0:
                    vcopy(0)
                transposes(vcm, Vtok, "vcm", "Vtok")

            def setup_closures(k):
                hd, d = divmod(k, 2)
                r0 = hd * 128
                b = k % 2
                Qb = Qt[b]; Kb = Kt[b]; gb = gam[b]
                Qk = ("Qt", b); Kk = ("Kt", b); gk = ("gam", b)
                lbc = lb[:, d, hd:hd + 1]; omc = oml[:, d, hd:hd + 1]
                cl = []
                cl.append(lambda: S.op("vector", lambda e: e.tensor_scalar(out=W[0][:, 0:256], in0=SF[:, 0:256], scalar1=omc, scalar2=lbc, op0=ALU.mult, op1=ALU.add), r=["SF", "lbs", "oml"], w=["W0"]))
                cl.append(lambda: S.op("vector", lambda e: e.tensor_scalar(out=nat3(W[0], T, 256), in0=perm_in(SF, T), scalar1=omc, scalar2=lbc, op0=ALU.mult, op1=ALU.add), r=["SF", "lbs", "oml"], w=["W0"]))
                if d == 0:
                    cl.append(lambda: S.op("sync", lambda e: e.dma_start(out=SF[:], in_=G_d["sfb"].ap()[r0:r0 + 128, :]), w=["SF"], dma_key="SF"))
                elif hd + 1 < 4:
                    cl.append(lambda: S.op("sync", lambda e: e.dma_start(out=SF[:], in_=G_d["sff"].ap()[r0 + 128:r0 + 256, :]), w=["SF"], dma_key="SF"))
                cl.append(lambda: S.op("gpsimd", lambda e: e.tensor_scalar(out=W[1][:], in0=W[0][:], scalar1=-1.0, scalar2=1.0, op0=ALU.mult, op1=ALU.add), r=["W0"], w=["W1"]))
                cl.append(lambda: S.op("scalar", lambda e: e.activation(out=W[0][:], in_=W[0][:], func=AF.Ln), r=["W0", "W1"], w=["W0"]))
                if d == 0:
                    cl.append(lambda: S.op("vector", lambda e: e.tensor_tensor_scan(out=W[2][:], data0=Mx[:, 0:T], data1=W[0][:], initial=0.0, op0=ALU.mult, op1=ALU.add), r=["W0", "Mx"], w=["W2"]))
                else:
                    cl.append(lambda: S.op("vector", lambda e: e.tensor_tensor_scan(out=AP(W[2], T - 1, [[T, 128], [-1, T]]), data0=AP(Mx, T, [[T + 1, 128], [-1, T]]), data1=AP(W[0], T - 1, [[T, 128], [-1, T]]), initial=0.0, op0=ALU.mult, op1=ALU.add), r=["W0", "Mx"], w=["W2"]))
                cl.append(lambda: S.op("gpsimd", lambda e: e.tensor_scalar(out=W[2][:], in0=W[2][:], scalar1=-80.0, scalar2=None, op0=ALU.max), r=["W2"], w=["W2"]))
                cl.append(lambda: S.op("scalar", lambda e: e.activation(out=W[0][:], in_=W[2][:], func=AF.Exp), r=["W2"], w=["W0"]))
                cl.append(lambda: S.op("scalar", lambda e: e.activation(out=W[2][:], in_=W[2][:], func=AF.Exp, scale=-1.0), r=["W2", "W0"], w=["W2"]))
                ge = 63 if d == 0 else 0
                cl.append(lambda: S.op("gpsimd", lambda e: e.tensor_copy(out=gb[:], in_=AP(W[0], ge, [[T, 128], [64, NCHK]])), r=["W0"], w=[gk]))
                cl.append(lambda: S.op("vector", lambda e: e.tensor_tensor(out=nat3(Qb, SEQ, 0), in0=perm_in(qb, SEQ), in1=nat3(W[0], T, 256), op=ALU.mult), r=["qb", "W0"], w=[Qk]))
                if d == 1 and hd + 1 < 4:
                    cl.append(lambda: S.op("sync", lambda e: e.dma_start(out=qb[:], in_=G_d["q"].ap()[r0 + 128:r0 + 256, :]), w=["qb"], dma_key="qb"))
                cl.append(lambda: S.op("gpsimd", lambda e: e.tensor_tensor(out=Kb[:], in0=W[1][:], in1=W[2][:], op=ALU.mult), r=["W1", "W2"], w=[Kk]))
                return cl

            def pestage(k):
                hd, d = divmod(k, 2)
                b = k % 2
                Qb = Qt[b]; Kb = Kt[b]; Qk = ("Qt", b); Kk = ("Kt", b)
                transposes(Kb, Ktok, Kk, "Ktok")
                msk = mF if d == 0 else mB
                mk = "mF" if d == 0 else "mB"
                for g8 in range(8):
                    pa_ = pA[g8 % 2]; pak = ("pA", g8 % 2)
                    for i8 in range(8):
                        lc = g8 * 8 + i8
                        S.op("tensor", lambda e, pa_=pa_, i8=i8, lc=lc: e.matmul(pa_[0:64, i8 * 64:(i8 + 1) * 64], lhsT=Kb[:, (4 + lc) * 64:(5 + lc) * 64], rhs=Qb[:, lc * 64:(lc + 1) * 64], start=True, stop=True), r=[Kk, Qk], w=[pak])
                    S.op("vector", lambda e, pa_=pa_, g8=g8, msk=msk: e.tensor_tensor(out=AP(AT, g8 * 512, [[SEQ, 64], [64, 8], [1, 64]]), in0=AP(pa_, 0, [[512, 64], [64, 8], [1, 64]]), in1=AP(msk, 0, [[64, 64], [0, 8], [1, 64]]), op=ALU.mult), r=[pak, mk], w=["AT"])

            def recurrence(k, extra):
                hd, d = divmod(k, 2)
                b = k % 2
                Qb = Qt[b]; Qk = ("Qt", b); gb = gam[b]; gk = ("gam", b)
                order = [0, 1, 2, 3] + list(range(4, NCHK)) if d == 0 else [3, 2, 1, 0] + list(range(NCHK - 1, 3, -1))
                LA = 3
                nst = len(order)
                ring = [(pS[0], ("pS", 0)), (pS[1], ("pS", 1)), (pA[0], ("pA", 0)), (pA[1], ("pA", 1))]
                stride = max(1, (nst - 8) // max(1, len(extra))) if extra else nst
                ei = 0

                def emit_ps(i):
                    ck_ = order[i]
                    ps_, psk_ = ring[i % 4]
                    S.op("tensor", lambda e, ps_=ps_, ck_=ck_: e.matmul(ps_[:, 0:128], lhsT=Ktok[:, ck_, :], rhs=Vtok[:, ck_, :], start=True, stop=True), r=["Ktok", "Vtok"], w=[psk_])
                for i in range(min(LA, nst)):
                    emit_ps(i)
                prev = None
                for i, ck in enumerate(order):
                    tn = Tst[i % 4]; tnk = ("Tst", i % 4)
                    to = Tst[(i - 1) % 4]; tok_ = ("Tst", (i - 1) % 4)
                    sb_n = Sbf[i % 4]; sbk = ("Sbf", i % 4)
                    sb_o = Sbf[(i - 1) % 4]; sbok = ("Sbf", (i - 1) % 4)
                    if ck >= 4:
                        lc = ck - 4
                        slot = lc % 8
                        po_ = pO[(lc // 8) % 2]; pok = ("pO", (lc // 8) % 2)
                        S.op("tensor", lambda e, po_=po_, slot=slot, ck=ck, lc=lc: e.matmul(po_[:, slot * 64:(slot + 1) * 64], lhsT=Vtok[:, ck, :], rhs=AT[:, lc * 64:(lc + 1) * 64], start=True, stop=False), r=["Vtok", "AT"], w=[pok])
                        S.op("tensor", lambda e, po_=po_, slot=slot, lc=lc, sb_o=sb_o: e.matmul(po_[:, slot * 64:(slot + 1) * 64], lhsT=sb_o[:], rhs=Qb[:, lc * 64:(lc + 1) * 64], start=False, stop=True), r=[sbok, Qk], w=[pok])
                    if i + LA < nst:
                        emit_ps(i + LA)
                    ps_, psk = ring[i % 4]
                    if i == 0:
                        S.op("vector", lambda e, tn=tn, ps_=ps_: e.tensor_copy(out=tn[:], in_=ps_[:, 0:128]), r=[psk], w=[tnk])
                    else:
                        S.op("vector", lambda e, tn=tn, to=to, ps_=ps_, prev=prev: e.scalar_tensor_tensor(out=tn[:], in0=to[:], scalar=gb[:, prev:prev + 1], in1=ps_[:, 0:128], op0=ALU.mult, op1=ALU.add), r=[tok_, psk, gk], w=[tnk])
                    if i + 1 < nst:
                        S.op("scalar", lambda e, sb_n=sb_n, tn=tn, ck=ck: e.activation(out=sb_n[:], in_=tn[:], func=AF.Identity, scale=gb[:, ck:ck + 1]), r=[tnk, gk], w=[sbk])
                    prev = ck
                    if ck >= 4:
                        lc = ck - 4
                        done = (lc % 8 == 7) if d == 0 else (lc % 8 == 0)
                        if done:
                            g8 = lc // 8
                            if d == 0:
                                S.op("scalar", lambda e, po_=po_, g8=g8: e.activation(out=O[:, g8 * 512:(g8 + 1) * 512], in_=po_[:], func=AF.Identity), r=[pok], w=[("O", g8)])
                            else:
                                S.op("vector", lambda e, po_=po_, g8=g8: e.tensor_tensor(out=O[:, g8 * 512:(g8 + 1) * 512], in0=po_[:], in1=O[:, g8 * 512:(g8 + 1) * 512], op=ALU.add), r=[pok, ("O", g8)], w=[("O", g8)])
                    if extra and ei < len(extra) and i >= 4 and (i - 4) % stride == 0:
                        extra[ei]()
                        ei += 1
                while extra and ei < len(extra):
                    extra[ei]()
                    ei += 1

            def final(hd):
                Ok = [("O", g8) for g8 in range(8)]
                sqb = Kt[1]; sqk = ("Kt", 1)
                yb_ = Qt[1]; ybk_ = ("Qt", 1)
                S.op("scalar", lambda e: e.activation(out=sqb[:, 0:SEQ], in_=O[:], func=AF.Square), r=Ok + [sqk], w=[sqk])
                for g8 in range(8):
                    pa_ = pA[g8 % 2]; pak = ("pA", g8 % 2)
                    S.op("tensor", lambda e, pa_=pa_, g8=g8: e.matmul(pa_[:], lhsT=ones_bf[:], rhs=sqb[:, g8 * 512:(g8 + 1) * 512], start=True, stop=True), r=[sqk, "ones"], w=[pak])
                    S.op("scalar", lambda e, pa_=pa_, g8=g8: e.activation(out=W[0][:, g8 * 512:(g8 + 1) * 512], in_=pa_[:], func=AF.Ln, scale=1.0 / 128, bias=epsT[:, 0:1]), r=[pak, "eps"], w=["W0"])
                S.op("scalar", lambda e: e.activation(out=W[0][:, 0:SEQ], in_=W[0][:, 0:SEQ], func=AF.Exp, scale=-0.5), r=["W0"], w=["W0"])
                S.op("vector", lambda e: e.tensor_tensor(out=O[:], in0=O[:], in1=W[0][:, 0:SEQ], op=ALU.mult), r=Ok + ["W0"], w=Ok)
                S.op("vector", lambda e: e.scalar_tensor_tensor(out=nat3(yb_, SEQ, 0), in0=AP(O, 0, [[SEQ, 128], [1, 64], [64, 64]]), scalar=hnw[:, 0:1], in1=nat3(sgbb, SEQ, 0), op0=ALU.mult, op1=ALU.mult), r=Ok + ["hnw", "sgbb", ybk_], w=[ybk_])
                S.op("sync", lambda e, hd=hd: e.dma_start(out=y_u[4 + hd].ap(), in_=yb_[:]), r=[ybk_], w=[("y_u", 4 + hd)], dma_key="yBst")
                if not debug:
                    S.op("gpsimd", lambda e, hd=hd: e.collective_compute("AllGather", ALU.bypass, replica_groups=RG, ins=[y_u[4 + hd].ap().opt()], outs=[yg_u[4 + hd].ap().opt()]), r=[("y_u", 4 + hd)], w=[("yg_u", 4 + hd)], dma_key=("ag", 4 + hd), inc=1)

            S.op("sync", lambda e: e.dma_start(out=vb[:], in_=G_d["v"].ap()[0:128, :]), w=["vb"], dma_key="vb")
            S.op("sync", lambda e: e.dma_start(out=qb[:], in_=G_d["q"].ap()[0:128, :]), w=["qb"], dma_key="qb")
            S.op("sync", lambda e: e.dma_start(out=SF[:], in_=G_d["sff"].ap()[0:128, :]), w=["SF"], dma_key="SF")
            for c_ in setup_closures(0):
                c_()
            for k in range(8):
                hd, d = divmod(k, 2)
                if d == 0:
                    vstage(hd)
                pestage(k)
                extra = setup_closures(k + 1) if k + 1 < 8 else []
                if d == 0 and hd + 1 < 4:
                    extra = extra + [lambda h_=hd + 1: vcopy(h_)]
                recurrence(k, extra)
                if d == 1:
                    final(hd)
            S.emit(nofinal=[("ag", 7)])
            for n_ in range(4, 8):
                if ("ag", n_) in S.dma_sems:
                    agsig[n_] = tuple(S.dma_sems[("ag", n_)])
            print("phase c waits", S.nwaits, "ops", len(S.ops))
        if stop_after <= 3:
            return nc

        with ExitStack() as ph:
            S = Sched(nc, top, "d")
            wo_sb = sbt(ph, "wo_sb", [128, 32, 512], BF16)
            Yb = [sbt(ph, "Yb%d" % i, [128, 32, 512], BF16) for i in range(2)]
            Z = sbt(ph, "Z", [128, 32, 512])
            gate_b = sbt(ph, "gate_b", [128, 512]); fnw_b = sbt(ph, "fnw_b", [128, 512])
            tmpb = [sbt(ph, "tmpb%d" % i, [128, 512]) for i in range(2)]
            junk = sbt(ph, "junk", [128, 512])
            ss3 = sbt(ph, "ss3", [128, 32]); ssg = sbt(ph, "ssg", [128, 4, 32]); rs3 = sbt(ph, "rs3", [128, 32])
            po3 = [pst(ph, "po3_%d" % i, [128, 512]) for i in range(4)]
            import os
            wstg = [sbt(ph, "wstg%d" % i, [128, 4, 512]) for i in range(2)]

            def wload():
                for c4 in range(8):
                    ws = wstg[c4 % 2]; wk = ("wstg", c4 % 2)
                    S.op("sync", lambda e, ws=ws, c4=c4: e.dma_start(out=ws[:], in_=AP(wout_d, c4 * 4 * 128 * 512, [[512, 128], [128 * 512, 4], [1, 512]])), w=[wk], dma_key=wk)
                    S.op("scalar", lambda e, ws=ws, c4=c4: e.activation(out=wo_sb[:, c4 * 4:(c4 + 1) * 4, :], in_=ws[:], func=AF.Identity), r=[wk], w=[("wo", c4 // 2)])
            if not os.environ.get("K_NOGATE"):
                S.op("sync", lambda e: e.dma_start(out=gate_b[:], in_=AP(ada_g, 0, [[0, 128], [1, 512]])), w=["gate"], dma_key="par", bulk=True)
            if not os.environ.get("K_NOFNW"):
                S.op("sync", lambda e: e.dma_start(out=fnw_b[:], in_=AP(fnw_d, 0, [[0, 128], [1, 512]])), w=["fnw"], dma_key="par", bulk=True)
            import os
            NBLK = int(os.environ.get('K_D_NBLK', 8))
            for blk in range(NBLK):
                yb_ = Yb[blk % 2]; ybk = ("Yb", blk % 2)
                S.op("sync", lambda e, blk=blk: e.dma_start(out=Z[:, blk * 4:(blk + 1) * 4, :], in_=AP(xq_d, blk * 512 * 512, [[512, 128], [128 * 512, 4], [1, 512]])), w=[("Z", blk * 4 + t_) for t_ in range(4)], dma_key=("zl", blk))
                if blk == 0:
                    wload()
                for i in range(8):
                    S.op("sync", lambda e, yb_=yb_, i=i, blk=blk: e.dma_start(out=yb_[:, i * 4:(i + 1) * 4, :], in_=AP(yg_u[i], blk * 512, [[SEQ, 128], [128 * SEQ, 4], [1, 512]])), w=[(ybk, i)], dma_key=ybk, grp=blk, ext=([agsig[i]] if i in agsig else []))
                for tl in range(4):
                    ti = blk * 4 + tl
                    zk = ("Z", ti)
                    p_ = po3[ti % 4]; pk = ("po3", ti % 4)
                    for kc in range(32):
                        S.op("tensor", lambda e, p_=p_, kc=kc, tl=tl, yb_=yb_: e.matmul(p_[:], lhsT=yb_[:, kc, tl * 128:(tl + 1) * 128], rhs=wo_sb[:, kc, :], start=(kc == 0), stop=(kc == 31)),
                             r=[(ybk, kc // 4), ("wo", kc // 8)], w=[pk])
                    tb = tmpb[ti % 2]; tbk = ("tmpb", ti % 2)
                    S.op("vector", lambda e, p_=p_, tb=tb: e.tensor_tensor(out=tb[:], in0=p_[:], in1=gate_b[:], op=ALU.mult), r=[pk, "gate"], w=[tbk])
                    S.op("gpsimd", lambda e, tb=tb, ti=ti: e.tensor_tensor(out=Z[:, ti, :], in0=Z[:, ti, :], in1=tb[:], op=ALU.add), r=[zk, tbk], w=[zk])
                    S.op("scalar", lambda e, ti=ti: e.activation(out=junk[:], in_=Z[:, ti, :], func=AF.Square, accum_out=ss3[:, ti:ti + 1]), r=[zk], w=["junk", ("ss3", ti)])
            if os.environ.get('K_D_NOFIN'):
                S.emit()
                return nc
            ssk = [("ss3", ti) for ti in range(32)]
            S.op("sync", lambda e: e.dma_start(out=ss_loc.ap(), in_=ss3[:]), r=ssk, w=["ss_loc"], dma_key="ssst")
            import os
            if os.environ.get("K_SKIP_SSAG"):
                S.op("sync", lambda e: e.dma_start(out=ss_all.ap()[0:128, :], in_=ss_loc.ap()), r=["ss_loc"], w=["ss_all"], dma_key="agss")
            else:
                S.op("gpsimd", lambda e: e.collective_compute("AllGather", ALU.bypass, replica_groups=RG, ins=[ss_loc.ap().opt()], outs=[ss_all.ap().opt()]), r=["ss_loc"], w=["ss_all"], dma_key="agss", inc=1)
            S.op("sync", lambda e: e.dma_start(out=ssg[:], in_=AP(ss_all, 0, [[32, 128], [128 * 32, 4], [1, 32]])), r=["ss_all"], w=["ssg"], dma_key="ssld")
            S.op("vector", lambda e: e.tensor_tensor(out=rs3[:], in0=ssg[:, 0, :], in1=ssg[:, 1, :], op=ALU.add), r=["ssg"], w=["rs3"])
            S.op("vector", lambda e: e.tensor_tensor(out=rs3[:], in0=rs3[:], in1=ssg[:, 2, :], op=ALU.add), r=["ssg", "rs3"], w=["rs3"])
            S.op("vector", lambda e: e.tensor_tensor(out=rs3[:], in0=rs3[:], in1=ssg[:, 3, :], op=ALU.add), r=["ssg", "rs3"], w=["rs3"])
            S.op("vector", lambda e: e.tensor_scalar(out=rs3[:], in0=rs3[:], scalar1=1.0 / D, scalar2=EPS, op0=ALU.mult, op1=ALU.add), r=["rs3"], w=["rs3"])
            S.op("scalar", lambda e: e.activation(out=rs3[:], in_=rs3[:], func=AF.Ln), r=["rs3"], w=["rs3"])
            S.op("scalar", lambda e: e.activation(out=rs3[:], in_=rs3[:], func=AF.Exp, scale=-0.5), r=["rs3"], w=["rs3"])
            for ti in range(32):
                zk = ("Z", ti)
                eng = "vector" if ti % 2 == 0 else "vector"
                S.op(eng, lambda e, ti=ti: e.scalar_tensor_tensor(out=Z[:, ti, :], in0=Z[:, ti, :], scalar=rs3[:, ti:ti + 1], in1=fnw_b[:], op0=ALU.mult, op1=ALU.mult), r=[zk, "rs3", "fnw"], w=[zk])
                S.op("sync", lambda e, ti=ti: e.dma_start(out=out_d.ap()[ti * 128:(ti + 1) * 128, :], in_=Z[:, ti, :]), r=[zk], w=[("outd", ti)], dma_key=("ost", ti % 4))
            S.emit()
            print("phase d waits", S.nwaits, "ops", len(S.ops))
    return nc


def shard_inputs(inputs):
    f = lambda a: np.ascontiguousarray(a, dtype=np.float32)
    x = inputs["x"]; ctx = inputs["ctx"]; c = inputs["c"]; c_ctx = inputs["c_ctx"]
    w_in = inputs["w_in"][0]; w_out = inputs["w_out"][0]
    maps = []
    for b in range(2):
        for j in range(4):
            cols = np.concatenate([np.arange(g * D + j * 512, g * D + (j + 1) * 512) for g in range(7)])
            rows = np.concatenate([(0 if i < 4 else D) + r * 512 + (i % 4) * 128 + np.arange(128) for i in range(8) for r in range(4)])
            fq = slice(j * 512, (j + 1) * 512)
            sl = slice(j * 512, (j + 1) * 512)
            m = {
                "x": f(x[b]), "ctx": f(ctx[b]), "cvec": f(np.stack([c[b], c_ctx])),
                "ada_w": f(inputs["ada_w"][0]), "ada_b": f(inputs["ada_b"][0][None]), "norm_w": f(inputs["norm_w"][0][None]),
                "w_in": f(w_in[:, cols]),
                "conv_w": f(inputs["conv_w"][0][:, sl]), "conv_b": f(inputs["conv_b"][0][None, sl]),
                "lru_wr": f(inputs["lru_wr"][0][:, j * 4:(j + 1) * 4]), "lru_wi": f(inputs["lru_wi"][0][:, j * 4:(j + 1) * 4]),
                "lru_br": f(inputs["lru_br"][0][:, sl]), "lru_bi": f(inputs["lru_bi"][0][:, sl]), "lru_lam": f(inputs["lru_lambda"][0][:, sl]),
                "lb_logits": f(inputs["hgrn_lb_logits"][:, :, sl]), "hnorm_w": f(inputs["hgrn_norm_w"][0][None]),
                "w_out": f(w_out[rows][:, fq]), "fnorm_w": f(inputs["final_norm_w"][None, fq]),
                "xq": f(x[b][:, fq]), "ada_wg": f(inputs["ada_w"][0][:, 2 * D + j * 512:2 * D + (j + 1) * 512]),
                "ada_bg": f(inputs["ada_b"][0][None, 2 * D + j * 512:2 * D + (j + 1) * 512]),
            }
            maps.append(m)
    return maps


def kernel(**inputs):
    nc = build_program()
    maps = shard_inputs(inputs)
    res = run_bass_kernel_spmd(nc, maps, core_ids=list(range(8)))
    out = np.zeros((2, SEQ, D), np.float32)
    for b in range(2):
        for j in range(4):
            out[b, :, j * 512:(j + 1) * 512] = res.results[b * 4 + j]["out"]
    return out
```

```python
import numpy as np
import concourse.bass as bass
import concourse.mybir as mybir
from concourse.bass_utils import run_bass_kernel_spmd
from contextlib import ExitStack

F32 = mybir.dt.float32
BF16 = mybir.dt.bfloat16
AF = mybir.ActivationFunctionType
ALU = mybir.AluOpType

D = 2048
SEQ = 4096
CTX = 256
T = CTX + SEQ
NCOL = 3584
EPS = 1e-6
CH = 64
NCHK = T // CH


class Sched:
    ENG = ("tensor", "vector", "scalar", "gpsimd", "sync")
    ROT = 4000

    def __init__(self, nc, stack, name, sync_same=True):
        self.nc = nc
        self.stack = stack
        self.name = name
        self.ops = []
        self.last_w = {}
        self.readers = {}
        self.sync_same = sync_same
        self.bulk = set()

    def op(self, eng, fn, r=(), w=(), dma_key=None, bulk=False, inc=16, grp=None, ext=()):
        i = len(self.ops)
        deps = set()
        for k in r:
            if k in self.last_w:
                deps.add(self.last_w[k])
        for k in w:
            if k in self.last_w:
                deps.add(self.last_w[k])
            last = {}
            for ri in self.readers.get(k, ()):
                ro = self.ops[ri]
                if ro["dma_key"] is not None:
                    deps.add(ri)
                else:
                    last[ro["eng"]] = max(last.get(ro["eng"], -1), ri)
            deps.update(last.values())
        pf = getattr(self, "pending_fence", None)
        if pf and eng in pf:
            deps.update(pf.pop(eng))
        self.ops.append(dict(eng=eng, fn=fn, deps=deps, dma_key=dma_key, idx=i, sig=None, inc=inc))
        if bulk and grp is None:
            grp = "all"
        self.ops[-1]["grp"] = grp
        self.ops[-1]["ext"] = list(ext)
        for k in r:
            self.readers.setdefault(k, []).append(i)
        for k in w:
            self.last_w[k] = i
            self.readers[k] = []
        return i

    def fence(self):
        last = {}
        dmas = set()
        for o in self.ops:
            if o["dma_key"] is not None:
                dmas.add(o["idx"])
            else:
                last[o["eng"]] = o["idx"]
        self.pending_fence = {e: set(last.values()) | set(dmas) for e in self.ENG}

    def emit(self, nofinal=()):
        nc = self.nc
        ops = self.ops
        need = [False] * len(ops)
        for o in ops:
            for d in o["deps"]:
                if ops[d]["dma_key"] is not None:
                    continue
                if ops[d]["eng"] != o["eng"] or (self.sync_same and o["eng"] != "tensor"):
                    need[d] = True
        eng_cnt = {e: 0 for e in self.ENG}
        eng_sems = {e: [] for e in self.ENG}
        dma_sems = {}
        for o in ops:
            if o["dma_key"] is not None:
                k = o["dma_key"]
                if k not in dma_sems:
                    dma_sems[k] = [self.stack.enter_context(nc.semaphore("d%s_%d" % (self.name, len(dma_sems)))), 0]
                dma_sems[k][1] += o["inc"]
                o["sig"] = [dma_sems[k][0], dma_sems[k][1], o["inc"]]
            elif need[o["idx"]]:
                e = o["eng"]
                c = eng_cnt[e]
                si = c // self.ROT
                if si >= len(eng_sems[e]):
                    eng_sems[e].append(self.stack.enter_context(nc.semaphore("e%s_%s_%d" % (self.name, e, si))))
                o["sig"] = [eng_sems[e][si], c % self.ROT + 1, 1]
                eng_cnt[e] = c + 1
        gmax = {}
        for o in ops:
            if o["dma_key"] is not None and o["grp"] is not None:
                gk = (o["dma_key"], o["grp"])
                gmax[gk] = max(gmax.get(gk, 0), o["sig"][1])
        for o in ops:
            if o["dma_key"] is not None and o["grp"] is not None:
                o["sig"][1] = gmax[(o["dma_key"], o["grp"])]
        self.dma_sems = dma_sems
        nwaits = {e: 0 for e in self.ENG}
        with nc.Block(no_gpsimd_drain=(self.name != "a")) as block:
            for eng in self.ENG:
                def body(e, eng=eng):
                    seen = {}
                    for o in ops:
                        if o["eng"] != eng:
                            continue
                        for d in sorted(o["deps"]):
                            sg = ops[d]["sig"]
                            if sg is None:
                                continue
                            if ops[d]["dma_key"] is None and ops[d]["eng"] == eng and not (self.sync_same and eng != "tensor"):
                                continue
                            sem, val, _ = sg
                            key = id(sem)
                            if seen.get(key, 0) >= val:
                                continue
                            e.wait_ge(sem, val)
                            nwaits[eng] += 1
                            seen[key] = val
                        for (xsem, xval) in o["ext"]:
                            if seen.get(id(xsem), 0) < xval:
                                e.wait_ge(xsem, xval)
                                seen[id(xsem)] = xval
                        inst = o["fn"](e)
                        if o["sig"] is not None:
                            inst.then_inc(o["sig"][0], o["sig"][2])
                    if eng == "sync":
                        for k, (sem, cnt) in dma_sems.items():
                            if k in nofinal:
                                continue
                            e.wait_ge(sem, cnt)
                getattr(block, eng)(body)
        self.nwaits = nwaits


def AP(t, off, pat):
    return bass.AP(t, off, [list(p) for p in pat])


GROUPS = [
    ("xa", None, F32, True),
    ("sga", AF.Silu, BF16, False),
    ("q", AF.Silu, BF16, False),
    ("sff", AF.Sigmoid, F32, True),
    ("sfb", AF.Sigmoid, F32, True),
    ("v", None, BF16, True),
    ("sgb", AF.Silu, BF16, False),
]


def build_program(debug=False, stop_after=99):
    nc = bass.Bass("TRN2", target_bir_lowering=False)
    I = {}

    def din(name, shape, dt=F32):
        I[name] = nc.dram_tensor(name, list(shape), dt, kind="ExternalInput")
        return I[name]

    x_d = din("x", [SEQ, D]); ctx_d = din("ctx", [CTX, D]); cvec_d = din("cvec", [2, D])
    adaw_d = din("ada_w", [D, 3 * D]); adab_d = din("ada_b", [1, 3 * D]); normw_d = din("norm_w", [1, D])
    win_d = din("w_in", [D, NCOL])
    convw_d = din("conv_w", [4, 512]); convb_d = din("conv_b", [1, 512])
    wr_d = din("lru_wr", [2, 4, 128, 128]); wi_d = din("lru_wi", [2, 4, 128, 128])
    br_d = din("lru_br", [2, 512]); bi_d = din("lru_bi", [2, 512]); lam_d = din("lru_lam", [2, 512])
    lbl_d = din("lb_logits", [2, 2, 512]); hnw_d = din("hnorm_w", [1, 128])
    wout_d = din("w_out", [2 * D, 512]); fnw_d = din("fnorm_w", [1, 512])
    xq_d = din("xq", [SEQ, 512]); adawg_d = din("ada_wg", [D, 512]); adabg_d = din("ada_bg", [1, 512])
    out_d = nc.dram_tensor("out", [SEQ, 512], F32, kind="ExternalOutput")

    skind = "ExternalOutput" if debug else "Internal"

    def dscr(name, shape, dt):
        return nc.dram_tensor(name, list(shape), dt, kind=skind)

    ada_s = dscr("ada_s", [2, 3 * D], F32)
    G_d = {}
    for (gname, _, gdt, needctx) in GROUPS:
        G_d[gname] = dscr("p_" + gname, [512, T if needctx else SEQ], gdt)
    ada_g = dscr("ada_g", [2, 512], F32)
    y_u = [dscr("y_u%d" % i, [128, SEQ], BF16) for i in range(8)]
    yg_u = [nc.dram_tensor("yg_u%d" % i, [512, SEQ], BF16) for i in range(8)]
    ss_loc = nc.dram_tensor("ss_loc", [128, 32], F32)
    ss_all = nc.dram_tensor("ss_all", [512, 32], F32)
    RG = [[0, 1, 2, 3], [4, 5, 6, 7]]

    with ExitStack() as top:
        def sbt(stack, name, shape, dt=F32):
            return stack.enter_context(nc.sbuf_tensor(name, list(shape), dt))

        def pst(stack, name, shape, dt=F32):
            return stack.enter_context(nc.psum_tensor(name, list(shape), dt))

        with ExitStack() as ph:
            S = Sched(nc, top, "a")
            w_sb = sbt(ph, "w_sb", [128, 16, NCOL], BF16)
            ident_f = sbt(ph, "ident_f", [128, 128]); ident = sbt(ph, "ident", [128, 128], BF16)
            ccol = sbt(ph, "ccol", [128, 2, 16]); scolT = sbt(ph, "scolT", [128, 16, 2])
            modc = sbt(ph, "modc", [128, 4, 16]); nwc = sbt(ph, "nwc", [128, 16])
            gsh = sbt(ph, "gsh", [128, 4, 16])
            ss = sbt(ph, "ss", [128, 40]); rstd = sbt(ph, "rstd", [128, 40])
            pacc = [pst(ph, "pacc%d" % i, [128, 512]) for i in range(4)]
            ptr = [pst(ph, "ptr%d" % i, [128, 1024], BF16) for i in range(4)]

            S.op("gpsimd", lambda e: e.memset(ident_f[:], 1.0), w=["identf"])
            S.op("gpsimd", lambda e: e.affine_select(out=ident_f[:], in_=ident_f[:], pattern=[[-1, 128]], compare_op=ALU.is_equal, fill=0.0, base=0, channel_multiplier=1), r=["identf"], w=["identf"])
            S.op("vector", lambda e: e.tensor_copy(out=ident[:], in_=ident_f[:]), r=["identf"], w=["ident"])
            for g in range(7):
                for kc in range(16):
                    S.op("gpsimd", lambda e, g=g, kc=kc: e.dma_start(out=w_sb[:, kc, g * 512:(g + 1) * 512], in_=win_d.ap()[kc * 128:(kc + 1) * 128, g * 512:(g + 1) * 512]),
                         w=[("w", g, kc)], dma_key=("w", g), bulk=True)
            for i in range(2):
                S.op("sync", lambda e, i=i: e.dma_start(out=ccol[:, i, :], in_=AP(cvec_d, i * D, [[1, 128], [128, 16]]), allow_slow_non_contiguous=True), w=[("ccol", i)], dma_key="cn", bulk=True)
            S.op("sync", lambda e: e.dma_start(out=nwc[:], in_=AP(normw_d, 0, [[1, 128], [128, 16]]), allow_slow_non_contiguous=True), w=["nwc"], dma_key="cn", bulk=True)
            for i in range(2):
                S.op("scalar", lambda e, i=i: e.activation(out=AP(scolT, i, [[32, 128], [2, 16]]), in_=ccol[:, i, :], func=AF.Silu), r=[("ccol", i)], w=[("scolT", i)])
            with ExitStack() as p0:
                aslot = [sbt(p0, "aslot%d" % i, [128, 2048]) for i in range(3)]
                rows = sbt(p0, "rows", [2, 3 * D]); adab2 = sbt(p0, "adab2", [2, 3 * D])
                S.op("sync", lambda e: e.dma_start(out=adab2[:], in_=AP(adab_d, 0, [[0, 2], [1, 3 * D]])), w=["adab2"], dma_key="adab2")
                li = 0
                for sec in range(2):
                    for kc in range(16):
                        sl = li % 3
                        S.op("sync", lambda e, sec=sec, kc=kc, sl=sl: e.dma_start(out=aslot[sl][:], in_=adaw_d.ap()[kc * 128:(kc + 1) * 128, sec * D:(sec + 1) * D]), w=[("aslot", sl)], dma_key=("aslot", sl))
                        for nb in range(4):
                            S.op("tensor", lambda e, kc=kc, sl=sl, nb=nb: e.matmul(pacc[nb][0:2, :], lhsT=scolT[:, kc, :], rhs=aslot[sl][:, nb * 512:(nb + 1) * 512], start=(kc == 0), stop=(kc == 15)),
                                 r=[("aslot", sl), ("scolT", 0), ("scolT", 1)], w=[("pacc", nb)])
                        li += 1
                    for nb in range(4):
                        c0 = sec * D + nb * 512
                        S.op("vector", lambda e, nb=nb, c0=c0: e.tensor_tensor(out=rows[0:2, c0:c0 + 512], in0=pacc[nb][0:2, :], in1=adab2[0:2, c0:c0 + 512], op=ALU.add),
                             r=[("pacc", nb), "adab2"], w=["rows"])
                for kc in range(16):
                    sl = li % 3
                    S.op("sync", lambda e, kc=kc, sl=sl: e.dma_start(out=aslot[sl][:, 0:512], in_=adawg_d.ap()[kc * 128:(kc + 1) * 128, :]), w=[("aslot", sl)], dma_key=("aslot", sl))
                    S.op("tensor", lambda e, kc=kc, sl=sl: e.matmul(pacc[0][0:2, :], lhsT=scolT[:, kc, :], rhs=aslot[sl][:, 0:512], start=(kc == 0), stop=(kc == 15)),
                         r=[("aslot", sl), ("scolT", 0), ("scolT", 1)], w=[("pacc", 0)])
                    li += 1
                rowsg = sbt(p0, "rowsg", [2, 512]); adabg2 = sbt(p0, "adabg2", [2, 512])
                S.op("sync", lambda e: e.dma_start(out=adabg2[:], in_=AP(adabg_d, 0, [[0, 2], [1, 512]])), w=["adabg2"], dma_key="adabg2")
                S.op("vector", lambda e: e.tensor_tensor(out=rowsg[0:2, :], in0=pacc[0][0:2, :], in1=adabg2[0:2, :], op=ALU.add), r=[("pacc", 0), "adabg2"], w=["rowsg"])
                S.op("sync", lambda e: e.dma_start(out=ada_g.ap(), in_=rowsg[:]), r=["rowsg"], w=["ada_g"], dma_key="ada_g")
                S.op("sync", lambda e: e.dma_start(out=ada_s.ap()[:, 0:2 * D], in_=rows[:, 0:2 * D]), r=["rows"], w=["ada_s"], dma_key="ada_s")
                for i, (row, sec) in enumerate([(0, 0), (0, 1), (1, 0), (1, 1)]):
                    S.op("sync", lambda e, i=i, row=row, sec=sec: e.dma_start(out=modc[:, i, :], in_=AP(ada_s, row * 3 * D + sec * D, [[1, 128], [128, 16]]), allow_slow_non_contiguous=True),
                         r=["ada_s"], w=[("modc", i)], dma_key="modc", bulk=True)
                for tgt, (sh_i, sc_i) in enumerate([(0, 1), (2, 3)]):
                    S.op("vector", lambda e, tgt=tgt, sc_i=sc_i: e.scalar_tensor_tensor(out=gsh[:, 2 * tgt, :], in0=modc[:, sc_i, :], scalar=1.0, in1=nwc[:], op0=ALU.add, op1=ALU.mult),
                         r=[("modc", sc_i), "nwc"], w=[("gsh", 2 * tgt)])
                    S.op("vector", lambda e, tgt=tgt, sh_i=sh_i: e.tensor_copy(out=gsh[:, 2 * tgt + 1, :], in_=modc[:, sh_i, :]), r=[("modc", sh_i)], w=[("gsh", 2 * tgt + 1)])

            S.fence()
            with ExitStack() as p1:
                xt = [sbt(p1, "xt%d" % i, [128, D]) for i in range(2)]
                xn = [sbt(p1, "xn%d" % i, [128, D], BF16) for i in range(2)]
                hT = [sbt(p1, "hT%d" % i, [128, 16, 512], BF16) for i in range(2)]
                stgF = [sbt(p1, "stgF%d" % i, [128, 4, 512]) for i in range(2)]
                stgB = [sbt(p1, "stgB%d" % i, [128, 4, 512], BF16) for i in range(2)]
                mtmp = [sbt(p1, "mtmp%d" % i, [128, 8, 128]) for i in range(2)]
                mcount = [0]
                scountB = [0]
                mstmp = sbt(p1, "mstmp", [128, 40])
                blocks = [(0, 2, True)] + [(2 + 4 * i, 4, False) for i in range(8)]
                tcount = [0]
                scount = [0]

                pstate = {}

                def prepA(bi, tl):
                    t0, nt, isctx = blocks[bi]
                    tt = t0 + tl
                    xs = xt[tcount[0] % 2]; xk = ("xt", tcount[0] % 2)
                    xb = xn[tcount[0] % 2]; xnk = ("xn", tcount[0] % 2)
                    pk = tcount[0] % 2
                    tcount[0] += 1
                    pstate[(bi, tl)] = (xb, xnk, pk)
                    src = ctx_d.ap()[tt * 128:(tt + 1) * 128, :] if isctx else x_d.ap()[(tt - 2) * 128:(tt - 1) * 128, :]
                    S.op("sync", lambda e, xs=xs, src=src: e.dma_start(out=xs[:], in_=src), w=[xk], dma_key=xk)
                    S.op("scalar", lambda e, xs=xs, xb=xb, tt=tt: e.activation(out=xb[:], in_=xs[:], func=AF.Square, accum_out=ss[:, tt:tt + 1]), r=[xk], w=[xnk, ("ss", tt)])
                    S.op("vector", lambda e, tt=tt: e.tensor_scalar(out=mstmp[:, tt:tt + 1], in0=ss[:, tt:tt + 1], scalar1=1.0 / D, scalar2=EPS, op0=ALU.mult, op1=ALU.add), r=[("ss", tt)], w=[("ms", tt)])
                    S.op("scalar", lambda e, tt=tt: e.activation(out=mstmp[:, tt:tt + 1], in_=mstmp[:, tt:tt + 1], func=AF.Ln), r=[("ms", tt)], w=[("ms", tt)])
                    S.op("scalar", lambda e, tt=tt: e.activation(out=rstd[:, tt:tt + 1], in_=mstmp[:, tt:tt + 1], func=AF.Exp, scale=-0.5), r=[("ms", tt)], w=[("rstd", tt)])
                    S.op("scalar", lambda e, xs=xs, xb=xb, tt=tt: e.activation(out=xb[:], in_=xs[:], func=AF.Identity, scale=rstd[:, tt:tt + 1]), r=[xk, ("rstd", tt)], w=[xnk])

                def prepB(bi, tl):
                    t0, nt, isctx = blocks[bi]
                    hb = hT[bi % 2]
                    gi = 2 if isctx else 0
                    xb, xnk, pk = pstate[(bi, tl)]
                    for half in range(2):
                        pt_ = ptr[pk * 2 + half]; ptk = ("ptr", pk * 2 + half)
                        for kl in range(8):
                            kc = half * 8 + kl
                            S.op("tensor", lambda e, pt_=pt_, xb=xb, kl=kl, kc=kc: e.transpose(out=pt_[:, kl * 128:(kl + 1) * 128], in_=xb[:, kc * 128:(kc + 1) * 128], identity=ident[:]),
                                 r=[xnk, "ident"], w=[ptk])
                        hk = ("hT", bi % 2)
                        dst = AP(hb, half * 8 * 512 + tl * 128, [[16 * 512, 128], [512, 8], [1, 128]])
                        srcp = AP(pt_, 0, [[1024, 128], [128, 8], [1, 128]])
                        gb_ = AP(gsh, gi * 16 + half * 8, [[64, 128], [1, 8], [0, 128]])
                        sb_ = AP(gsh, (gi + 1) * 16 + half * 8, [[64, 128], [1, 8], [0, 128]])
                        mt = mtmp[mcount[0] % 2]; mk = ("mtmp", mcount[0] % 2)
                        mcount[0] += 1
                        S.op("vector", lambda e, mt=mt, srcp=srcp, gb_=gb_: e.tensor_tensor(out=mt[:], in0=srcp, in1=gb_, op=ALU.mult), r=[ptk, ("gsh", gi)], w=[mk])
                        S.op("gpsimd", lambda e, dst=dst, mt=mt, sb_=sb_: e.tensor_tensor(out=dst, in0=mt[:], in1=sb_, op=ALU.add), r=[mk, ("gsh", gi + 1)], w=[hk])

                def proj(bi, hooks=()):
                    hooks = list(hooks)
                    t0, nt, isctx = blocks[bi]
                    ntok = nt * 128
                    hb = hT[bi % 2]; hk = ("hT", bi % 2)
                    for g, (gname, func, gdt, needctx) in enumerate(GROUPS):
                        if isctx and not needctx:
                            continue
                        if gdt == F32:
                            si = scount[0] % 2
                            scount[0] += 1
                            sk = ("stgF", si)
                            st_ = stgF[si]
                        else:
                            si = scountB[0] % 2
                            scountB[0] += 1
                            sk = ("stgB", si)
                            st_ = stgB[si]
                        for ml in range(4):
                            m = g * 4 + ml
                            pa = pacc[m % 4]; pak = ("pacc", m % 4)
                            for kc in range(16):
                                S.op("tensor", lambda e, pa=pa, kc=kc, m=m, hb=hb, ntok=ntok: e.matmul(pa[:, 0:ntok], lhsT=w_sb[:, kc, m * 128:(m + 1) * 128], rhs=hb[:, kc, 0:ntok], start=(kc == 0), stop=(kc == 15)),
                                     r=[hk, ("w", g, kc)], w=[pak])
                            dst = st_[:, ml, 0:ntok]
                            if func is None:
                                S.op("vector", lambda e, dst=dst, pa=pa, ntok=ntok: e.tensor_copy(out=dst, in_=pa[:, 0:ntok]), r=[pak], w=[sk])
                            else:
                                S.op("scalar", lambda e, dst=dst, pa=pa, ntok=ntok, func=func: e.activation(out=dst, in_=pa[:, 0:ntok], func=func), r=[pak], w=[sk])
                        Tg = T if needctx else SEQ
                        toff = t0 * 128 if needctx else (t0 - 2) * 128
                        srcs = st_[:, :, 0:ntok]
                        dstd = AP(G_d[gname], toff, [[Tg, 128], [128 * Tg, 4], [1, ntok]])
                        S.op("gpsimd", lambda e, dstd=dstd, srcs=srcs: e.dma_start(out=dstd, in_=srcs), r=[sk], w=[("G", gname, bi)], dma_key=sk)
                        if hooks:
                            for fn_ in hooks.pop(0):
                                fn_()
                    while hooks:
                        for fn_ in hooks.pop(0):
                            fn_()

                for tl in range(blocks[0][1]):
                    prepA(0, tl)
                    prepB(0, tl)
                for bi in range(len(blocks)):
                    hooks = []
                    if bi + 1 < len(blocks):
                        nt1 = blocks[bi + 1][1]
                        hooks.append([lambda b=bi + 1: prepA(b, 0)])
                        for tl in range(1, nt1):
                            hooks.append([lambda b=bi + 1, t=tl: prepB(b, t - 1), lambda b=bi + 1, t=tl: prepA(b, t)])
                        hooks.append([lambda b=bi + 1, t=nt1: prepB(b, t - 1)])
                    proj(bi, hooks)
            S.emit()
            print("phase a waits", S.nwaits, "ops", len(S.ops))
        if stop_after <= 1:
            return nc

        with ExitStack() as ph:
            S = Sched(nc, top, "b")
            NP = T + 6
            xpad = sbt(ph, "xpad", [128, NP]); uu = [sbt(ph, "u%d" % i, [128, T]) for i in range(2)]; ubfs = [sbt(ph, "ubf%d" % i, [128, T], BF16) for i in range(2)]
            Rb = [sbt(ph, "Rb%d" % i, [128, T]) for i in range(2)]; Ib = [sbt(ph, "Ib%d" % i, [128, T]) for i in range(2)]; Zb = [sbt(ph, "Zb%d" % i, [128, T]) for i in range(2)]
            sga = [sbt(ph, "sga%d" % i, [128, SEQ], BF16) for i in range(2)]; yA = sbt(ph, "yA", [128, SEQ], BF16)
            cw = sbt(ph, "cw", [128, 4, 4]); cb = sbt(ph, "cb", [128, 4]); onesA = sbt(ph, "onesA", [128, 1])
            S.op("vector", lambda e: e.memset(onesA[:], 1.0), w=["onesA"])
            brt = sbt(ph, "brt", [128, 2, 4]); bit = sbt(ph, "bit", [128, 2, 4]); lamt = sbt(ph, "lamt", [128, 2, 4]); c1 = sbt(ph, "c1", [128, 2, 4])
            wr_sb = sbt(ph, "wr_sb", [128, 8, 128], BF16); wi_sb = sbt(ph, "wi_sb", [128, 8, 128], BF16)
            pg = [pst(ph, "pg%d" % i, [128, 512]) for i in range(4)]
            for k in range(4):
                S.op("sync", lambda e, k=k: e.dma_start(out=cw[:, :, k], in_=AP(convw_d, k * 512, [[1, 128], [128, 4]]), allow_slow_non_contiguous=True), w=[("cw", k)], dma_key="par", bulk=True)
            S.op("sync", lambda e: e.dma_start(out=cb[:], in_=AP(convb_d, 0, [[1, 128], [128, 4]]), allow_slow_non_contiguous=True), w=["cb"], dma_key="par", bulk=True)
            for d in range(2):
                for nm, tl_, dd in (("brt", brt, br_d), ("bit", bit, bi_d), ("lamt", lamt, lam_d)):
                    S.op("sync", lambda e, d=d, tl_=tl_, dd=dd: e.dma_start(out=tl_[:, d, :], in_=AP(dd, d * 512, [[1, 128], [128, 4]]), allow_slow_non_contiguous=True), w=[(nm, d)], dma_key="par", bulk=True)
            S.op("gpsimd", lambda e: e.dma_start(out=wr_sb[:], in_=AP(wr_d, 0, [[128, 128], [16384, 8], [1, 128]])), w=["wr"], dma_key="wr")
            S.op("gpsimd", lambda e: e.dma_start(out=wi_sb[:], in_=AP(wi_d, 0, [[128, 128], [16384, 8], [1, 128]])), w=["wi"], dma_key="wi")
            S.op("scalar", lambda e: e.activation(out=c1[:], in_=lamt[:], func=AF.Exp, scale=-1.0), r=[("lamt", 0), ("lamt", 1)], w=["c1"])
            S.op("vector", lambda e: e.tensor_scalar(out=c1[:], in0=c1[:], scalar1=1.0, scalar2=None, op0=ALU.add), r=["c1"], w=["c1"])
            S.op("scalar", lambda e: e.activation(out=c1[:], in_=c1[:], func=AF.Ln), r=["c1"], w=["c1"])
            S.op("vector", lambda e: e.tensor_scalar(out=c1[:], in0=c1[:], scalar1=-8.0, scalar2=None, op0=ALU.mult), r=["c1"], w=["c1"])
            for (a0, a1) in ((0, 2), (258, 261), (4357, 4358)):
                S.op("gpsimd", lambda e, a0=a0, a1=a1: e.memset(xpad[:, a0:a1], 0.0), w=[("pad", a0)])
            def loadA(n):
                S.op("sync", lambda e, n=n: e.dma_start(out=xpad[:, 2:258], in_=G_d["xa"].ap()[n * 128:(n + 1) * 128, 0:256]), w=["xc"], dma_key="xc")
                S.op("sync", lambda e, n=n: e.dma_start(out=xpad[:, 261:4357], in_=G_d["xa"].ap()[n * 128:(n + 1) * 128, 256:T]), w=["xl"], dma_key="xl")

            def loadS(n):
                S.op("sync", lambda e, n=n: e.dma_start(out=sga[n % 2][:], in_=G_d["sga"].ap()[n * 128:(n + 1) * 128, :]), w=[("sga", n % 2)], dma_key=("sga", n % 2))
            loadA(0)
            loadS(0)
            loadS(1)

            def convA(n):
                u = uu[n % 2]; ubf = ubfs[n % 2]
                padk = [("pad", 0), ("pad", 258), ("pad", 4357)]
                for (uo, xo, ln, xk) in ((0, 0, 256, "xc"), (256, 259, SEQ, "xl")):
                    S.op("vector", lambda e, uo=uo, xo=xo, ln=ln, n=n, u=u: e.tensor_scalar(out=u[:, uo:uo + ln], in0=xpad[:, xo:xo + ln], scalar1=cw[:, n, 0:1], scalar2=cb[:, n:n + 1], op0=ALU.mult, op1=ALU.add),
                         r=[xk, ("cw", 0), "cb"] + padk, w=[("u", n % 2, uo)])
                    for k in range(1, 4):
                        S.op("vector", lambda e, uo=uo, xo=xo, ln=ln, n=n, k=k, u=u: e.scalar_tensor_tensor(out=u[:, uo:uo + ln], in0=xpad[:, xo + k:xo + k + ln], scalar=cw[:, n, k:k + 1], in1=u[:, uo:uo + ln], op0=ALU.mult, op1=ALU.add),
                             r=[xk, ("cw", k)] + padk, w=[("u", n % 2, uo)])
                if n + 1 < 4:
                    loadA(n + 1)
                S.op("gpsimd", lambda e, u=u, ubf=ubf: e.tensor_copy(out=ubf[:], in_=u[:]), r=[("u", n % 2, 0), ("u", n % 2, 256)], w=[("ubf", n % 2)])
            convA(0)
            for n in range(4):
                u = uu[n % 2]; ubf = ubfs[n % 2]
                ubk = ("ubf", n % 2)
                for blk in range(9):
                    b0 = blk * 512; bn = min(512, T - b0)
                    for d in range(2):
                        pr = pg[d * 2]; pi = pg[d * 2 + 1]
                        S.op("tensor", lambda e, pr=pr, d=d, n=n, b0=b0, bn=bn, ubf=ubf: e.matmul(pr[:, 0:bn], lhsT=wr_sb[:, d * 4 + n, :], rhs=ubf[:, b0:b0 + bn], start=True, stop=True), r=[ubk, "wr"], w=[("pg", d * 2)])
                        S.op("tensor", lambda e, pi=pi, d=d, n=n, b0=b0, bn=bn, ubf=ubf: e.matmul(pi[:, 0:bn], lhsT=wi_sb[:, d * 4 + n, :], rhs=ubf[:, b0:b0 + bn], start=True, stop=True), r=[ubk, "wi"], w=[("pg", d * 2 + 1)])
                        S.op("scalar", lambda e, pr=pr, d=d, n=n, b0=b0, bn=bn: e.activation(out=Rb[d][:, b0:b0 + bn], in_=pr[:, 0:bn], func=AF.Sigmoid, bias=brt[:, d, n:n + 1]), r=[("pg", d * 2), ("brt", d)], w=[("Rb", d)])
                        S.op("scalar", lambda e, pi=pi, d=d, n=n, b0=b0, bn=bn: e.activation(out=Ib[d][:, b0:b0 + bn], in_=pi[:, 0:bn], func=AF.Sigmoid, bias=bit[:, d, n:n + 1]), r=[("pg", d * 2 + 1), ("bit", d)], w=[("Ib", d)])
                if n + 1 < 4:
                    convA(n + 1)
                for d in range(2):
                    S.op("scalar", lambda e, d=d, n=n: e.activation(out=Rb[d][:], in_=Rb[d][:], func=AF.Exp, scale=c1[:, d, n:n + 1]), r=[("Rb", d), "c1"], w=[("Rb", d)])
                for d in range(2):
                    S.op("gpsimd", lambda e, d=d, u=u: e.tensor_tensor(out=Ib[d][:], in0=Ib[d][:], in1=u[:], op=ALU.mult), r=[("Ib", d), ("u", n % 2, 0), ("u", n % 2, 256)], w=[("Ib", d)])
                for d in range(2):
                    S.op("scalar", lambda e, d=d: e.activation(out=Zb[d][:], in_=Rb[d][:], func=AF.Square), r=[("Rb", d)], w=[("Zb", d)])
                for d in range(2):
                    S.op("scalar", lambda e, d=d: e.activation(out=Zb[d][:], in_=Zb[d][:], func=AF.Ln, scale=-1.0, bias=onesA[:, 0:1]), r=[("Zb", d), "onesA"], w=[("Zb", d)])
                for d in range(2):
                    S.op("scalar", lambda e, d=d: e.activation(out=Zb[d][:], in_=Zb[d][:], func=AF.Exp, scale=0.5), r=[("Zb", d)], w=[("Zb", d)])
                for d in range(2):
                    S.op("vector", lambda e, d=d: e.tensor_tensor(out=Ib[d][:], in0=Ib[d][:], in1=Zb[d][:], op=ALU.mult), r=[("Ib", d), ("Zb", d)], w=[("Ib", d)])
                S.op("vector", lambda e: e.tensor_tensor_scan(out=Zb[0][:], data0=Rb[0][:], data1=Ib[0][:], initial=0.0, op0=ALU.mult, op1=ALU.add), r=[("Rb", 0), ("Ib", 0), ("Zb", 0)], w=[("Zb", 0)])
                rv = lambda t_, a0, ln: AP(t_, a0 + ln - 1, [[T, 128], [-1, ln]])
                S.op("vector", lambda e: e.tensor_tensor_scan(out=rv(Zb[1], 0, 256), data0=rv(Rb[1], 0, 256), data1=rv(Ib[1], 0, 256), initial=0.0, op0=ALU.mult, op1=ALU.add), r=[("Rb", 1), ("Ib", 1), ("Zb", 1)], w=[("Zb", 1)])
                S.op("vector", lambda e: e.tensor_tensor_scan(out=rv(Zb[1], 256, SEQ), data0=rv(Rb[1], 256, SEQ), data1=rv(Ib[1], 256, SEQ), initial=Zb[1][:, 0:1], op0=ALU.mult, op1=ALU.add), r=[("Rb", 1), ("Ib", 1), ("Zb", 1)], w=[("Zb", 1)])
                S.op("gpsimd", lambda e: e.tensor_tensor(out=Zb[0][:, 256:T], in0=Zb[0][:, 256:T], in1=Zb[1][:, 256:T], op=ALU.add), r=[("Zb", 0), ("Zb", 1)], w=[("Zb", 0)])
                S.op("vector", lambda e, n=n: e.tensor_tensor(out=yA[:], in0=Zb[0][:, 256:T], in1=sga[n % 2][:], op=ALU.mult), r=[("Zb", 0), ("sga", n % 2)], w=["yA"])
                if n + 2 < 4:
                    loadS(n + 2)
                S.op("sync", lambda e, n=n: e.dma_start(out=y_u[n].ap(), in_=yA[:]), r=["yA"], w=[("y_u", n)], dma_key="yAst")
                if not debug:
                    S.op("gpsimd", lambda e, n=n: e.collective_compute("AllGather", ALU.bypass, replica_groups=RG, ins=[y_u[n].ap().opt()], outs=[yg_u[n].ap().opt()]), r=[("y_u", n)], w=[("yg_u", n)], dma_key=("ag", n), inc=1)
            S.emit(nofinal=[("ag", 3)])
            agsig = {}
            for n_ in range(4):
                if ("ag", n_) in S.dma_sems:
                    agsig[n_] = tuple(S.dma_sems[("ag", n_)])
            print("phase b waits", S.nwaits, "ops", len(S.ops))
        if stop_after <= 2:
            return nc

        with ExitStack() as ph:
            S = Sched(nc, top, "c")
            SF = sbt(ph, "SF", [128, T]); qb = sbt(ph, "qb", [128, SEQ], BF16); vb = sbt(ph, "vb", [128, T], BF16); vcm = sbt(ph, "vcm", [128, T], BF16)
            Vtok = sbt(ph, "Vtok", [64, NCHK, 128], BF16); Ktok = sbt(ph, "Ktok", [64, NCHK, 128], BF16)
            W = [sbt(ph, "W%d" % i, [128, T]) for i in range(3)]
            Qt = [sbt(ph, "Qt%d" % i, [128, SEQ], BF16) for i in range(2)]; Kt = [sbt(ph, "Kt%d" % i, [128, T], BF16) for i in range(2)]
            AT = sbt(ph, "AT", [64, SEQ], BF16); O = sbt(ph, "O", [128, SEQ])
            Mx = sbt(ph, "Mx", [128, T + 1], BF16)
            sgbb = sbt(ph, "sgbb", [128, SEQ], BF16)
            Tst = [sbt(ph, "Tst%d" % i, [128, 128]) for i in range(4)]
            Sbf = [sbt(ph, "Sbf%d" % i, [128, 128], BF16) for i in range(4)]
            gam = [sbt(ph, "gam%d" % i, [128, NCHK]) for i in range(2)]
            lbl = sbt(ph, "lbl", [128, 4, 4]); lb = sbt(ph, "lb", [128, 2, 4]); oml = sbt(ph, "oml", [128, 2, 4])
            hnw = sbt(ph, "hnw", [128, 1]); epsT = sbt(ph, "epsT", [128, 1])
            mF = sbt(ph, "mF", [64, 64]); mB = sbt(ph, "mB", [64, 64]); ones_bf = sbt(ph, "ones_bf", [128, 128], BF16)
            identb_f = sbt(ph, "identb_f", [128, 128]); identb = sbt(ph, "identb", [128, 128], BF16)
            pT = [pst(ph, "pT%d" % i, [128, 1024], BF16) for i in range(2)]
            pA = [pst(ph, "pA%d" % i, [128, 512]) for i in range(2)]
            pO = [pst(ph, "pO%d" % i, [128, 512]) for i in range(2)]
            pS = [pst(ph, "pS%d" % i, [128, 512]) for i in range(2)]
            S.op("gpsimd", lambda e: e.memset(identb_f[:], 1.0), w=["identf"])
            S.op("gpsimd", lambda e: e.affine_select(out=identb_f[:], in_=identb_f[:], pattern=[[-1, 128]], compare_op=ALU.is_equal, fill=0.0, base=0, channel_multiplier=1), r=["identf"], w=["identf"])
            S.op("vector", lambda e: e.tensor_copy(out=identb[:], in_=identb_f[:]), r=["identf"], w=["ident"])
            S.op("gpsimd", lambda e: e.memset(mF[:], 1.0), w=["mF"])
            S.op("gpsimd", lambda e: e.affine_select(out=mF[:], in_=mF[:], pattern=[[1, 64]], compare_op=ALU.is_ge, fill=0.0, base=0, channel_multiplier=-1), r=["mF"], w=["mF"])
            S.op("gpsimd", lambda e: e.memset(mB[:], 1.0), w=["mB"])
            S.op("gpsimd", lambda e: e.affine_select(out=mB[:], in_=mB[:], pattern=[[-1, 64]], compare_op=ALU.is_ge, fill=0.0, base=0, channel_multiplier=1), r=["mB"], w=["mB"])
            S.op("vector", lambda e: e.memset(ones_bf[:], 1.0), w=["ones"])
            S.op("vector", lambda e: e.memset(epsT[:], EPS), w=["eps"])
            S.op("vector", lambda e: e.memset(Mx[:], 1.0), w=["Mx"])
            S.op("vector", lambda e: e.memset(AP(Mx, 0, [[T + 1, 128], [64, NCHK + 1]]), 0.0), r=["Mx"], w=["Mx"])
            for d in range(2):
                for l in range(2):
                    S.op("sync", lambda e, d=d, l=l: e.dma_start(out=lbl[:, d * 2 + l, :], in_=AP(lbl_d, (d * 2 + l) * 512, [[1, 128], [128, 4]]), allow_slow_non_contiguous=True), w=[("lbl", d * 2 + l)], dma_key="par", bulk=True)
            S.op("sync", lambda e: e.dma_start(out=hnw[:], in_=AP(hnw_d, 0, [[1, 128], [1, 1]])), w=["hnw"], dma_key="par", bulk=True)
            for d in range(2):
                S.op("vector", lambda e, d=d: e.tensor_tensor(out=lb[:, d, :], in0=lbl[:, 2 * d, :], in1=lbl[:, 2 * d + 1, :], op=ALU.subtract), r=[("lbl", 2 * d), ("lbl", 2 * d + 1)], w=[("lb", d)])
            S.op("scalar", lambda e: e.activation(out=lb[:], in_=lb[:], func=AF.Sigmoid), r=[("lb", 0), ("lb", 1)], w=["lbs"])
            S.op("vector", lambda e: e.tensor_scalar(out=oml[:], in0=lb[:], scalar1=-1.0, scalar2=1.0, op0=ALU.mult, op1=ALU.add), r=["lbs"], w=["oml"])

            def perm_in(t_, tw):
                return AP(t_, tw - SEQ, [[tw, 128], [1, 64], [64, 64]])

            def nat3(t_, tw, off):
                return AP(t_, off, [[tw, 128], [64, 64], [1, 64]])

            tctr = [0]

            def transposes(src, dstT, srck, dstk):
                for g8 in range(9):
                    n8 = min(8, NCHK - g8 * 8)
                    pt_ = pT[tctr[0] % 2]; ptk = ("pT", tctr[0] % 2)
                    tctr[0] += 1
                    for i8 in range(n8):
                        ck = g8 * 8 + i8
                        S.op("tensor", lambda e, pt_=pt_, i8=i8, ck=ck, src=src: e.transpose(out=pt_[0:64, i8 * 128:(i8 + 1) * 128], in_=src[:, ck * 64:(ck + 1) * 64], identity=identb[:]), r=[srck, "ident"], w=[ptk])
                    if g8 % 2 == 0:
                        S.op("scalar", lambda e, pt_=pt_, g8=g8, n8=n8, dstT=dstT: e.activation(out=dstT[:, g8 * 8:g8 * 8 + n8, :], in_=pt_[0:64, 0:n8 * 128], func=AF.Identity), r=[ptk], w=[dstk])
                    else:
                        S.op("vector", lambda e, pt_=pt_, g8=g8, n8=n8, dstT=dstT: e.tensor_copy(out=dstT[:, g8 * 8:g8 * 8 + n8, :], in_=pt_[0:64, 0:n8 * 128]), r=[ptk], w=[dstk])

            def vcopy(hd):
                r0 = hd * 128
                S.op("gpsimd", lambda e: e.tensor_copy(out=vcm[:, 0:256], in_=vb[:, 0:256]), r=["vb"], w=["vcm"])
                S.op("gpsimd", lambda e: e.tensor_copy(out=nat3(vcm, T, 256), in_=perm_in(vb, T)), r=["vb"], w=["vcm"])
                if hd + 1 < 4:
                    S.op("sync", lambda e, r0=r0: e.dma_start(out=vb[:], in_=G_d["v"].ap()[r0 + 128:r0 + 256, :]), w=["vb"], dma_key="vb")

            def vstage(hd):
                r0 = hd * 128
                S.op("sync", lambda e, r0=r0: e.dma_start(out=sgbb[:], in_=G_d["sgb"].ap()[r0:r0 + 128, :]), w=["sgbb"], dma_key="sgbb")
                if hd == 0:
                    vcopy(0)
                transposes(vcm, Vtok, "vcm", "Vtok")

            def setup_closures(k):
                hd, d = divmod(k, 2)
                r0 = hd * 128
                b = k % 2
                Qb = Qt[b]; Kb = Kt[b]; gb = gam[b]
                Qk = ("Qt", b); Kk = ("Kt", b); gk = ("gam", b)
                lbc = lb[:, d, hd:hd + 1]; omc = oml[:, d, hd:hd + 1]
                cl = []
                cl.append(lambda: S.op("vector", lambda e: e.tensor_scalar(out=W[0][:, 0:256], in0=SF[:, 0:256], scalar1=omc, scalar2=lbc, op0=ALU.mult, op1=ALU.add), r=["SF", "lbs", "oml"], w=["W0"]))
                cl.append(lambda: S.op("vector", lambda e: e.tensor_scalar(out=nat3(W[0], T, 256), in0=perm_in(SF, T), scalar1=omc, scalar2=lbc, op0=ALU.mult, op1=ALU.add), r=["SF", "lbs", "oml"], w=["W0"]))
                if d == 0:
                    cl.append(lambda: S.op("sync", lambda e: e.dma_start(out=SF[:], in_=G_d["sfb"].ap()[r0:r0 + 128, :]), w=["SF"], dma_key="SF"))
                elif hd + 1 < 4:
                    cl.append(lambda: S.op("sync", lambda e: e.dma_start(out=SF[:], in_=G_d["sff"].ap()[r0 + 128:r0 + 256, :]), w=["SF"], dma_key="SF"))
                cl.append(lambda: S.op("gpsimd", lambda e: e.tensor_scalar(out=W[1][:], in0=W[0][:], scalar1=-1.0, scalar2=1.0, op0=ALU.mult, op1=ALU.add), r=["W0"], w=["W1"]))
                cl.append(lambda: S.op("scalar", lambda e: e.activation(out=W[0][:], in_=W[0][:], func=AF.Ln), r=["W0", "W1"], w=["W0"]))
                if d == 0:
                    cl.append(lambda: S.op("vector", lambda e: e.tensor_tensor_scan(out=W[2][:], data0=Mx[:, 0:T], data1=W[0][:], initial=0.0, op0=ALU.mult, op1=ALU.add), r=["W0", "Mx"], w=["W2"]))
                else:
                    cl.append(lambda: S.op("vector", lambda e: e.tensor_tensor_scan(out=AP(W[2], T - 1, [[T, 128], [-1, T]]), data0=AP(Mx, T, [[T + 1, 128], [-1, T]]), data1=AP(W[0], T - 1, [[T, 128], [-1, T]]), initial=0.0, op0=ALU.mult, op1=ALU.add), r=["W0", "Mx"], w=["W2"]))
                cl.append(lambda: S.op("gpsimd", lambda e: e.tensor_scalar(out=W[2][:], in0=W[2][:], scalar1=0.0, scalar2=-80.0, op0=ALU.min, op1=ALU.max), r=["W2"], w=["W2"]))
                cl.append(lambda: S.op("scalar", lambda e: e.activation(out=W[0][:], in_=W[2][:], func=AF.Exp), r=["W2"], w=["W0"]))
                cl.append(lambda: S.op("scalar", lambda e: e.activation(out=W[2][:], in_=W[2][:], func=AF.Exp, scale=-1.0), r=["W2", "W0"], w=["W2"]))
                ge = 63 if d == 0 else 0
                cl.append(lambda: S.op("gpsimd", lambda e: e.tensor_copy(out=gb[:], in_=AP(W[0], ge, [[T, 128], [64, NCHK]])), r=["W0"], w=[gk]))
                cl.append(lambda: S.op("vector", lambda e: e.tensor_tensor(out=nat3(Qb, SEQ, 0), in0=perm_in(qb, SEQ), in1=nat3(W[0], T, 256), op=ALU.mult), r=["qb", "W0"], w=[Qk]))
                if d == 1 and hd + 1 < 4:
                    cl.append(lambda: S.op("sync", lambda e: e.dma_start(out=qb[:], in_=G_d["q"].ap()[r0 + 128:r0 + 256, :]), w=["qb"], dma_key="qb"))
                cl.append(lambda: S.op("gpsimd", lambda e: e.tensor_tensor(out=Kb[:], in0=W[1][:], in1=W[2][:], op=ALU.mult), r=["W1", "W2"], w=[Kk]))
                return cl

            def pestage(k):
                hd, d = divmod(k, 2)
                b = k % 2
                Qb = Qt[b]; Kb = Kt[b]; Qk = ("Qt", b); Kk = ("Kt", b)
                transposes(Kb, Ktok, Kk, "Ktok")
                msk = mF if d == 0 else mB
                mk = "mF" if d == 0 else "mB"
                for g8 in range(8):
                    pa_ = pA[g8 % 2]; pak = ("pA", g8 % 2)
                    for i8 in range(8):
                        lc = g8 * 8 + i8
                        S.op("tensor", lambda e, pa_=pa_, i8=i8, lc=lc: e.matmul(pa_[0:64, i8 * 64:(i8 + 1) * 64], lhsT=Kb[:, (4 + lc) * 64:(5 + lc) * 64], rhs=Qb[:, lc * 64:(lc + 1) * 64], start=True, stop=True), r=[Kk, Qk], w=[pak])
                    S.op("vector", lambda e, pa_=pa_, g8=g8, msk=msk: e.tensor_tensor(out=AP(AT, g8 * 512, [[SEQ, 64], [64, 8], [1, 64]]), in0=AP(pa_, 0, [[512, 64], [64, 8], [1, 64]]), in1=AP(msk, 0, [[64, 64], [0, 8], [1, 64]]), op=ALU.mult), r=[pak, mk], w=["AT"])

            def recurrence(k, extra):
                hd, d = divmod(k, 2)
                b = k % 2
                Qb = Qt[b]; Qk = ("Qt", b); gb = gam[b]; gk = ("gam", b)
                order = [0, 1, 2, 3] + list(range(4, NCHK)) if d == 0 else [3, 2, 1, 0] + list(range(NCHK - 1, 3, -1))
                LA = 3
                nst = len(order)
                ring = [(pS[0], ("pS", 0)), (pS[1], ("pS", 1)), (pA[0], ("pA", 0)), (pA[1], ("pA", 1))]
                stride = max(1, (nst - 8) // max(1, len(extra))) if extra else nst
                ei = 0

                def emit_ps(i):
                    ck_ = order[i]
                    ps_, psk_ = ring[i % 4]
                    S.op("tensor", lambda e, ps_=ps_, ck_=ck_: e.matmul(ps_[:, 0:128], lhsT=Ktok[:, ck_, :], rhs=Vtok[:, ck_, :], start=True, stop=True), r=["Ktok", "Vtok"], w=[psk_])
                for i in range(min(LA, nst)):
                    emit_ps(i)
                prev = None
                for i, ck in enumerate(order):
                    tn = Tst[i % 4]; tnk = ("Tst", i % 4)
                    to = Tst[(i - 1) % 4]; tok_ = ("Tst", (i - 1) % 4)
                    sb_n = Sbf[i % 4]; sbk = ("Sbf", i % 4)
                    sb_o = Sbf[(i - 1) % 4]; sbok = ("Sbf", (i - 1) % 4)
                    if ck >= 4:
                        lc = ck - 4
                        slot = lc % 8
                        po_ = pO[(lc // 8) % 2]; pok = ("pO", (lc // 8) % 2)
                        S.op("tensor", lambda e, po_=po_, slot=slot, ck=ck, lc=lc: e.matmul(po_[:, slot * 64:(slot + 1) * 64], lhsT=Vtok[:, ck, :], rhs=AT[:, lc * 64:(lc + 1) * 64], start=True, stop=False), r=["Vtok", "AT"], w=[pok])
                        S.op("tensor", lambda e, po_=po_, slot=slot, lc=lc, sb_o=sb_o: e.matmul(po_[:, slot * 64:(slot + 1) * 64], lhsT=sb_o[:], rhs=Qb[:, lc * 64:(lc + 1) * 64], start=False, stop=True), r=[sbok, Qk], w=[pok])
                    if i + LA < nst:
                        emit_ps(i + LA)
                    ps_, psk = ring[i % 4]
                    if i == 0:
                        S.op("vector", lambda e, tn=tn, ps_=ps_: e.tensor_copy(out=tn[:], in_=ps_[:, 0:128]), r=[psk], w=[tnk])
                    else:
                        S.op("vector", lambda e, tn=tn, to=to, ps_=ps_, prev=prev: e.scalar_tensor_tensor(out=tn[:], in0=to[:], scalar=gb[:, prev:prev + 1], in1=ps_[:, 0:128], op0=ALU.mult, op1=ALU.add), r=[tok_, psk, gk], w=[tnk])
                    if i + 1 < nst:
                        S.op("scalar", lambda e, sb_n=sb_n, tn=tn, ck=ck: e.activation(out=sb_n[:], in_=tn[:], func=AF.Identity, scale=gb[:, ck:ck + 1]), r=[tnk, gk], w=[sbk])
                    prev = ck
                    if ck >= 4:
                        lc = ck - 4
                        done = (lc % 8 == 7) if d == 0 else (lc % 8 == 0)
                        if done:
                            g8 = lc // 8
                            if d == 0:
                                S.op("scalar", lambda e, po_=po_, g8=g8: e.activation(out=O[:, g8 * 512:(g8 + 1) * 512], in_=po_[:], func=AF.Identity), r=[pok], w=[("O", g8)])
                            else:
                                S.op("vector", lambda e, po_=po_, g8=g8: e.tensor_tensor(out=O[:, g8 * 512:(g8 + 1) * 512], in0=po_[:], in1=O[:, g8 * 512:(g8 + 1) * 512], op=ALU.add), r=[pok, ("O", g8)], w=[("O", g8)])
                    if extra and ei < len(extra) and i >= 4 and (i - 4) % stride == 0:
                        extra[ei]()
                        ei += 1
                while extra and ei < len(extra):
                    extra[ei]()
                    ei += 1

            def final(hd):
                Ok = [("O", g8) for g8 in range(8)]
                sqb = Kt[1]; sqk = ("Kt", 1)
                yb_ = Qt[1]; ybk_ = ("Qt", 1)
                S.op("scalar", lambda e: e.activation(out=sqb[:, 0:SEQ], in_=O[:], func=AF.Square), r=Ok + [sqk], w=[sqk])
                for g8 in range(8):
                    pa_ = pA[g8 % 2]; pak = ("pA", g8 % 2)
                    S.op("tensor", lambda e, pa_=pa_, g8=g8: e.matmul(pa_[:], lhsT=ones_bf[:], rhs=sqb[:, g8 * 512:(g8 + 1) * 512], start=True, stop=True), r=[sqk, "ones"], w=[pak])
                    S.op("scalar", lambda e, pa_=pa_, g8=g8: e.activation(out=W[0][:, g8 * 512:(g8 + 1) * 512], in_=pa_[:], func=AF.Ln, scale=1.0 / 128, bias=epsT[:, 0:1]), r=[pak, "eps"], w=["W0"])
                S.op("scalar", lambda e: e.activation(out=W[0][:, 0:SEQ], in_=W[0][:, 0:SEQ], func=AF.Exp, scale=-0.5), r=["W0"], w=["W0"])
                S.op("vector", lambda e: e.tensor_tensor(out=O[:], in0=O[:], in1=W[0][:, 0:SEQ], op=ALU.mult), r=Ok + ["W0"], w=Ok)
                S.op("vector", lambda e: e.scalar_tensor_tensor(out=nat3(yb_, SEQ, 0), in0=AP(O, 0, [[SEQ, 128], [1, 64], [64, 64]]), scalar=hnw[:, 0:1], in1=nat3(sgbb, SEQ, 0), op0=ALU.mult, op1=ALU.mult), r=Ok + ["hnw", "sgbb", ybk_], w=[ybk_])
                S.op("sync", lambda e, hd=hd: e.dma_start(out=y_u[4 + hd].ap(), in_=yb_[:]), r=[ybk_], w=[("y_u", 4 + hd)], dma_key="yBst")
                if not debug:
                    S.op("gpsimd", lambda e, hd=hd: e.collective_compute("AllGather", ALU.bypass, replica_groups=RG, ins=[y_u[4 + hd].ap().opt()], outs=[yg_u[4 + hd].ap().opt()]), r=[("y_u", 4 + hd)], w=[("yg_u", 4 + hd)], dma_key=("ag", 4 + hd), inc=1)

            S.op("sync", lambda e: e.dma_start(out=vb[:], in_=G_d["v"].ap()[0:128, :]), w=["vb"], dma_key="vb")
            S.op("sync", lambda e: e.dma_start(out=qb[:], in_=G_d["q"].ap()[0:128, :]), w=["qb"], dma_key="qb")
            S.op("sync", lambda e: e.dma_start(out=SF[:], in_=G_d["sff"].ap()[0:128, :]), w=["SF"], dma_key="SF")
            for c_ in setup_closures(0):
                c_()
            for k in range(8):
                hd, d = divmod(k, 2)
                if d == 0:
                    vstage(hd)
                pestage(k)
                extra = setup_closures(k + 1) if k + 1 < 8 else []
                if d == 0 and hd + 1 < 4:
                    extra = extra + [lambda h_=hd + 1: vcopy(h_)]
                recurrence(k, extra)
                if d == 1:
                    final(hd)
            S.emit(nofinal=[("ag", 7)])
            for n_ in range(4, 8):
                if ("ag", n_) in S.dma_sems:
                    agsig[n_] = tuple(S.dma_sems[("ag", n_)])
            print("phase c waits", S.nwaits, "ops", len(S.ops))
        if stop_after <= 3:
            return nc

        with ExitStack() as ph:
            S = Sched(nc, top, "d")
            wo_sb = sbt(ph, "wo_sb", [128, 32, 512], BF16)
            Yb = [sbt(ph, "Yb%d" % i, [128, 32, 512], BF16) for i in range(2)]
            Z = sbt(ph, "Z", [128, 32, 512])
            gate_b = sbt(ph, "gate_b", [128, 512]); fnw_b = sbt(ph, "fnw_b", [128, 512])
            tmpb = [sbt(ph, "tmpb%d" % i, [128, 512]) for i in range(2)]
            junk = sbt(ph, "junk", [128, 512])
            ss3 = sbt(ph, "ss3", [128, 32]); ssg = sbt(ph, "ssg", [128, 4, 32]); rs3 = sbt(ph, "rs3", [128, 32])
            po3 = [pst(ph, "po3_%d" % i, [128, 512]) for i in range(4)]
            import os
            wstg = [sbt(ph, "wstg%d" % i, [128, 4, 512]) for i in range(2)]

            def wload():
                for c4 in range(8):
                    ws = wstg[c4 % 2]; wk = ("wstg", c4 % 2)
                    S.op("sync", lambda e, ws=ws, c4=c4: e.dma_start(out=ws[:], in_=AP(wout_d, c4 * 4 * 128 * 512, [[512, 128], [128 * 512, 4], [1, 512]])), w=[wk], dma_key=wk)
                    S.op("scalar", lambda e, ws=ws, c4=c4: e.activation(out=wo_sb[:, c4 * 4:(c4 + 1) * 4, :], in_=ws[:], func=AF.Identity), r=[wk], w=[("wo", c4 // 2)])
            if not os.environ.get("K_NOGATE"):
                S.op("sync", lambda e: e.dma_start(out=gate_b[:], in_=AP(ada_g, 0, [[0, 128], [1, 512]])), w=["gate"], dma_key="par", bulk=True)
            if not os.environ.get("K_NOFNW"):
                S.op("sync", lambda e: e.dma_start(out=fnw_b[:], in_=AP(fnw_d, 0, [[0, 128], [1, 512]])), w=["fnw"], dma_key="par", bulk=True)
            import os
            NBLK = int(os.environ.get('K_D_NBLK', 8))
            for blk in range(NBLK):
                yb_ = Yb[blk % 2]; ybk = ("Yb", blk % 2)
                S.op("sync", lambda e, blk=blk: e.dma_start(out=Z[:, blk * 4:(blk + 1) * 4, :], in_=AP(xq_d, blk * 512 * 512, [[512, 128], [128 * 512, 4], [1, 512]])), w=[("Z", blk * 4 + t_) for t_ in range(4)], dma_key=("zl", blk))
                if blk == 0:
                    wload()
                for i in range(8):
                    S.op("sync", lambda e, yb_=yb_, i=i, blk=blk: e.dma_start(out=yb_[:, i * 4:(i + 1) * 4, :], in_=AP(yg_u[i], blk * 512, [[SEQ, 128], [128 * SEQ, 4], [1, 512]])), w=[(ybk, i)], dma_key=ybk, grp=blk, ext=([agsig[i]] if i in agsig else []))
                for tl in range(4):
                    ti = blk * 4 + tl
                    zk = ("Z", ti)
                    p_ = po3[ti % 4]; pk = ("po3", ti % 4)
                    for kc in range(32):
                        S.op("tensor", lambda e, p_=p_, kc=kc, tl=tl, yb_=yb_: e.matmul(p_[:], lhsT=yb_[:, kc, tl * 128:(tl + 1) * 128], rhs=wo_sb[:, kc, :], start=(kc == 0), stop=(kc == 31)),
                             r=[(ybk, kc // 4), ("wo", kc // 8)], w=[pk])
                    tb = tmpb[ti % 2]; tbk = ("tmpb", ti % 2)
                    S.op("vector", lambda e, p_=p_, tb=tb: e.tensor_tensor(out=tb[:], in0=p_[:], in1=gate_b[:], op=ALU.mult), r=[pk, "gate"], w=[tbk])
                    S.op("gpsimd", lambda e, tb=tb, ti=ti: e.tensor_tensor(out=Z[:, ti, :], in0=Z[:, ti, :], in1=tb[:], op=ALU.add), r=[zk, tbk], w=[zk])
                    S.op("scalar", lambda e, ti=ti: e.activation(out=junk[:], in_=Z[:, ti, :], func=AF.Square, accum_out=ss3[:, ti:ti + 1]), r=[zk], w=["junk", ("ss3", ti)])
            if os.environ.get('K_D_NOFIN'):
                S.emit()
                return nc
            ssk = [("ss3", ti) for ti in range(32)]
            S.op("sync", lambda e: e.dma_start(out=ss_loc.ap(), in_=ss3[:]), r=ssk, w=["ss_loc"], dma_key="ssst")
            import os
            if os.environ.get("K_SKIP_SSAG"):
                S.op("sync", lambda e: e.dma_start(out=ss_all.ap()[0:128, :], in_=ss_loc.ap()), r=["ss_loc"], w=["ss_all"], dma_key="agss")
            else:
                S.op("gpsimd", lambda e: e.collective_compute("AllGather", ALU.bypass, replica_groups=RG, ins=[ss_loc.ap().opt()], outs=[ss_all.ap().opt()]), r=["ss_loc"], w=["ss_all"], dma_key="agss", inc=1)
            S.op("sync", lambda e: e.dma_start(out=ssg[:], in_=AP(ss_all, 0, [[32, 128], [128 * 32, 4], [1, 32]])), r=["ss_all"], w=["ssg"], dma_key="ssld")
            S.op("vector", lambda e: e.tensor_tensor(out=rs3[:], in0=ssg[:, 0, :], in1=ssg[:, 1, :], op=ALU.add), r=["ssg"], w=["rs3"])
            S.op("vector", lambda e: e.tensor_tensor(out=rs3[:], in0=rs3[:], in1=ssg[:, 2, :], op=ALU.add), r=["ssg", "rs3"], w=["rs3"])
            S.op("vector", lambda e: e.tensor_tensor(out=rs3[:], in0=rs3[:], in1=ssg[:, 3, :], op=ALU.add), r=["ssg", "rs3"], w=["rs3"])
            S.op("vector", lambda e: e.tensor_scalar(out=rs3[:], in0=rs3[:], scalar1=1.0 / D, scalar2=EPS, op0=ALU.mult, op1=ALU.add), r=["rs3"], w=["rs3"])
            S.op("scalar", lambda e: e.activation(out=rs3[:], in_=rs3[:], func=AF.Ln), r=["rs3"], w=["rs3"])
            S.op("scalar", lambda e: e.activation(out=rs3[:], in_=rs3[:], func=AF.Exp, scale=-0.5), r=["rs3"], w=["rs3"])
            for ti in range(32):
                zk = ("Z", ti)
                eng = "vector" if ti % 2 == 0 else "vector"
                S.op(eng, lambda e, ti=ti: e.scalar_tensor_tensor(out=Z[:, ti, :], in0=Z[:, ti, :], scalar=rs3[:, ti:ti + 1], in1=fnw_b[:], op0=ALU.mult, op1=ALU.mult), r=[zk, "rs3", "fnw"], w=[zk])
                S.op("sync", lambda e, ti=ti: e.dma_start(out=out_d.ap()[ti * 128:(ti + 1) * 128, :], in_=Z[:, ti, :]), r=[zk], w=[("outd", ti)], dma_key=("ost", ti % 4))
            S.emit()
            print("phase d waits", S.nwaits, "ops", len(S.ops))
    return nc


def shard_inputs(inputs):
    f = lambda a: np.ascontiguousarray(a, dtype=np.float32)
    x = inputs["x"]; ctx = inputs["ctx"]; c = inputs["c"]; c_ctx = inputs["c_ctx"]
    w_in = inputs["w_in"][0]; w_out = inputs["w_out"][0]
    maps = []
    for b in range(2):
        for j in range(4):
            cols = np.concatenate([np.arange(g * D + j * 512, g * D + (j + 1) * 512) for g in range(7)])
            rows = np.concatenate([(0 if i < 4 else D) + r * 512 + (i % 4) * 128 + np.arange(128) for i in range(8) for r in range(4)])
            fq = slice(j * 512, (j + 1) * 512)
            sl = slice(j * 512, (j + 1) * 512)
            m = {
                "x": f(x[b]), "ctx": f(ctx[b]), "cvec": f(np.stack([c[b], c_ctx])),
                "ada_w": f(inputs["ada_w"][0]), "ada_b": f(inputs["ada_b"][0][None]), "norm_w": f(inputs["norm_w"][0][None]),
                "w_in": f(w_in[:, cols]),
                "conv_w": f(inputs["conv_w"][0][:, sl]), "conv_b": f(inputs["conv_b"][0][None, sl]),
                "lru_wr": f(inputs["lru_wr"][0][:, j * 4:(j + 1) * 4]), "lru_wi": f(inputs["lru_wi"][0][:, j * 4:(j + 1) * 4]),
                "lru_br": f(inputs["lru_br"][0][:, sl]), "lru_bi": f(inputs["lru_bi"][0][:, sl]), "lru_lam": f(inputs["lru_lambda"][0][:, sl]),
                "lb_logits": f(inputs["hgrn_lb_logits"][:, :, sl]), "hnorm_w": f(inputs["hgrn_norm_w"][0][None]),
                "w_out": f(w_out[rows][:, fq]), "fnorm_w": f(inputs["final_norm_w"][None, fq]),
                "xq": f(x[b][:, fq]), "ada_wg": f(inputs["ada_w"][0][:, 2 * D + j * 512:2 * D + (j + 1) * 512]),
                "ada_bg": f(inputs["ada_b"][0][None, 2 * D + j * 512:2 * D + (j + 1) * 512]),
            }
            maps.append(m)
    return maps


def kernel(**inputs):
    nc = build_program()
    maps = shard_inputs(inputs)
    res = run_bass_kernel_spmd(nc, maps, core_ids=list(range(8)))
    out = np.zeros((2, SEQ, D), np.float32)
    for b in range(2):
        for j in range(4):
            out[b, :, j * 512:(j + 1) * 512] = res.results[b * 4 + j]["out"]
    return out
```

```python
import numpy as np
import concourse.bass as bass
import concourse.mybir as mybir
from concourse.bass_utils import run_bass_kernel_spmd
from contextlib import ExitStack

F32 = mybir.dt.float32
BF16 = mybir.dt.bfloat16
AF = mybir.ActivationFunctionType
ALU = mybir.AluOpType

D = 2048
SEQ = 4096
CTX = 256
T = CTX + SEQ
NCOL = 3584
EPS = 1e-6
CH = 64
NCHK = T // CH


class Sched:
    ENG = ("tensor", "vector", "scalar", "gpsimd", "sync")
    ROT = 4000

    def __init__(self, nc, stack, name, sync_same=True):
        self.nc = nc
        self.stack = stack
        self.name = name
        self.ops = []
        self.last_w = {}
        self.readers = {}
        self.sync_same = sync_same
        self.bulk = set()

    def op(self, eng, fn, r=(), w=(), dma_key=None, bulk=False, inc=16, grp=None, ext=()):
        i = len(self.ops)
        deps = set()
        for k in r:
            if k in self.last_w:
                deps.add(self.last_w[k])
        for k in w:
            if k in self.last_w:
                deps.add(self.last_w[k])
            last = {}
            for ri in self.readers.get(k, ()):
                ro = self.ops[ri]
                if ro["dma_key"] is not None:
                    deps.add(ri)
                else:
                    last[ro["eng"]] = max(last.get(ro["eng"], -1), ri)
            deps.update(last.values())
        pf = getattr(self, "pending_fence", None)
        if pf and eng in pf:
            deps.update(pf.pop(eng))
        self.ops.append(dict(eng=eng, fn=fn, deps=deps, dma_key=dma_key, idx=i, sig=None, inc=inc))
        if bulk and grp is None:
            grp = "all"
        self.ops[-1]["grp"] = grp
        self.ops[-1]["ext"] = list(ext)
        for k in r:
            self.readers.setdefault(k, []).append(i)
        for k in w:
            self.last_w[k] = i
            self.readers[k] = []
        return i

    def fence(self):
        last = {}
        dmas = set()
        for o in self.ops:
            if o["dma_key"] is not None:
                dmas.add(o["idx"])
            else:
                last[o["eng"]] = o["idx"]
        self.pending_fence = {e: set(last.values()) | set(dmas) for e in self.ENG}

    def emit(self, nofinal=()):
        nc = self.nc
        ops = self.ops
        need = [False] * len(ops)
        for o in ops:
            for d in o["deps"]:
                if ops[d]["dma_key"] is not None:
                    continue
                if ops[d]["eng"] != o["eng"] or (self.sync_same and o["eng"] != "tensor"):
                    need[d] = True
        eng_cnt = {e: 0 for e in self.ENG}
        eng_sems = {e: [] for e in self.ENG}
        dma_sems = {}
        for o in ops:
            if o["dma_key"] is not None:
                k = o["dma_key"]
                if k not in dma_sems:
                    dma_sems[k] = [self.stack.enter_context(nc.semaphore("d%s_%d" % (self.name, len(dma_sems)))), 0]
                dma_sems[k][1] += o["inc"]
                o["sig"] = [dma_sems[k][0], dma_sems[k][1], o["inc"]]
            elif need[o["idx"]]:
                e = o["eng"]
                c = eng_cnt[e]
                si = c // self.ROT
                if si >= len(eng_sems[e]):
                    eng_sems[e].append(self.stack.enter_context(nc.semaphore("e%s_%s_%d" % (self.name, e, si))))
                o["sig"] = [eng_sems[e][si], c % self.ROT + 1, 1]
                eng_cnt[e] = c + 1
        gmax = {}
        for o in ops:
            if o["dma_key"] is not None and o["grp"] is not None:
                gk = (o["dma_key"], o["grp"])
                gmax[gk] = max(gmax.get(gk, 0), o["sig"][1])
        for o in ops:
            if o["dma_key"] is not None and o["grp"] is not None:
                o["sig"][1] = gmax[(o["dma_key"], o["grp"])]
        self.dma_sems = dma_sems
        nwaits = {e: 0 for e in self.ENG}
        with nc.Block(no_gpsimd_drain=(self.name != "a")) as block:
            for eng in self.ENG:
                def body(e, eng=eng):
                    seen = {}
                    for o in ops:
                        if o["eng"] != eng:
                            continue
                        for d in sorted(o["deps"]):
                            sg = ops[d]["sig"]
                            if sg is None:
                                continue
                            if ops[d]["dma_key"] is None and ops[d]["eng"] == eng and not (self.sync_same and eng != "tensor"):
                                continue
                            sem, val, _ = sg
                            key = id(sem)
                            if seen.get(key, 0) >= val:
                                continue
                            e.wait_ge(sem, val)
                            nwaits[eng] += 1
                            seen[key] = val
                        for (xsem, xval) in o["ext"]:
                            if seen.get(id(xsem), 0) < xval:
                                e.wait_ge(xsem, xval)
                                seen[id(xsem)] = xval
                        inst = o["fn"](e)
                        if o["sig"] is not None:
                            inst.then_inc(o["sig"][0], o["sig"][2])
                    if eng == "sync":
                        for k, (sem, cnt) in dma_sems.items():
                            if k in nofinal:
                                continue
                            e.wait_ge(sem, cnt)
                getattr(block, eng)(body)
        self.nwaits = nwaits


def AP(t, off, pat):
    return bass.AP(t, off, [list(p) for p in pat])


GROUPS = [
    ("xa", None, F32, True),
    ("sga", AF.Silu, BF16, False),
    ("q", AF.Silu, BF16, False),
    ("sff", AF.Sigmoid, F32, True),
    ("sfb", AF.Sigmoid, F32, True),
    ("v", None, BF16, True),
    ("sgb", AF.Silu, BF16, False),
]


def build_program(debug=False, stop_after=99):
    nc = bass.Bass("TRN2", target_bir_lowering=False)
    I = {}

    def din(name, shape, dt=F32):
        I[name] = nc.dram_tensor(name, list(shape), dt, kind="ExternalInput")
        return I[name]

    x_d = din("x", [SEQ, D]); ctx_d = din("ctx", [CTX, D]); cvec_d = din("cvec", [2, D])
    adaw_d = din("ada_w", [D, 3 * D]); adab_d = din("ada_b", [1, 3 * D]); normw_d = din("norm_w", [1, D])
    win_d = din("w_in", [D, NCOL])
    convw_d = din("conv_w", [4, 512]); convb_d = din("conv_b", [1, 512])
    wr_d = din("lru_wr", [2, 4, 128, 128]); wi_d = din("lru_wi", [2, 4, 128, 128])
    br_d = din("lru_br", [2, 512]); bi_d = din("lru_bi", [2, 512]); lam_d = din("lru_lam", [2, 512])
    lbl_d = din("lb_logits", [2, 2, 512]); hnw_d = din("hnorm_w", [1, 128])
    wout_d = din("w_out", [2 * D, 512]); fnw_d = din("fnorm_w", [1, 512])
    xq_d = din("xq", [SEQ, 512]); adawg_d = din("ada_wg", [D, 512]); adabg_d = din("ada_bg", [1, 512])
    out_d = nc.dram_tensor("out", [SEQ, 512], F32, kind="ExternalOutput")

    skind = "ExternalOutput" if debug else "Internal"

    def dscr(name, shape, dt):
        return nc.dram_tensor(name, list(shape), dt, kind=skind)

    ada_s = dscr("ada_s", [2, 3 * D], F32)
    G_d = {}
    for (gname, _, gdt, needctx) in GROUPS:
        G_d[gname] = dscr("p_" + gname, [512, T if needctx else SEQ], gdt)
    ada_g = dscr("ada_g", [2, 512], F32)
    y_u = [dscr("y_u%d" % i, [128, SEQ], BF16) for i in range(8)]
    yg_u = [nc.dram_tensor("yg_u%d" % i, [512, SEQ], BF16) for i in range(8)]
    ss_loc = nc.dram_tensor("ss_loc", [128, 32], F32)
    ss_all = nc.dram_tensor("ss_all", [512, 32], F32)
    RG = [[0, 1, 2, 3], [4, 5, 6, 7]]

    with ExitStack() as top:
        def sbt(stack, name, shape, dt=F32):
            return stack.enter_context(nc.sbuf_tensor(name, list(shape), dt))

        def pst(stack, name, shape, dt=F32):
            return stack.enter_context(nc.psum_tensor(name, list(shape), dt))

        with ExitStack() as ph:
            S = Sched(nc, top, "a")
            w_sb = sbt(ph, "w_sb", [128, 16, NCOL], BF16)
            ident_f = sbt(ph, "ident_f", [128, 128]); ident = sbt(ph, "ident", [128, 128], BF16)
            ccol = sbt(ph, "ccol", [128, 2, 16]); scolT = sbt(ph, "scolT", [128, 16, 2])
            modc = sbt(ph, "modc", [128, 4, 16]); nwc = sbt(ph, "nwc", [128, 16])
            gsh = sbt(ph, "gsh", [128, 4, 16])
            ss = sbt(ph, "ss", [128, 40]); rstd = sbt(ph, "rstd", [128, 40])
            pacc = [pst(ph, "pacc%d" % i, [128, 512]) for i in range(4)]
            ptr = [pst(ph, "ptr%d" % i, [128, 1024], BF16) for i in range(4)]

            S.op("gpsimd", lambda e: e.memset(ident_f[:], 1.0), w=["identf"])
            S.op("gpsimd", lambda e: e.affine_select(out=ident_f[:], in_=ident_f[:], pattern=[[-1, 128]], compare_op=ALU.is_equal, fill=0.0, base=0, channel_multiplier=1), r=["identf"], w=["identf"])
            S.op("vector", lambda e: e.tensor_copy(out=ident[:], in_=ident_f[:]), r=["identf"], w=["ident"])
            for g in range(7):
                for kc in range(16):
                    S.op("gpsimd", lambda e, g=g, kc=kc: e.dma_start(out=w_sb[:, kc, g * 512:(g + 1) * 512], in_=win_d.ap()[kc * 128:(kc + 1) * 128, g * 512:(g + 1) * 512]),
                         w=[("w", g, kc)], dma_key=("w", g), bulk=True)
            for i in range(2):
                S.op("sync", lambda e, i=i: e.dma_start(out=ccol[:, i, :], in_=AP(cvec_d, i * D, [[1, 128], [128, 16]]), allow_slow_non_contiguous=True), w=[("ccol", i)], dma_key="cn", bulk=True)
            S.op("sync", lambda e: e.dma_start(out=nwc[:], in_=AP(normw_d, 0, [[1, 128], [128, 16]]), allow_slow_non_contiguous=True), w=["nwc"], dma_key="cn", bulk=True)
            for i in range(2):
                S.op("scalar", lambda e, i=i: e.activation(out=AP(scolT, i, [[32, 128], [2, 16]]), in_=ccol[:, i, :], func=AF.Silu), r=[("ccol", i)], w=[("scolT", i)])
            with ExitStack() as p0:
                aslot = [sbt(p0, "aslot%d" % i, [128, 2048]) for i in range(3)]
                rows = sbt(p0, "rows", [2, 3 * D]); adab2 = sbt(p0, "adab2", [2, 3 * D])
                S.op("sync", lambda e: e.dma_start(out=adab2[:], in_=AP(adab_d, 0, [[0, 2], [1, 3 * D]])), w=["adab2"], dma_key="adab2")
                li = 0
                for sec in range(2):
                    for kc in range(16):
                        sl = li % 3
                        S.op("sync", lambda e, sec=sec, kc=kc, sl=sl: e.dma_start(out=aslot[sl][:], in_=adaw_d.ap()[kc * 128:(kc + 1) * 128, sec * D:(sec + 1) * D]), w=[("aslot", sl)], dma_key=("aslot", sl))
                        for nb in range(4):
                            S.op("tensor", lambda e, kc=kc, sl=sl, nb=nb: e.matmul(pacc[nb][0:2, :], lhsT=scolT[:, kc, :], rhs=aslot[sl][:, nb * 512:(nb + 1) * 512], start=(kc == 0), stop=(kc == 15)),
                                 r=[("aslot", sl), ("scolT", 0), ("scolT", 1)], w=[("pacc", nb)])
                        li += 1
                    for nb in range(4):
                        c0 = sec * D + nb * 512
                        S.op("vector", lambda e, nb=nb, c0=c0: e.tensor_tensor(out=rows[0:2, c0:c0 + 512], in0=pacc[nb][0:2, :], in1=adab2[0:2, c0:c0 + 512], op=ALU.add),
                             r=[("pacc", nb), "adab2"], w=["rows"])
                for kc in range(16):
                    sl = li % 3
                    S.op("sync", lambda e, kc=kc, sl=sl: e.dma_start(out=aslot[sl][:, 0:512], in_=adawg_d.ap()[kc * 128:(kc + 1) * 128, :]), w=[("aslot", sl)], dma_key=("aslot", sl))
                    S.op("tensor", lambda e, kc=kc, sl=sl: e.matmul(pacc[0][0:2, :], lhsT=scolT[:, kc, :], rhs=aslot[sl][:, 0:512], start=(kc == 0), stop=(kc == 15)),
                         r=[("aslot", sl), ("scolT", 0), ("scolT", 1)], w=[("pacc", 0)])
                    li += 1
                rowsg = sbt(p0, "rowsg", [2, 512]); adabg2 = sbt(p0, "adabg2", [2, 512])
                S.op("sync", lambda e: e.dma_start(out=adabg2[:], in_=AP(adabg_d, 0, [[0, 2], [1, 512]])), w=["adabg2"], dma_key="adabg2")
                S.op("vector", lambda e: e.tensor_tensor(out=rowsg[0:2, :], in0=pacc[0][0:2, :], in1=adabg2[0:2, :], op=ALU.add), r=[("pacc", 0), "adabg2"], w=["rowsg"])
                S.op("sync", lambda e: e.dma_start(out=ada_g.ap(), in_=rowsg[:]), r=["rowsg"], w=["ada_g"], dma_key="ada_g")
                S.op("sync", lambda e: e.dma_start(out=ada_s.ap()[:, 0:2 * D], in_=rows[:, 0:2 * D]), r=["rows"], w=["ada_s"], dma_key="ada_s")
                for i, (row, sec) in enumerate([(0, 0), (0, 1), (1, 0), (1, 1)]):
                    S.op("sync", lambda e, i=i, row=row, sec=sec: e.dma_start(out=modc[:, i, :], in_=AP(ada_s, row * 3 * D + sec * D, [[1, 128], [128, 16]]), allow_slow_non_contiguous=True),
                         r=["ada_s"], w=[("modc", i)], dma_key="modc", bulk=True)
                for tgt, (sh_i, sc_i) in enumerate([(0, 1), (2, 3)]):
                    S.op("vector", lambda e, tgt=tgt, sc_i=sc_i: e.scalar_tensor_tensor(out=gsh[:, 2 * tgt, :], in0=modc[:, sc_i, :], scalar=1.0, in1=nwc[:], op0=ALU.add, op1=ALU.mult),
                         r=[("modc", sc_i), "nwc"], w=[("gsh", 2 * tgt)])
                    S.op("vector", lambda e, tgt=tgt, sh_i=sh_i: e.tensor_copy(out=gsh[:, 2 * tgt + 1, :], in_=modc[:, sh_i, :]), r=[("modc", sh_i)], w=[("gsh", 2 * tgt + 1)])

            S.fence()
            with ExitStack() as p1:
                xt = [sbt(p1, "xt%d" % i, [128, D]) for i in range(2)]
                xn = [sbt(p1, "xn%d" % i, [128, D], BF16) for i in range(2)]
                hT = [sbt(p1, "hT%d" % i, [128, 16, 512], BF16) for i in range(2)]
                stgF = [sbt(p1, "stgF%d" % i, [128, 4, 512]) for i in range(2)]
                stgB = [sbt(p1, "stgB%d" % i, [128, 4, 512], BF16) for i in range(2)]
                mtmp = [sbt(p1, "mtmp%d" % i, [128, 8, 128]) for i in range(2)]
                mcount = [0]
                scountB = [0]
                mstmp = sbt(p1, "mstmp", [128, 40])
                blocks = [(0, 2, True)] + [(2 + 4 * i, 4, False) for i in range(8)]
                tcount = [0]
                scount = [0]

                pstate = {}

                def prepA(bi, tl):
                    t0, nt, isctx = blocks[bi]
                    tt = t0 + tl
                    xs = xt[tcount[0] % 2]; xk = ("xt", tcount[0] % 2)
                    xb = xn[tcount[0] % 2]; xnk = ("xn", tcount[0] % 2)
                    pk = tcount[0] % 2
                    tcount[0] += 1
                    pstate[(bi, tl)] = (xb, xnk, pk)
                    src = ctx_d.ap()[tt * 128:(tt + 1) * 128, :] if isctx else x_d.ap()[(tt - 2) * 128:(tt - 1) * 128, :]
                    S.op("sync", lambda e, xs=xs, src=src: e.dma_start(out=xs[:], in_=src), w=[xk], dma_key=xk)
                    S.op("scalar", lambda e, xs=xs, xb=xb, tt=tt: e.activation(out=xb[:], in_=xs[:], func=AF.Square, accum_out=ss[:, tt:tt + 1]), r=[xk], w=[xnk, ("ss", tt)])
                    S.op("vector", lambda e, tt=tt: e.tensor_scalar(out=mstmp[:, tt:tt + 1], in0=ss[:, tt:tt + 1], scalar1=1.0 / D, scalar2=EPS, op0=ALU.mult, op1=ALU.add), r=[("ss", tt)], w=[("ms", tt)])
                    S.op("scalar", lambda e, tt=tt: e.activation(out=mstmp[:, tt:tt + 1], in_=mstmp[:, tt:tt + 1], func=AF.Ln), r=[("ms", tt)], w=[("ms", tt)])
                    S.op("scalar", lambda e, tt=tt: e.activation(out=rstd[:, tt:tt + 1], in_=mstmp[:, tt:tt + 1], func=AF.Exp, scale=-0.5), r=[("ms", tt)], w=[("rstd", tt)])
                    S.op("scalar", lambda e, xs=xs, xb=xb, tt=tt: e.activation(out=xb[:], in_=xs[:], func=AF.Identity, scale=rstd[:, tt:tt + 1]), r=[xk, ("rstd", tt)], w=[xnk])

                def prepB(bi, tl):
                    t0, nt, isctx = blocks[bi]
                    hb = hT[bi % 2]
                    gi = 2 if isctx else 0
                    xb, xnk, pk = pstate[(bi, tl)]
                    for half in range(2):
                        pt_ = ptr[pk * 2 + half]; ptk = ("ptr", pk * 2 + half)
                        for kl in range(8):
                            kc = half * 8 + kl
                            S.op("tensor", lambda e, pt_=pt_, xb=xb, kl=kl, kc=kc: e.transpose(out=pt_[:, kl * 128:(kl + 1) * 128], in_=xb[:, kc * 128:(kc + 1) * 128], identity=ident[:]),
                                 r=[xnk, "ident"], w=[ptk])
                        hk = ("hT", bi % 2)
                        dst = AP(hb, half * 8 * 512 + tl * 128, [[16 * 512, 128], [512, 8], [1, 128]])
                        srcp = AP(pt_, 0, [[1024, 128], [128, 8], [1, 128]])
                        gb_ = AP(gsh, gi * 16 + half * 8, [[64, 128], [1, 8], [0, 128]])
                        sb_ = AP(gsh, (gi + 1) * 16 + half * 8, [[64, 128], [1, 8], [0, 128]])
                        mt = mtmp[mcount[0] % 2]; mk = ("mtmp", mcount[0] % 2)
                        mcount[0] += 1
                        S.op("vector", lambda e, mt=mt, srcp=srcp, gb_=gb_: e.tensor_tensor(out=mt[:], in0=srcp, in1=gb_, op=ALU.mult), r=[ptk, ("gsh", gi)], w=[mk])
                        S.op("gpsimd", lambda e, dst=dst, mt=mt, sb_=sb_: e.tensor_tensor(out=dst, in0=mt[:], in1=sb_, op=ALU.add), r=[mk, ("gsh", gi + 1)], w=[hk])

                def proj(bi, hooks=()):
                    hooks = list(hooks)
                    t0, nt, isctx = blocks[bi]
                    ntok = nt * 128
                    hb = hT[bi % 2]; hk = ("hT", bi % 2)
                    for g, (gname, func, gdt, needctx) in enumerate(GROUPS):
                        if isctx and not needctx:
                            continue
                        if gdt == F32:
                            si = scount[0] % 2
                            scount[0] += 1
                            sk = ("stgF", si)
                            st_ = stgF[si]
                        else:
                            si = scountB[0] % 2
                            scountB[0] += 1
                            sk = ("stgB", si)
                            st_ = stgB[si]
                        for ml in range(4):
                            m = g * 4 + ml
                            pa = pacc[m % 4]; pak = ("pacc", m % 4)
                            for kc in range(16):
                                S.op("tensor", lambda e, pa=pa, kc=kc, m=m, hb=hb, ntok=ntok: e.matmul(pa[:, 0:ntok], lhsT=w_sb[:, kc, m * 128:(m + 1) * 128], rhs=hb[:, kc, 0:ntok], start=(kc == 0), stop=(kc == 15)),
                                     r=[hk, ("w", g, kc)], w=[pak])
                            dst = st_[:, ml, 0:ntok]
                            if func is None:
                                S.op("vector", lambda e, dst=dst, pa=pa, ntok=ntok: e.tensor_copy(out=dst, in_=pa[:, 0:ntok]), r=[pak], w=[sk])
                            else:
                                S.op("scalar", lambda e, dst=dst, pa=pa, ntok=ntok, func=func: e.activation(out=dst, in_=pa[:, 0:ntok], func=func), r=[pak], w=[sk])
                        Tg = T if needctx else SEQ
                        toff = t0 * 128 if needctx else (t0 - 2) * 128
                        srcs = st_[:, :, 0:ntok]
                        dstd = AP(G_d[gname], toff, [[Tg, 128], [128 * Tg, 4], [1, ntok]])
                        S.op("gpsimd", lambda e, dstd=dstd, srcs=srcs: e.dma_start(out=dstd, in_=srcs), r=[sk], w=[("G", gname, bi)], dma_key=sk)
                        if hooks:
                            for fn_ in hooks.pop(0):
                                fn_()
                    while hooks:
                        for fn_ in hooks.pop(0):
                            fn_()

                for tl in range(blocks[0][1]):
                    prepA(0, tl)
                    prepB(0, tl)
                for bi in range(len(blocks)):
                    hooks = []
                    if bi + 1 < len(blocks):
                        nt1 = blocks[bi + 1][1]
                        hooks.append([lambda b=bi + 1: prepA(b, 0)])
                        for tl in range(1, nt1):
                            hooks.append([lambda b=bi + 1, t=tl: prepB(b, t - 1), lambda b=bi + 1, t=tl: prepA(b, t)])
                        hooks.append([lambda b=bi + 1, t=nt1: prepB(b, t - 1)])
                    proj(bi, hooks)
            S.emit()
            print("phase a waits", S.nwaits, "ops", len(S.ops))
        if stop_after <= 1:
            return nc

        with ExitStack() as ph:
            S = Sched(nc, top, "b")
            NP = T + 6
            xpad = sbt(ph, "xpad", [128, NP]); uu = [sbt(ph, "u%d" % i, [128, T]) for i in range(2)]; ubfs = [sbt(ph, "ubf%d" % i, [128, T], BF16) for i in range(2)]
            Rb = [sbt(ph, "Rb%d" % i, [128, T]) for i in range(2)]; Ib = [sbt(ph, "Ib%d" % i, [128, T]) for i in range(2)]; Zb = [sbt(ph, "Zb%d" % i, [128, T]) for i in range(2)]
            sga = [sbt(ph, "sga%d" % i, [128, SEQ], BF16) for i in range(2)]; yA = sbt(ph, "yA", [128, SEQ], BF16)
            cw = sbt(ph, "cw", [128, 4, 4]); cb = sbt(ph, "cb", [128, 4]); onesA = sbt(ph, "onesA", [128, 1])
            S.op("vector", lambda e: e.memset(onesA[:], 1.0), w=["onesA"])
            brt = sbt(ph, "brt", [128, 2, 4]); bit = sbt(ph, "bit", [128, 2, 4]); lamt = sbt(ph, "lamt", [128, 2, 4]); c1 = sbt(ph, "c1", [128, 2, 4])
            wr_sb = sbt(ph, "wr_sb", [128, 8, 128], BF16); wi_sb = sbt(ph, "wi_sb", [128, 8, 128], BF16)
            pg = [pst(ph, "pg%d" % i, [128, 512]) for i in range(4)]
            for k in range(4):
                S.op("sync", lambda e, k=k: e.dma_start(out=cw[:, :, k], in_=AP(convw_d, k * 512, [[1, 128], [128, 4]]), allow_slow_non_contiguous=True), w=[("cw", k)], dma_key="par", bulk=True)
            S.op("sync", lambda e: e.dma_start(out=cb[:], in_=AP(convb_d, 0, [[1, 128], [128, 4]]), allow_slow_non_contiguous=True), w=["cb"], dma_key="par", bulk=True)
            for d in range(2):
                for nm, tl_, dd in (("brt", brt, br_d), ("bit", bit, bi_d), ("lamt", lamt, lam_d)):
                    S.op("sync", lambda e, d=d, tl_=tl_, dd=dd: e.dma_start(out=tl_[:, d, :], in_=AP(dd, d * 512, [[1, 128], [128, 4]]), allow_slow_non_contiguous=True), w=[(nm, d)], dma_key="par", bulk=True)
            S.op("gpsimd", lambda e: e.dma_start(out=wr_sb[:], in_=AP(wr_d, 0, [[128, 128], [16384, 8], [1, 128]])), w=["wr"], dma_key="wr")
            S.op("gpsimd", lambda e: e.dma_start(out=wi_sb[:], in_=AP(wi_d, 0, [[128, 128], [16384, 8], [1, 128]])), w=["wi"], dma_key="wi")
            S.op("scalar", lambda e: e.activation(out=c1[:], in_=lamt[:], func=AF.Exp, scale=-1.0), r=[("lamt", 0), ("lamt", 1)], w=["c1"])
            S.op("vector", lambda e: e.tensor_scalar(out=c1[:], in0=c1[:], scalar1=1.0, scalar2=None, op0=ALU.add), r=["c1"], w=["c1"])
            S.op("scalar", lambda e: e.activation(out=c1[:], in_=c1[:], func=AF.Ln), r=["c1"], w=["c1"])
            S.op("vector", lambda e: e.tensor_scalar(out=c1[:], in0=c1[:], scalar1=-8.0, scalar2=None, op0=ALU.mult), r=["c1"], w=["c1"])
            for (a0, a1) in ((0, 2), (258, 261), (4357, 4358)):
                S.op("gpsimd", lambda e, a0=a0, a1=a1: e.memset(xpad[:, a0:a1], 0.0), w=[("pad", a0)])
            def loadA(n):
                S.op("sync", lambda e, n=n: e.dma_start(out=xpad[:, 2:258], in_=G_d["xa"].ap()[n * 128:(n + 1) * 128, 0:256]), w=["xc"], dma_key="xc")
                S.op("sync", lambda e, n=n: e.dma_start(out=xpad[:, 261:4357], in_=G_d["xa"].ap()[n * 128:(n + 1) * 128, 256:T]), w=["xl"], dma_key="xl")

            def loadS(n):
                S.op("sync", lambda e, n=n: e.dma_start(out=sga[n % 2][:], in_=G_d["sga"].ap()[n * 128:(n + 1) * 128, :]), w=[("sga", n % 2)], dma_key=("sga", n % 2))
            loadA(0)
            loadS(0)
            loadS(1)

            def convA(n):
                u = uu[n % 2]; ubf = ubfs[n % 2]
                padk = [("pad", 0), ("pad", 258), ("pad", 4357)]
                for (uo, xo, ln, xk) in ((0, 0, 256, "xc"), (256, 259, SEQ, "xl")):
                    S.op("vector", lambda e, uo=uo, xo=xo, ln=ln, n=n, u=u: e.tensor_scalar(out=u[:, uo:uo + ln], in0=xpad[:, xo:xo + ln], scalar1=cw[:, n, 0:1], scalar2=cb[:, n:n + 1], op0=ALU.mult, op1=ALU.add),
                         r=[xk, ("cw", 0), "cb"] + padk, w=[("u", n % 2, uo)])
                    for k in range(1, 4):
                        S.op("vector", lambda e, uo=uo, xo=xo, ln=ln, n=n, k=k, u=u: e.scalar_tensor_tensor(out=u[:, uo:uo + ln], in0=xpad[:, xo + k:xo + k + ln], scalar=cw[:, n, k:k + 1], in1=u[:, uo:uo + ln], op0=ALU.mult, op1=ALU.add),
                             r=[xk, ("cw", k)] + padk, w=[("u", n % 2, uo)])
                if n + 1 < 4:
                    loadA(n + 1)
                S.op("gpsimd", lambda e, u=u, ubf=ubf: e.tensor_copy(out=ubf[:], in_=u[:]), r=[("u", n % 2, 0), ("u", n % 2, 256)], w=[("ubf", n % 2)])
            convA(0)
            for n in range(4):
                u = uu[n % 2]; ubf = ubfs[n % 2]
                ubk = ("ubf", n % 2)
                for blk in range(9):
                    b0 = blk * 512; bn = min(512, T - b0)
                    for d in range(2):
                        pr = pg[d * 2]; pi = pg[d * 2 + 1]
                        S.op("tensor", lambda e, pr=pr, d=d, n=n, b0=b0, bn=bn, ubf=ubf: e.matmul(pr[:, 0:bn], lhsT=wr_sb[:, d * 4 + n, :], rhs=ubf[:, b0:b0 + bn], start=True, stop=True), r=[ubk, "wr"], w=[("pg", d * 2)])
                        S.op("tensor", lambda e, pi=pi, d=d, n=n, b0=b0, bn=bn, ubf=ubf: e.matmul(pi[:, 0:bn], lhsT=wi_sb[:, d * 4 + n, :], rhs=ubf[:, b0:b0 + bn], start=True, stop=True), r=[ubk, "wi"], w=[("pg", d * 2 + 1)])
                        S.op("scalar", lambda e, pr=pr, d=d, n=n, b0=b0, bn=bn: e.activation(out=Rb[d][:, b0:b0 + bn], in_=pr[:, 0:bn], func=AF.Sigmoid, bias=brt[:, d, n:n + 1]), r=[("pg", d * 2), ("brt", d)], w=[("Rb", d)])
                        S.op("scalar", lambda e, pi=pi, d=d, n=n, b0=b0, bn=bn: e.activation(out=Ib[d][:, b0:b0 + bn], in_=pi[:, 0:bn], func=AF.Sigmoid, bias=bit[:, d, n:n + 1]), r=[("pg", d * 2 + 1), ("bit", d)], w=[("Ib", d)])
                if n + 1 < 4:
                    convA(n + 1)
                for d in range(2):
                    S.op("scalar", lambda e, d=d, n=n: e.activation(out=Rb[d][:], in_=Rb[d][:], func=AF.Exp, scale=c1[:, d, n:n + 1]), r=[("Rb", d), "c1"], w=[("Rb", d)])
                for d in range(2):
                    S.op("gpsimd", lambda e, d=d, u=u: e.tensor_tensor(out=Ib[d][:], in0=Ib[d][:], in1=u[:], op=ALU.mult), r=[("Ib", d), ("u", n % 2, 0), ("u", n % 2, 256)], w=[("Ib", d)])
                for d in range(2):
                    S.op("scalar", lambda e, d=d: e.activation(out=Zb[d][:], in_=Rb[d][:], func=AF.Square), r=[("Rb", d)], w=[("Zb", d)])
                for d in range(2):
                    S.op("scalar", lambda e, d=d: e.activation(out=Zb[d][:], in_=Zb[d][:], func=AF.Ln, scale=-1.0, bias=onesA[:, 0:1]), r=[("Zb", d), "onesA"], w=[("Zb", d)])
                for d in range(2):
                    S.op("scalar", lambda e, d=d: e.activation(out=Zb[d][:], in_=Zb[d][:], func=AF.Exp, scale=0.5), r=[("Zb", d)], w=[("Zb", d)])
                for d in range(2):
                    S.op("vector", lambda e, d=d: e.tensor_tensor(out=Ib[d][:], in0=Ib[d][:], in1=Zb[d][:], op=ALU.mult), r=[("Ib", d), ("Zb", d)], w=[("Ib", d)])
                S.op("vector", lambda e: e.tensor_tensor_scan(out=Zb[0][:], data0=Rb[0][:], data1=Ib[0][:], initial=0.0, op0=ALU.mult, op1=ALU.add), r=[("Rb", 0), ("Ib", 0), ("Zb", 0)], w=[("Zb", 0)])
                rv = lambda t_, a0, ln: AP(t_, a0 + ln - 1, [[T, 128], [-1, ln]])
                S.op("vector", lambda e: e.tensor_tensor_scan(out=rv(Zb[1], 0, 256), data0=rv(Rb[1], 0, 256), data1=rv(Ib[1], 0, 256), initial=0.0, op0=ALU.mult, op1=ALU.add), r=[("Rb", 1), ("Ib", 1), ("Zb", 1)], w=[("Zb", 1)])
                S.op("vector", lambda e: e.tensor_tensor_scan(out=rv(Zb[1], 256, SEQ), data0=rv(Rb[1], 256, SEQ), data1=rv(Ib[1], 256, SEQ), initial=Zb[1][:, 0:1], op0=ALU.mult, op1=ALU.add), r=[("Rb", 1), ("Ib", 1), ("Zb", 1)], w=[("Zb", 1)])
                S.op("gpsimd", lambda e: e.tensor_tensor(out=Zb[0][:, 256:T], in0=Zb[0][:, 256:T], in1=Zb[1][:, 256:T], op=ALU.add), r=[("Zb", 0), ("Zb", 1)], w=[("Zb", 0)])
                S.op("vector", lambda e, n=n: e.tensor_tensor(out=yA[:], in0=Zb[0][:, 256:T], in1=sga[n % 2][:], op=ALU.mult), r=[("Zb", 0), ("sga", n % 2)], w=["yA"])
                if n + 2 < 4:
                    loadS(n + 2)
                S.op("sync", lambda e, n=n: e.dma_start(out=y_u[n].ap(), in_=yA[:]), r=["yA"], w=[("y_u", n)], dma_key="yAst")
                if not debug:
                    S.op("gpsimd", lambda e, n=n: e.collective_compute("AllGather", ALU.bypass, replica_groups=RG, ins=[y_u[n].ap().opt()], outs=[yg_u[n].ap().opt()]), r=[("y_u", n)], w=[("yg_u", n)], dma_key=("ag", n), inc=1)
            S.emit(nofinal=[("ag", 3)])
            agsig = {}
            for n_ in range(4):
                if ("ag", n_) in S.dma_sems:
                    agsig[n_] = tuple(S.dma_sems[("ag", n_)])
            print("phase b waits", S.nwaits, "ops", len(S.ops))
        if stop_after <= 2:
            return nc

        with ExitStack() as ph:
            S = Sched(nc, top, "c")
            SF = sbt(ph, "SF", [128, T]); qb = sbt(ph, "qb", [128, SEQ], BF16); vb = sbt(ph, "vb", [128, T], BF16); vcm = sbt(ph, "vcm", [128, T], BF16)
            Vtok = sbt(ph, "Vtok", [64, NCHK, 128], BF16); Ktok = sbt(ph, "Ktok", [64, NCHK, 128], BF16)
            W = [sbt(ph, "W%d" % i, [128, T]) for i in range(3)]
            Qt = [sbt(ph, "Qt%d" % i, [128, SEQ], BF16) for i in range(2)]; Kt = [sbt(ph, "Kt%d" % i, [128, T], BF16) for i in range(2)]
            AT = sbt(ph, "AT", [64, SEQ], BF16); O = sbt(ph, "O", [128, SEQ])
            Mx = sbt(ph, "Mx", [128, T + 1], BF16)
            sgbb = sbt(ph, "sgbb", [128, SEQ], BF16)
            Tst = [sbt(ph, "Tst%d" % i, [128, 128]) for i in range(4)]
            Sbf = [sbt(ph, "Sbf%d" % i, [128, 128], BF16) for i in range(4)]
            gam = [sbt(ph, "gam%d" % i, [128, NCHK]) for i in range(2)]
            lbl = sbt(ph, "lbl", [128, 4, 4]); lb = sbt(ph, "lb", [128, 2, 4]); oml = sbt(ph, "oml", [128, 2, 4])
            hnw = sbt(ph, "hnw", [128, 1]); epsT = sbt(ph, "epsT", [128, 1])
            mF = sbt(ph, "mF", [64, 64]); mB = sbt(ph, "mB", [64, 64]); ones_bf = sbt(ph, "ones_bf", [128, 128], BF16)
            identb_f = sbt(ph, "identb_f", [128, 128]); identb = sbt(ph, "identb", [128, 128], BF16)
            pT = [pst(ph, "pT%d" % i, [128, 1024], BF16) for i in range(2)]
            pA = [pst(ph, "pA%d" % i, [128, 512]) for i in range(2)]
            pO = [pst(ph, "pO%d" % i, [128, 512]) for i in range(2)]
            pS = [pst(ph, "pS%d" % i, [128, 512]) for i in range(2)]
            S.op("gpsimd", lambda e: e.memset(identb_f[:], 1.0), w=["identf"])
            S.op("gpsimd", lambda e: e.affine_select(out=identb_f[:], in_=identb_f[:], pattern=[[-1, 128]], compare_op=ALU.is_equal, fill=0.0, base=0, channel_multiplier=1), r=["identf"], w=["identf"])
            S.op("vector", lambda e: e.tensor_copy(out=identb[:], in_=identb_f[:]), r=["identf"], w=["ident"])
            S.op("gpsimd", lambda e: e.memset(mF[:], 1.0), w=["mF"])
            S.op("gpsimd", lambda e: e.affine_select(out=mF[:], in_=mF[:], pattern=[[1, 64]], compare_op=ALU.is_ge, fill=0.0, base=0, channel_multiplier=-1), r=["mF"], w=["mF"])
            S.op("gpsimd", lambda e: e.memset(mB[:], 1.0), w=["mB"])
            S.op("gpsimd", lambda e: e.affine_select(out=mB[:], in_=mB[:], pattern=[[-1, 64]], compare_op=ALU.is_ge, fill=0.0, base=0, channel_multiplier=1), r=["mB"], w=["mB"])
            S.op("vector", lambda e: e.memset(ones_bf[:], 1.0), w=["ones"])
            S.op("vector", lambda e: e.memset(epsT[:], EPS), w=["eps"])
            S.op("vector", lambda e: e.memset(Mx[:], 1.0), w=["Mx"])
            S.op("vector", lambda e: e.memset(AP(Mx, 0, [[T + 1, 128], [64, NCHK + 1]]), 0.0), r=["Mx"], w=["Mx"])
            for d in range(2):
                for l in range(2):
                    S.op("sync", lambda e, d=d, l=l: e.dma_start(out=lbl[:, d * 2 + l, :], in_=AP(lbl_d, (d * 2 + l) * 512, [[1, 128], [128, 4]]), allow_slow_non_contiguous=True), w=[("lbl", d * 2 + l)], dma_key="par", bulk=True)
            S.op("sync", lambda e: e.dma_start(out=hnw[:], in_=AP(hnw_d, 0, [[1, 128], [1, 1]])), w=["hnw"], dma_key="par", bulk=True)
            for d in range(2):
                S.op("vector", lambda e, d=d: e.tensor_tensor(out=lb[:, d, :], in0=lbl[:, 2 * d, :], in1=lbl[:, 2 * d + 1, :], op=ALU.subtract), r=[("lbl", 2 * d), ("lbl", 2 * d + 1)], w=[("lb", d)])
            S.op("scalar", lambda e: e.activation(out=lb[:], in_=lb[:], func=AF.Sigmoid), r=[("lb", 0), ("lb", 1)], w=["lbs"])
            S.op("vector", lambda e: e.tensor_scalar(out=oml[:], in0=lb[:], scalar1=-1.0, scalar2=1.0, op0=ALU.mult, op1=ALU.add), r=["lbs"], w=["oml"])

            def perm_in(t_, tw):
                return AP(t_, tw - SEQ, [[tw, 128], [1, 64], [64, 64]])

            def nat3(t_, tw, off):
                return AP(t_, off, [[tw, 128], [64, 64], [1, 64]])

            tctr = [0]

            def transposes(src, dstT, srck, dstk):
                for g8 in range(9):
                    n8 = min(8, NCHK - g8 * 8)
                    pt_ = pT[tctr[0] % 2]; ptk = ("pT", tctr[0] % 2)
                    tctr[0] += 1
                    for i8 in range(n8):
                        ck = g8 * 8 + i8
                        S.op("tensor", lambda e, pt_=pt_, i8=i8, ck=ck, src=src: e.transpose(out=pt_[0:64, i8 * 128:(i8 + 1) * 128], in_=src[:, ck * 64:(ck + 1) * 64], identity=identb[:]), r=[srck, "ident"], w=[ptk])
                    if g8 % 2 == 0:
                        S.op("scalar", lambda e, pt_=pt_, g8=g8, n8=n8, dstT=dstT: e.activation(out=dstT[:, g8 * 8:g8 * 8 + n8, :], in_=pt_[0:64, 0:n8 * 128], func=AF.Identity), r=[ptk], w=[dstk])
                    else:
                        S.op("vector", lambda e, pt_=pt_, g8=g8, n8=n8, dstT=dstT: e.tensor_copy(out=dstT[:, g8 * 8:g8 * 8 + n8, :], in_=pt_[0:64, 0:n8 * 128]), r=[ptk], w=[dstk])

            def vcopy(hd):
                r0 = hd * 128
                S.op("gpsimd", lambda e: e.tensor_copy(out=vcm[:, 0:256], in_=vb[:, 0:256]), r=["vb"], w=["vcm"])
                S.op("gpsimd", lambda e: e.tensor_copy(out=nat3(vcm, T, 256), in_=perm_in(vb, T)), r=["vb"], w=["vcm"])
                if hd + 1 < 4:
                    S.op("sync", lambda e, r0=r0: e.dma_start(out=vb[:], in_=G_d["v"].ap()[r0 + 128:r0 + 256, :]), w=["vb"], dma_key="vb")

            def vstage(hd):
                r0 = hd * 128
                S.op("sync", lambda e, r0=r0: e.dma_start(out=sgbb[:], in_=G_d["sgb"].ap()[r0:r0 + 128, :]), w=["sgbb"], dma_key="sgbb")
                if hd == 0:
                    vcopy(0)
                transposes(vcm, Vtok, "vcm", "Vtok")

            def setup_closures(k):
                hd, d = divmod(k, 2)
                r0 = hd * 128
                b = k % 2
                Qb = Qt[b]; Kb = Kt[b]; gb = gam[b]
                Qk = ("Qt", b); Kk = ("Kt", b); gk = ("gam", b)
                lbc = lb[:, d, hd:hd + 1]; omc = oml[:, d, hd:hd + 1]
                cl = []
                cl.append(lambda: S.op("vector", lambda e: e.tensor_scalar(out=W[0][:, 0:256], in0=SF[:, 0:256], scalar1=omc, scalar2=lbc, op0=ALU.mult, op1=ALU.add), r=["SF", "lbs", "oml"], w=["W0"]))
                cl.append(lambda: S.op("vector", lambda e: e.tensor_scalar(out=nat3(W[0], T, 256), in0=perm_in(SF, T), scalar1=omc, scalar2=lbc, op0=ALU.mult, op1=ALU.add), r=["SF", "lbs", "oml"], w=["W0"]))
                if d == 0:
                    cl.append(lambda: S.op("sync", lambda e: e.dma_start(out=SF[:], in_=G_d["sfb"].ap()[r0:r0 + 128, :]), w=["SF"], dma_key="SF"))
                elif hd + 1 < 4:
                    cl.append(lambda: S.op("sync", lambda e: e.dma_start(out=SF[:], in_=G_d["sff"].ap()[r0 + 128:r0 + 256, :]), w=["SF"], dma_key="SF"))
                cl.append(lambda: S.op("gpsimd", lambda e: e.tensor_scalar(out=W[1][:], in0=W[0][:], scalar1=-1.0, scalar2=1.0, op0=ALU.mult, op1=ALU.add), r=["W0"], w=["W1"]))
                cl.append(lambda: S.op("scalar", lambda e: e.activation(out=W[0][:], in_=W[0][:], func=AF.Ln), r=["W0", "W1"], w=["W0"]))
                if d == 0:
                    cl.append(lambda: S.op("vector", lambda e: e.tensor_tensor_scan(out=W[2][:], data0=Mx[:, 0:T], data1=W[0][:], initial=0.0, op0=ALU.mult, op1=ALU.add), r=["W0", "Mx"], w=["W2"]))
                else:
                    cl.append(lambda: S.op("vector", lambda e: e.tensor_tensor_scan(out=AP(W[2], T - 1, [[T, 128], [-1, T]]), data0=AP(Mx, T, [[T + 1, 128], [-1, T]]), data1=AP(W[0], T - 1, [[T, 128], [-1, T]]), initial=0.0, op0=ALU.mult, op1=ALU.add), r=["W0", "Mx"], w=["W2"]))
                cl.append(lambda: S.op("gpsimd", lambda e: e.tensor_scalar(out=W[2][:], in0=W[2][:], scalar1=0.0, scalar2=-80.0, op0=ALU.min, op1=ALU.max), r=["W2"], w=["W2"]))
                cl.append(lambda: S.op("scalar", lambda e: e.activation(out=W[0][:], in_=W[2][:], func=AF.Exp), r=["W2"], w=["W0"]))
                cl.append(lambda: S.op("scalar", lambda e: e.activation(out=W[2][:], in_=W[2][:], func=AF.Exp, scale=-1.0), r=["W2", "W0"], w=["W2"]))
                ge = 63 if d == 0 else 0
                cl.append(lambda: S.op("gpsimd", lambda e: e.tensor_copy(out=gb[:], in_=AP(W[0], ge, [[T, 128], [64, NCHK]])), r=["W0"], w=[gk]))
                cl.append(lambda: S.op("gpsimd", lambda e: e.tensor_tensor(out=nat3(Qb, SEQ, 0), in0=perm_in(qb, SEQ), in1=nat3(W[0], T, 256), op=ALU.mult), r=["qb", "W0"], w=[Qk]))
                if d == 1 and hd + 1 < 4:
                    cl.append(lambda: S.op("sync", lambda e: e.dma_start(out=qb[:], in_=G_d["q"].ap()[r0 + 128:r0 + 256, :]), w=["qb"], dma_key="qb"))
                cl.append(lambda: S.op("gpsimd", lambda e: e.tensor_tensor(out=Kb[:], in0=W[1][:], in1=W[2][:], op=ALU.mult), r=["W1", "W2"], w=[Kk]))
                return cl

            def pestage(k):
                hd, d = divmod(k, 2)
                b = k % 2
                Qb = Qt[b]; Kb = Kt[b]; Qk = ("Qt", b); Kk = ("Kt", b)
                transposes(Kb, Ktok, Kk, "Ktok")
                msk = mF if d == 0 else mB
                mk = "mF" if d == 0 else "mB"
                for g8 in range(8):
                    pa_ = pA[g8 % 2]; pak = ("pA", g8 % 2)
                    for i8 in range(8):
                        lc = g8 * 8 + i8
                        S.op("tensor", lambda e, pa_=pa_, i8=i8, lc=lc: e.matmul(pa_[0:64, i8 * 64:(i8 + 1) * 64], lhsT=Kb[:, (4 + lc) * 64:(5 + lc) * 64], rhs=Qb[:, lc * 64:(lc + 1) * 64], start=True, stop=True), r=[Kk, Qk], w=[pak])
                    S.op("vector", lambda e, pa_=pa_, g8=g8, msk=msk: e.tensor_tensor(out=AP(AT, g8 * 512, [[SEQ, 64], [64, 8], [1, 64]]), in0=AP(pa_, 0, [[512, 64], [64, 8], [1, 64]]), in1=AP(msk, 0, [[64, 64], [0, 8], [1, 64]]), op=ALU.mult), r=[pak, mk], w=["AT"])

            def recurrence(k, extra):
                hd, d = divmod(k, 2)
                b = k % 2
                Qb = Qt[b]; Qk = ("Qt", b); gb = gam[b]; gk = ("gam", b)
                order = [0, 1, 2, 3] + list(range(4, NCHK)) if d == 0 else [3, 2, 1, 0] + list(range(NCHK - 1, 3, -1))
                LA = 3
                nst = len(order)
                ring = [(pS[0], ("pS", 0)), (pS[1], ("pS", 1)), (pA[0], ("pA", 0)), (pA[1], ("pA", 1))]
                stride = max(1, (nst - 8) // max(1, len(extra))) if extra else nst
                ei = 0

                def emit_ps(i):
                    ck_ = order[i]
                    ps_, psk_ = ring[i % 4]
                    S.op("tensor", lambda e, ps_=ps_, ck_=ck_: e.matmul(ps_[:, 0:128], lhsT=Ktok[:, ck_, :], rhs=Vtok[:, ck_, :], start=True, stop=True), r=["Ktok", "Vtok"], w=[psk_])
                for i in range(min(LA, nst)):
                    emit_ps(i)
                prev = None
                for i, ck in enumerate(order):
                    tn = Tst[i % 4]; tnk = ("Tst", i % 4)
                    to = Tst[(i - 1) % 4]; tok_ = ("Tst", (i - 1) % 4)
                    sb_n = Sbf[i % 4]; sbk = ("Sbf", i % 4)
                    sb_o = Sbf[(i - 1) % 4]; sbok = ("Sbf", (i - 1) % 4)
                    if ck >= 4:
                        lc = ck - 4
                        slot = lc % 8
                        po_ = pO[(lc // 8) % 2]; pok = ("pO", (lc // 8) % 2)
                        S.op("tensor", lambda e, po_=po_, slot=slot, ck=ck, lc=lc: e.matmul(po_[:, slot * 64:(slot + 1) * 64], lhsT=Vtok[:, ck, :], rhs=AT[:, lc * 64:(lc + 1) * 64], start=True, stop=False), r=["Vtok", "AT"], w=[pok])
                        S.op("tensor", lambda e, po_=po_, slot=slot, lc=lc, sb_o=sb_o: e.matmul(po_[:, slot * 64:(slot + 1) * 64], lhsT=sb_o[:], rhs=Qb[:, lc * 64:(lc + 1) * 64], start=False, stop=True), r=[sbok, Qk], w=[pok])
                    if i + LA < nst:
                        emit_ps(i + LA)
                    ps_, psk = ring[i % 4]
                    if i == 0:
                        S.op("vector", lambda e, tn=tn, ps_=ps_: e.tensor_copy(out=tn[:], in_=ps_[:, 0:128]), r=[psk], w=[tnk])
                    else:
                        S.op("vector", lambda e, tn=tn, to=to, ps_=ps_, prev=prev: e.scalar_tensor_tensor(out=tn[:], in0=to[:], scalar=gb[:, prev:prev + 1], in1=ps_[:, 0:128], op0=ALU.mult, op1=ALU.add), r=[tok_, psk, gk], w=[tnk])
                    if i + 1 < nst:
                        S.op("scalar", lambda e, sb_n=sb_n, tn=tn, ck=ck: e.activation(out=sb_n[:], in_=tn[:], func=AF.Identity, scale=gb[:, ck:ck + 1]), r=[tnk, gk], w=[sbk])
                    prev = ck
                    if ck >= 4:
                        lc = ck - 4
                        done = (lc % 8 == 7) if d == 0 else (lc % 8 == 0)
                        if done:
                            g8 = lc // 8
                            if d == 0:
                                S.op("scalar", lambda e, po_=po_, g8=g8: e.activation(out=O[:, g8 * 512:(g8 + 1) * 512], in_=po_[:], func=AF.Identity), r=[pok], w=[("O", g8)])
                            else:
                                S.op("vector", lambda e, po_=po_, g8=g8: e.tensor_tensor(out=O[:, g8 * 512:(g8 + 1) * 512], in0=po_[:], in1=O[:, g8 * 512:(g8 + 1) * 512], op=ALU.add), r=[pok, ("O", g8)], w=[("O", g8)])
                    if extra and ei < len(extra) and i >= 4 and (i - 4) % stride == 0:
                        extra[ei]()
                        ei += 1
                while extra and ei < len(extra):
                    extra[ei]()
                    ei += 1

            def final(hd):
                Ok = [("O", g8) for g8 in range(8)]
                sqb = Kt[1]; sqk = ("Kt", 1)
                yb_ = Qt[1]; ybk_ = ("Qt", 1)
                S.op("scalar", lambda e: e.activation(out=sqb[:, 0:SEQ], in_=O[:], func=AF.Square), r=Ok + [sqk], w=[sqk])
                for g8 in range(8):
                    pa_ = pA[g8 % 2]; pak = ("pA", g8 % 2)
                    S.op("tensor", lambda e, pa_=pa_, g8=g8: e.matmul(pa_[:], lhsT=ones_bf[:], rhs=sqb[:, g8 * 512:(g8 + 1) * 512], start=True, stop=True), r=[sqk, "ones"], w=[pak])
                    S.op("scalar", lambda e, pa_=pa_, g8=g8: e.activation(out=W[0][:, g8 * 512:(g8 + 1) * 512], in_=pa_[:], func=AF.Ln, scale=1.0 / 128, bias=epsT[:, 0:1]), r=[pak, "eps"], w=["W0"])
                S.op("scalar", lambda e: e.activation(out=W[0][:, 0:SEQ], in_=W[0][:, 0:SEQ], func=AF.Exp, scale=-0.5), r=["W0"], w=["W0"])
                S.op("vector", lambda e: e.tensor_tensor(out=O[:], in0=O[:], in1=W[0][:, 0:SEQ], op=ALU.mult), r=Ok + ["W0"], w=Ok)
                S.op("vector", lambda e: e.scalar_tensor_tensor(out=nat3(yb_, SEQ, 0), in0=AP(O, 0, [[SEQ, 128], [1, 64], [64, 64]]), scalar=hnw[:, 0:1], in1=nat3(sgbb, SEQ, 0), op0=ALU.mult, op1=ALU.mult), r=Ok + ["hnw", "sgbb", ybk_], w=[ybk_])
                S.op("sync", lambda e, hd=hd: e.dma_start(out=y_u[4 + hd].ap(), in_=yb_[:]), r=[ybk_], w=[("y_u", 4 + hd)], dma_key="yBst")
                if not debug:
                    S.op("gpsimd", lambda e, hd=hd: e.collective_compute("AllGather", ALU.bypass, replica_groups=RG, ins=[y_u[4 + hd].ap().opt()], outs=[yg_u[4 + hd].ap().opt()]), r=[("y_u", 4 + hd)], w=[("yg_u", 4 + hd)], dma_key=("ag", 4 + hd), inc=1)

            S.op("sync", lambda e: e.dma_start(out=vb[:], in_=G_d["v"].ap()[0:128, :]), w=["vb"], dma_key="vb")
            S.op("sync", lambda e: e.dma_start(out=qb[:], in_=G_d["q"].ap()[0:128, :]), w=["qb"], dma_key="qb")
            S.op("sync", lambda e: e.dma_start(out=SF[:], in_=G_d["sff"].ap()[0:128, :]), w=["SF"], dma_key="SF")
            for c_ in setup_closures(0):
                c_()
            for k in range(8):
                hd, d = divmod(k, 2)
                if d == 0:
                    vstage(hd)
                pestage(k)
                extra = setup_closures(k + 1) if k + 1 < 8 else []
                if d == 0 and hd + 1 < 4:
                    extra = extra + [lambda h_=hd + 1: vcopy(h_)]
                recurrence(k, extra)
                if d == 1:
                    final(hd)
            S.emit(nofinal=[("ag", 7)])
            for n_ in range(4, 8):
                if ("ag", n_) in S.dma_sems:
                    agsig[n_] = tuple(S.dma_sems[("ag", n_)])
            print("phase c waits", S.nwaits, "ops", len(S.ops))
        if stop_after <= 3:
            return nc

        with ExitStack() as ph:
            S = Sched(nc, top, "d")
            wo_sb = sbt(ph, "wo_sb", [128, 32, 512], BF16)
            Yb = [sbt(ph, "Yb%d" % i, [128, 32, 512], BF16) for i in range(2)]
            Z = sbt(ph, "Z", [128, 32, 512])
            gate_b = sbt(ph, "gate_b", [128, 512]); fnw_b = sbt(ph, "fnw_b", [128, 512])
            tmpb = [sbt(ph, "tmpb%d" % i, [128, 512]) for i in range(2)]
            junk = sbt(ph, "junk", [128, 512])
            ss3 = sbt(ph, "ss3", [128, 32]); ssg = sbt(ph, "ssg", [128, 4, 32]); rs3 = sbt(ph, "rs3", [128, 32])
            po3 = [pst(ph, "po3_%d" % i, [128, 512]) for i in range(4)]
            import os
            wstg = [sbt(ph, "wstg%d" % i, [128, 4, 512]) for i in range(2)]

            def wload():
                for c4 in range(8):
                    ws = wstg[c4 % 2]; wk = ("wstg", c4 % 2)
                    S.op("sync", lambda e, ws=ws, c4=c4: e.dma_start(out=ws[:], in_=AP(wout_d, c4 * 4 * 128 * 512, [[512, 128], [128 * 512, 4], [1, 512]])), w=[wk], dma_key=wk)
                    S.op("scalar", lambda e, ws=ws, c4=c4: e.activation(out=wo_sb[:, c4 * 4:(c4 + 1) * 4, :], in_=ws[:], func=AF.Identity), r=[wk], w=[("wo", c4 // 2)])
            if not os.environ.get("K_NOGATE"):
                S.op("sync", lambda e: e.dma_start(out=gate_b[:], in_=AP(ada_g, 0, [[0, 128], [1, 512]])), w=["gate"], dma_key="par", bulk=True)
            if not os.environ.get("K_NOFNW"):
                S.op("sync", lambda e: e.dma_start(out=fnw_b[:], in_=AP(fnw_d, 0, [[0, 128], [1, 512]])), w=["fnw"], dma_key="par", bulk=True)
            import os
            NBLK = int(os.environ.get('K_D_NBLK', 8))
            for blk in range(NBLK):
                yb_ = Yb[blk % 2]; ybk = ("Yb", blk % 2)
                S.op("sync", lambda e, blk=blk: e.dma_start(out=Z[:, blk * 4:(blk + 1) * 4, :], in_=AP(xq_d, blk * 512 * 512, [[512, 128], [128 * 512, 4], [1, 512]])), w=[("Z", blk * 4 + t_) for t_ in range(4)], dma_key=("zl", blk))
                if blk == 0:
                    wload()
                for i in range(8):
                    S.op("sync", lambda e, yb_=yb_, i=i, blk=blk: e.dma_start(out=yb_[:, i * 4:(i + 1) * 4, :], in_=AP(yg_u[i], blk * 512, [[SEQ, 128], [128 * SEQ, 4], [1, 512]])), w=[(ybk, i)], dma_key=ybk, grp=blk, ext=([agsig[i]] if i in agsig else []))
                for tl in range(4):
                    ti = blk * 4 + tl
                    zk = ("Z", ti)
                    p_ = po3[ti % 4]; pk = ("po3", ti % 4)
                    for kc in range(32):
                        S.op("tensor", lambda e, p_=p_, kc=kc, tl=tl, yb_=yb_: e.matmul(p_[:], lhsT=yb_[:, kc, tl * 128:(tl + 1) * 128], rhs=wo_sb[:, kc, :], start=(kc == 0), stop=(kc == 31)),
                             r=[(ybk, kc // 4), ("wo", kc // 8)], w=[pk])
                    tb = tmpb[ti % 2]; tbk = ("tmpb", ti % 2)
                    S.op("vector", lambda e, p_=p_, tb=tb: e.tensor_tensor(out=tb[:], in0=p_[:], in1=gate_b[:], op=ALU.mult), r=[pk, "gate"], w=[tbk])
                    S.op("gpsimd", lambda e, tb=tb, ti=ti: e.tensor_tensor(out=Z[:, ti, :], in0=Z[:, ti, :], in1=tb[:], op=ALU.add), r=[zk, tbk], w=[zk])
                    S.op("scalar", lambda e, ti=ti: e.activation(out=junk[:], in_=Z[:, ti, :], func=AF.Square, accum_out=ss3[:, ti:ti + 1]), r=[zk], w=["junk", ("ss3", ti)])
            if os.environ.get('K_D_NOFIN'):
                S.emit()
                return nc
            ssk = [("ss3", ti) for ti in range(32)]
            S.op("sync", lambda e: e.dma_start(out=ss_loc.ap(), in_=ss3[:]), r=ssk, w=["ss_loc"], dma_key="ssst")
            import os
            if os.environ.get("K_SKIP_SSAG"):
                S.op("sync", lambda e: e.dma_start(out=ss_all.ap()[0:128, :], in_=ss_loc.ap()), r=["ss_loc"], w=["ss_all"], dma_key="agss")
            else:
                S.op("gpsimd", lambda e: e.collective_compute("AllGather", ALU.bypass, replica_groups=RG, ins=[ss_loc.ap().opt()], outs=[ss_all.ap().opt()]), r=["ss_loc"], w=["ss_all"], dma_key="agss", inc=1)
            S.op("sync", lambda e: e.dma_start(out=ssg[:], in_=AP(ss_all, 0, [[32, 128], [128 * 32, 4], [1, 32]])), r=["ss_all"], w=["ssg"], dma_key="ssld")
            S.op("vector", lambda e: e.tensor_tensor(out=rs3[:], in0=ssg[:, 0, :], in1=ssg[:, 1, :], op=ALU.add), r=["ssg"], w=["rs3"])
            S.op("vector", lambda e: e.tensor_tensor(out=rs3[:], in0=rs3[:], in1=ssg[:, 2, :], op=ALU.add), r=["ssg", "rs3"], w=["rs3"])
            S.op("vector", lambda e: e.tensor_tensor(out=rs3[:], in0=rs3[:], in1=ssg[:, 3, :], op=ALU.add), r=["ssg", "rs3"], w=["rs3"])
            S.op("vector", lambda e: e.tensor_scalar(out=rs3[:], in0=rs3[:], scalar1=1.0 / D, scalar2=EPS, op0=ALU.mult, op1=ALU.add), r=["rs3"], w=["rs3"])
            S.op("scalar", lambda e: e.activation(out=rs3[:], in_=rs3[:], func=AF.Ln), r=["rs3"], w=["rs3"])
            S.op("scalar", lambda e: e.activation(out=rs3[:], in_=rs3[:], func=AF.Exp, scale=-0.5), r=["rs3"], w=["rs3"])
            for ti in range(32):
                zk = ("Z", ti)
                eng = "vector" if ti % 2 == 0 else "vector"
                S.op(eng, lambda e, ti=ti: e.scalar_tensor_tensor(out=Z[:, ti, :], in0=Z[:, ti, :], scalar=rs3[:, ti:ti + 1], in1=fnw_b[:], op0=ALU.mult, op1=ALU.mult), r=[zk, "rs3", "fnw"], w=[zk])
                S.op("sync", lambda e, ti=ti: e.dma_start(out=out_d.ap()[ti * 128:(ti + 1) * 128, :], in_=Z[:, ti, :]), r=[zk], w=[("outd", ti)], dma_key=("ost", ti % 4))
            S.emit()
            print("phase d waits", S.nwaits, "ops", len(S.ops))
    return nc


def shard_inputs(inputs):
    f = lambda a: np.ascontiguousarray(a, dtype=np.float32)
    x = inputs["x"]; ctx = inputs["ctx"]; c = inputs["c"]; c_ctx = inputs["c_ctx"]
    w_in = inputs["w_in"][0]; w_out = inputs["w_out"][0]
    maps = []
    for b in range(2):
        for j in range(4):
            cols = np.concatenate([np.arange(g * D + j * 512, g * D + (j + 1) * 512) for g in range(7)])
            rows = np.concatenate([(0 if i < 4 else D) + r * 512 + (i % 4) * 128 + np.arange(128) for i in range(8) for r in range(4)])
            fq = slice(j * 512, (j + 1) * 512)
            sl = slice(j * 512, (j + 1) * 512)
            m = {
                "x": f(x[b]), "ctx": f(ctx[b]), "cvec": f(np.stack([c[b], c_ctx])),
                "ada_w": f(inputs["ada_w"][0]), "ada_b": f(inputs["ada_b"][0][None]), "norm_w": f(inputs["norm_w"][0][None]),
                "w_in": f(w_in[:, cols]),
                "conv_w": f(inputs["conv_w"][0][:, sl]), "conv_b": f(inputs["conv_b"][0][None, sl]),
                "lru_wr": f(inputs["lru_wr"][0][:, j * 4:(j + 1) * 4]), "lru_wi": f(inputs["lru_wi"][0][:, j * 4:(j + 1) * 4]),
                "lru_br": f(inputs["lru_br"][0][:, sl]), "lru_bi": f(inputs["lru_bi"][0][:, sl]), "lru_lam": f(inputs["lru_lambda"][0][:, sl]),
                "lb_logits": f(inputs["hgrn_lb_logits"][:, :, sl]), "hnorm_w": f(inputs["hgrn_norm_w"][0][None]),
                "w_out": f(w_out[rows][:, fq]), "fnorm_w": f(inputs["final_norm_w"][None, fq]),
                "xq": f(x[b][:, fq]), "ada_wg": f(inputs["ada_w"][0][:, 2 * D + j * 512:2 * D + (j + 1) * 512]),
                "ada_bg": f(inputs["ada_b"][0][None, 2 * D + j * 512:2 * D + (j + 1) * 512]),
            }
            maps.append(m)
    return maps


def kernel(**inputs):
    nc = build_program()
    maps = shard_inputs(inputs)
    res = run_bass_kernel_spmd(nc, maps, core_ids=list(range(8)))
    out = np.zeros((2, SEQ, D), np.float32)
    for b in range(2):
        for j in range(4):
            out[b, :, j * 512:(j + 1) * 512] = res.results[b * 4 + j]["out"]
    return out
```
